# Optimizing a Trainium2 kernel written in Bass

```python
import math
import jax, jax.numpy as jnp
from jax import lax
import numpy as np

D_MODEL = 1024
BATCH = 16
SEQ = 2048
DEPTH = 1

GRID_W = 64
CTX_LEN = 256
N_HEADS = 16
N_KV_HEADS = 4
HEAD_DIM = 64
KV_REP = N_HEADS // N_KV_HEADS
ATTN_W = N_HEADS * HEAD_DIM
KV_W = N_KV_HEADS * HEAD_DIM
Q_BLOCK = 128
ROPE_THETA = 10000.0
ROPE_AXIS_DIM = HEAD_DIM // 2
D_SSM = D_MODEL // 2
SSM_GROUP = 16
N_SSM_GROUPS = D_SSM // SSM_GROUP
SSM_STATE = 64
DT_MIN = 1e-3
DT_MAX = 1e-1
D_FF = 4 * D_MODEL
NORM_EPS = 1e-6
OFF_Q = 0
OFF_K = OFF_Q + ATTN_W
OFF_V = OFF_K + KV_W
OFF_U = OFF_V + KV_W
OFF_GA = OFF_U + D_SSM
OFF_GS = OFF_GA + D_MODEL
W_IN_COLS = OFF_GS + D_MODEL

kernel_name = 'hybrid_s5_gqa_gated_dit_block'


def rms_norm(x, g):
    xf = x.astype(jnp.float32)
    xf = xf * lax.rsqrt(jnp.mean(xf * xf, axis=-1, keepdims=True) + NORM_EPS)
    return xf.astype(x.dtype) * g


def modulate(h, shift, scale):
    return h * (1 + scale) + shift


def axial_rope_tables(n_tok):
    rows = n_tok // GRID_W
    row = jnp.repeat(jnp.arange(rows, dtype=jnp.float32), GRID_W)
    col = jnp.tile(jnp.arange(GRID_W, dtype=jnp.float32), rows)
    half = ROPE_AXIS_DIM // 2
    inv_freq = ROPE_THETA ** (-jnp.arange(half, dtype=jnp.float32) / half)
    ang = jnp.concatenate([row[:, None] * inv_freq, col[:, None] * inv_freq], axis=-1)
    return jnp.cos(ang), jnp.sin(ang)


def apply_axial_rope(x, cos, sin):
    xf = x.astype(jnp.float32)
    half = ROPE_AXIS_DIM // 2
    parts = []
    for axis_idx in range(2):
        xa = xf[..., axis_idx * ROPE_AXIS_DIM:(axis_idx + 1) * ROPE_AXIS_DIM]
        cs = cos[None, :, None, axis_idx * half:(axis_idx + 1) * half]
        sn = sin[None, :, None, axis_idx * half:(axis_idx + 1) * half]
        x1, x2 = xa[..., :half], xa[..., half:]
        parts += [x1 * cs - x2 * sn, x2 * cs + x1 * sn]
    return jnp.concatenate(parts, axis=-1).astype(x.dtype)


def attend(q, k, v):
    bsz, lq = q.shape[:2]
    nblk = lq // Q_BLOCK
    qb = q.reshape(bsz, nblk, Q_BLOCK, N_KV_HEADS, KV_REP, HEAD_DIM).transpose(1, 0, 2, 3, 4, 5)
    scale = HEAD_DIM ** -0.5

    def one_block(qblk):
        s = jnp.einsum('bqgrd,bkgd->bgrqk', qblk, k).astype(jnp.float32) * scale
        p = jax.nn.softmax(s, axis=-1).astype(v.dtype)
        return jnp.einsum('bgrqk,bkgd->bqgrd', p, v)

    o = lax.map(one_block, qb)
    return o.transpose(1, 0, 2, 3, 4, 5).reshape(bsz, lq, ATTN_W)


def ssm_discretize(lam_re, lam_im, log_dt, b_re, b_im):
    lam = lax.complex(lam_re.astype(jnp.float32), lam_im.astype(jnp.float32))
    dt = jnp.exp(log_dt.astype(jnp.float32))[:, None]
    a_bar = jnp.exp(lam * dt)
    b = lax.complex(b_re.astype(jnp.float32), b_im.astype(jnp.float32))
    b_bar = ((a_bar - 1) / lam)[..., None] * b
    return a_bar, b_bar


def _linear_recurrence_op(e1, e2):
    a1, b1 = e1
    a2, b2 = e2
    return a1 * a2, a2 * b1 + b2


def ssm_scan(u, a_bar, b_bar, h0, reverse):
    bsz, n = u.shape[:2]
    ug = u.astype(jnp.float32).reshape(bsz, n, N_SSM_GROUPS, SSM_GROUP).astype(jnp.complex64)
    bu = jnp.einsum('blgh,gph->blgp', ug, b_bar)
    edge = n - 1 if reverse else 0
    bu = bu.at[:, edge].add(a_bar * h0)
    a = jnp.broadcast_to(a_bar, bu.shape)
    _, states = lax.associative_scan(_linear_recurrence_op, (a, bu), reverse=reverse, axis=1)
    return states


def ssm_readout(states, c_re, c_im):
    bsz, n = states.shape[:2]
    cmat = lax.complex(c_re.astype(jnp.float32), c_im.astype(jnp.float32))
    y = jnp.einsum('blgp,ghp->blgh', states, cmat).real
    return y.reshape(bsz, n, D_SSM)


def merge_branches(attn_o, y_ssm, gate_a, gate_s, w_glu, b_glu, w_br_attn, w_br_ssm, w_out):
    y = jax.nn.gelu(y_ssm)
    y = y * jax.nn.sigmoid(y @ w_glu + b_glu)
    merged = jax.nn.sigmoid(gate_a) * (attn_o @ w_br_attn) + jax.nn.sigmoid(gate_s) * (y @ w_br_ssm)
    return merged @ w_out


def sq_relu_mlp(h, w1, b1, w2, b2):
    return jnp.square(jax.nn.relu(h @ w1 + b1)) @ w2 + b2


def _cols(p, base, start, stop):
    return p[..., start - base:stop - base]


def setup_inputs(seed: int = 0) -> dict:
    key = jax.random.key(seed)
    ks = jax.random.split(key, 32)
    nrm = lambda k, shape, s: jax.random.normal(k, shape, jnp.float32) * s
    G, P, H = N_SSM_GROUPS, SSM_STATE, SSM_GROUP
    lam_im_base = jnp.pi * jnp.arange(P, dtype=jnp.float32)
    return {
        'x': nrm(ks[0], (BATCH, SEQ, D_MODEL), 1.0),
        'c': nrm(ks[1], (BATCH, D_MODEL), 1.0),
        'ctx': nrm(ks[2], (BATCH, CTX_LEN, D_MODEL), 1.0),
        'c_ctx': nrm(ks[3], (D_MODEL,), 1.0),
        'w_mod': nrm(ks[4], (DEPTH, D_MODEL, 6 * D_MODEL), 0.5 * D_MODEL ** -0.5),
        'b_mod': nrm(ks[5], (DEPTH, 6 * D_MODEL), 0.01),
        'norm1_g': 1.0 + nrm(ks[6], (DEPTH, D_MODEL), 0.02),
        'norm2_g': 1.0 + nrm(ks[7], (DEPTH, D_MODEL), 0.02),
        'w_in': nrm(ks[8], (DEPTH, D_MODEL, W_IN_COLS), D_MODEL ** -0.5),
        'q_norm_g': 1.0 + nrm(ks[9], (DEPTH, HEAD_DIM), 0.02),
        'k_norm_g': 1.0 + nrm(ks[10], (DEPTH, HEAD_DIM), 0.02),
        'ssm_lambda_re': -0.5 + nrm(ks[11], (DEPTH, 2, G, P), 0.01),
        'ssm_lambda_im': lam_im_base + nrm(ks[12], (DEPTH, 2, G, P), 0.01),
        'ssm_log_dt': jax.random.uniform(ks[13], (DEPTH, 2, G), jnp.float32, math.log(DT_MIN), math.log(DT_MAX)),
        'ssm_b_re': nrm(ks[14], (DEPTH, 2, G, P, H), (2 * H) ** -0.5),
        'ssm_b_im': nrm(ks[15], (DEPTH, 2, G, P, H), (2 * H) ** -0.5),
        'ssm_c_re': nrm(ks[16], (DEPTH, 2, G, H, P), (2 * P) ** -0.5),
        'ssm_c_im': nrm(ks[17], (DEPTH, 2, G, H, P), (2 * P) ** -0.5),
        'ssm_d': nrm(ks[18], (DEPTH, D_SSM), 1.0),
        'w_glu': nrm(ks[19], (DEPTH, D_SSM, D_SSM), D_SSM ** -0.5),
        'b_glu': nrm(ks[20], (DEPTH, D_SSM), 0.01),
        'w_br_attn': nrm(ks[21], (DEPTH, ATTN_W, D_MODEL), ATTN_W ** -0.5),
        'w_br_ssm': nrm(ks[22], (DEPTH, D_SSM, D_MODEL), D_SSM ** -0.5),
        'w_out': nrm(ks[23], (DEPTH, D_MODEL, D_MODEL), D_MODEL ** -0.5),
        'w_mlp1': nrm(ks[24], (DEPTH, D_MODEL, D_FF), D_MODEL ** -0.5),
        'b_mlp1': nrm(ks[25], (DEPTH, D_FF), 0.01),
        'w_mlp2': nrm(ks[26], (DEPTH, D_FF, D_MODEL), D_FF ** -0.5),
        'b_mlp2': nrm(ks[27], (DEPTH, D_MODEL), 0.01),
    }


def reference(x, c, ctx, c_ctx, w_mod, b_mod, norm1_g, norm2_g, w_in, q_norm_g, k_norm_g,
              ssm_lambda_re, ssm_lambda_im, ssm_log_dt, ssm_b_re, ssm_b_im, ssm_c_re, ssm_c_im,
              ssm_d, w_glu, b_glu, w_br_attn, w_br_ssm, w_out, w_mlp1, b_mlp1, w_mlp2, b_mlp2):
    bsz, n_tok, _ = x.shape
    n_ctx = ctx.shape[1]
    cos, sin = axial_rope_tables(n_tok)
    h0_zero = jnp.zeros((bsz, N_SSM_GROUPS, SSM_STATE), jnp.complex64)
    ctx_s = ctx
    for l in range(DEPTH):
        last = l == DEPTH - 1
        sh1_l, sc1_l, g1_l, sh2_l, sc2_l, g2_l = jnp.split(
            (jax.nn.silu(c) @ w_mod[l] + b_mod[l])[:, None, :], 6, axis=-1)
        sh1_c, sc1_c, g1_c, sh2_c, sc2_c, g2_c = jnp.split(
            jax.nn.silu(c_ctx) @ w_mod[l] + b_mod[l], 6, axis=-1)

        h_l = modulate(rms_norm(x, norm1_g[l]), sh1_l, sc1_l)
        h_c = modulate(rms_norm(ctx_s, norm1_g[l]), sh1_c, sc1_c)
        p_l = h_l @ w_in[l]
        c0, c1 = (OFF_K, OFF_GA) if last else (0, W_IN_COLS)
        p_c = h_c @ w_in[l][:, c0:c1]

        q_l = apply_axial_rope(rms_norm(p_l[..., OFF_Q:OFF_K].reshape(bsz, n_tok, N_HEADS, HEAD_DIM), q_norm_g[l]), cos, sin)
        k_l = apply_axial_rope(rms_norm(p_l[..., OFF_K:OFF_V].reshape(bsz, n_tok, N_KV_HEADS, HEAD_DIM), k_norm_g[l]), cos, sin)
        v_l = p_l[..., OFF_V:OFF_U].reshape(bsz, n_tok, N_KV_HEADS, HEAD_DIM)
        u_l = p_l[..., OFF_U:OFF_GA]
        k_c = rms_norm(_cols(p_c, c0, OFF_K, OFF_V).reshape(bsz, n_ctx, N_KV_HEADS, HEAD_DIM), k_norm_g[l])
        v_c = _cols(p_c, c0, OFF_V, OFF_U).reshape(bsz, n_ctx, N_KV_HEADS, HEAD_DIM)
        u_c = _cols(p_c, c0, OFF_U, OFF_GA)

        attn_l = attend(q_l, jnp.concatenate([k_l, k_c], axis=1), jnp.concatenate([v_l, v_c], axis=1))

        a_f, b_f = ssm_discretize(ssm_lambda_re[l, 0], ssm_lambda_im[l, 0], ssm_log_dt[l, 0], ssm_b_re[l, 0], ssm_b_im[l, 0])
        a_b, b_b = ssm_discretize(ssm_lambda_re[l, 1], ssm_lambda_im[l, 1], ssm_log_dt[l, 1], ssm_b_re[l, 1], ssm_b_im[l, 1])
        s_cf = ssm_scan(u_c, a_f, b_f, h0_zero, reverse=False)
        s_cb = ssm_scan(u_c, a_b, b_b, h0_zero, reverse=True)
        y_l = (ssm_readout(ssm_scan(u_l, a_f, b_f, s_cf[:, -1], reverse=False), ssm_c_re[l, 0], ssm_c_im[l, 0])
               + ssm_readout(ssm_scan(u_l, a_b, b_b, s_cb[:, 0], reverse=True), ssm_c_re[l, 1], ssm_c_im[l, 1]))
        y_l = y_l.astype(x.dtype) + ssm_d[l] * u_l
        mix_l = merge_branches(attn_l, y_l, p_l[..., OFF_GA:OFF_GS], p_l[..., OFF_GS:W_IN_COLS],
                               w_glu[l], b_glu[l], w_br_attn[l], w_br_ssm[l], w_out[l])

        if not last:
            q_c = rms_norm(p_c[..., OFF_Q:OFF_K].reshape(bsz, n_ctx, N_HEADS, HEAD_DIM), q_norm_g[l])
            attn_c = attend(q_c, k_c, v_c)
            y_c = ssm_readout(s_cf, ssm_c_re[l, 0], ssm_c_im[l, 0]) + ssm_readout(s_cb, ssm_c_re[l, 1], ssm_c_im[l, 1])
            y_c = y_c.astype(ctx_s.dtype) + ssm_d[l] * u_c
            mix_c = merge_branches(attn_c, y_c, p_c[..., OFF_GA:OFF_GS], p_c[..., OFF_GS:W_IN_COLS],
                                   w_glu[l], b_glu[l], w_br_attn[l], w_br_ssm[l], w_out[l])
            ctx_s = ctx_s + g1_c * mix_c
            h2_c = modulate(rms_norm(ctx_s, norm2_g[l]), sh2_c, sc2_c)
            ctx_s = ctx_s + g2_c * sq_relu_mlp(h2_c, w_mlp1[l], b_mlp1[l], w_mlp2[l], b_mlp2[l])

        x = x + g1_l * mix_l
        h2_l = modulate(rms_norm(x, norm2_g[l]), sh2_l, sc2_l)
        x = x + g2_l * sq_relu_mlp(h2_l, w_mlp1[l], b_mlp1[l], w_mlp2[l], b_mlp2[l])
    return x
```

```python
import math
import threading
from contextlib import ExitStack

import numpy as np
import concourse.bass as bass
import concourse.mybir as mybir
from concourse.bass_utils import run_bass_kernel_spmd

F32 = mybir.dt.float32
BF16 = mybir.dt.bfloat16
I32 = mybir.dt.int32
AF = mybir.ActivationFunctionType
ALU = mybir.AluOpType
AX = mybir.AxisListType

D = 1024
SEQ = 2048
CTXL = 256
NB = 2
NCORES = 8
EPS = 1e-6
NL = NB * SEQ
NCX = NB * CTXL
KEYS = SEQ + CTXL


class Buf:
    __slots__ = ("name", "writes", "reads", "wbar")

    def __init__(self, name=""):
        self.name = name
        self.writes = {}
        self.reads = {}
        self.wbar = {}


class Eng:
    def __init__(self, K, name, h, nring):
        self.K = K
        self.name = name
        self.h = h
        self.sem = K.new_sem("s_" + name)
        self.count = 0
        self.seen = {}
        self.ring = [[K.new_sem("d_%s%d" % (name, i)), 0] for i in range(nring)]
        self.rpos = 0

    def wait(self, ev):
        sem, val = ev
        k = id(sem)
        if self.seen.get(k, 0) >= val:
            return
        self.h.wait_ge(sem, val)
        self.seen[k] = val


class Interleaver:
    def __init__(self, fn):
        self.go = threading.Semaphore(0)
        self.back = threading.Semaphore(0)
        self.finished = False
        self.started = False
        self.err = None
        self.t = threading.Thread(target=self._run, args=(fn,), daemon=True)

    def _run(self, fn):
        self.go.acquire()
        try:
            fn()
        except BaseException as e:
            self.err = e
        self.finished = True
        self.back.release()

    def pause(self):
        self.back.release()
        self.go.acquire()

    def step(self, n=1):
        for _ in range(n):
            if self.finished:
                break
            if not self.started:
                self.started = True
                self.t.start()
            self.go.release()
            self.back.acquire()
        if self.err is not None:
            raise self.err

    def finish(self):
        while not self.finished:
            self.step(1)
        if self.err is not None:
            raise self.err


class Kern:
    def __init__(self, nc):
        self.nc = nc
        self.co = None
        self.es = ExitStack()
        self.nsem = 0
        self.pe = Eng(self, "pe", nc.tensor, 0)
        self.act = Eng(self, "act", nc.scalar, 6)
        self.dve = Eng(self, "dve", nc.vector, 0)
        self.pool = Eng(self, "pool", nc.gpsimd, 8)
        self.sp = Eng(self, "sp", nc.sync, 12)
        self.engs = [self.pe, self.act, self.dve, self.pool, self.sp]
        self.dma_events = []

    def new_sem(self, name):
        self.nsem += 1
        return self.es.enter_context(self.nc.semaphore(name))

    def sb(self, name, shape, dt):
        return self.es.enter_context(self.nc.sbuf_tensor(name, shape, dt))

    def _deps(self, r, w, acc):
        deps = {}
        for b in r:
            for k, ev in b.writes.items():
                if deps.get(k, (None, 0))[1] < ev[1]:
                    deps[k] = ev
        for b in w:
            for k, ev in b.reads.items():
                if deps.get(k, (None, 0))[1] < ev[1]:
                    deps[k] = ev
            if not acc:
                for k, ev in b.writes.items():
                    if deps.get(k, (None, 0))[1] < ev[1]:
                        deps[k] = ev
            else:
                for k, ev in b.wbar.items():
                    if deps.get(k, (None, 0))[1] < ev[1]:
                        deps[k] = ev
        return deps

    def _record(self, ev, r, w, acc):
        k = id(ev[0])
        for b in r:
            b.reads[k] = ev
        for b in w:
            if not acc:
                wb = dict(b.reads)
                for kk, e2 in b.writes.items():
                    if wb.get(kk, (None, 0))[1] < e2[1]:
                        wb[kk] = e2
                b.wbar = wb
                b.writes = {}
                b.reads = {}
            b.writes[k] = ev

    def op(self, eng, fn, r=(), w=(), acc=False):
        for ev in self._deps(r, w, acc).values():
            eng.wait(ev)
        inst = fn()
        eng.count += 1
        inst.then_inc(eng.sem, 1)
        ev = (eng.sem, eng.count)
        eng.seen[id(eng.sem)] = max(eng.seen.get(id(eng.sem), 0), 0)
        self._record(ev, r, w, acc)
        self._maybe_pause()
        return ev

    def _maybe_pause(self):
        co = self.co
        if co is not None and co.started and not co.finished and threading.current_thread() is co.t and getattr(co, "armed", False):
            co.pause()

    def dma(self, eng, out, in_, r=(), w=(), acc=False, **kw):
        for ev in self._deps(r, w, acc).values():
            eng.wait(ev)
        slot = eng.ring[eng.rpos]
        eng.rpos = (eng.rpos + 1) % len(eng.ring)
        if slot[1] > 0:
            eng.wait((slot[0], slot[1]))
        slot[1] += 16
        eng.h.dma_start(out=out, in_=in_, **kw).then_inc(slot[0], 16)
        ev = (slot[0], slot[1])
        self._record(ev, r, w, acc)
        self.dma_events.append(ev)
        self._maybe_pause()
        return ev

    def barrier(self):
        evs = [(e.sem, e.count) for e in self.engs if e.count > 0]
        for e in self.engs:
            for s in e.ring:
                if s[1] > 0:
                    evs.append((s[0], s[1]))
        for e in self.engs:
            for ev in evs:
                if ev[0] is e.sem:
                    continue
                e.wait(ev)


def tt(eng, out, in0, in1, op):
    return eng.h.tensor_tensor(out=out, in0=in0, in1=in1, op=op)


def build_nc(debug=0):
    nc = bass.Bass("TRN2", target_bir_lowering=False)
    K = Kern(nc)
    P = K.pe
    A = K.act
    V = K.dve
    G = K.pool
    S = K.sp

    def din(name, shape, dt=F32):
        return nc.dram_tensor(name, list(shape), dt, kind="ExternalInput").ap()

    def dscr(name, shape, dt):
        kind = "ExternalOutput" if debug else "Internal"
        return nc.dram_tensor(name, list(shape), dt, kind=kind).ap()

    x_d = din("x", [NB, SEQ, D])
    ctx_d = din("ctx", [NB, CTXL, D])
    cT_d = din("cT", [128, 8, 3])
    wmod_d = din("w_mod", [D, 6 * D])
    bmod_d = din("b_mod", [1, 6 * D])
    n1g_d = din("n1g", [128, 8])
    n2g_d = din("n2g", [128, 8])
    win_d = din("w_in", [D, 4096])
    qg_d = din("qg", [128, 2, 64])
    kg_d = din("kg", [128, 2, 64])
    ropeC_d = din("ropeC", [128, 16, 64])
    ropeS_d = din("ropeS", [128, 16, 64])
    ident_d = din("ident", [128, 128])
    sel_d = din("sel", [3, NB, 128])
    esel_d = din("esel", [128, 128])
    lamr_d, lami_d, ldt_d = din("lamr", [128, 64]), din("lami", [128, 64]), din("ldt", [128, 64])
    bre_d, bim_d = din("bre", [128, 64, 16]), din("bim", [128, 64, 16])
    cre_d, cim_d = din("cre", [128, 64, 16]), din("cim", [128, 64, 16])
    dcol_d = din("dcol", [128, 32])
    maskf_d, maskb_d = din("maskf", [128, 128]), din("maskb", [128, 128])
    kv_d = din("kv", [128, 9])
    iota_d = din("iota2", [128, 2, 288])
    j128_d = din("j128", [128, 128])
    wglu_d = din("w_glu", [512, 512])
    bglu_d = din("bglu", [128, 4])
    wbra_d = din("w_br_attn", [D, D])
    wbrs_d = din("w_br_ssm", [512, D])
    wout_d = din("w_out", [D, D])
    w1_d = din("w_mlp1", [D, 4096])
    b1c_d = din("b1c", [128, 32])
    w2_d = din("w_mlp2", [4096, D])
    b2r_d = din("b2r", [1, D])
    out_d = nc.dram_tensor("out", [NB, SEQ, D], F32, kind="ExternalOutput").ap()

    hT_s = dscr("hT_s", [D, NL], BF16)
    qT_s = dscr("qT_s", [8, 128, NL], BF16)
    kT_s = dscr("kT_s", [2, 128, NB * KEYS], BF16)
    v_s = dscr("v_s", [NB * KEYS, 256], BF16)
    ul_s = dscr("ul_s", [NB, 256, 8, 512], BF16)
    uc_s = dscr("uc_s", [NB, 32, 8, 512], BF16)
    mod_s = dscr("mod_s", [3, 6 * D], F32)
    dram_bufs = {}

    def db(name):
        if name not in dram_bufs:
            dram_bufs[name] = Buf(name)
        return dram_bufs[name]

    ident_f = K.sb("ident_f", [128, 128], F32)
    ident_b = K.sb("ident_b", [128, 128], BF16)
    b_identf = Buf("identf")
    b_identb = Buf("identb")
    K.dma(S, ident_f[:], ident_d, w=[b_identf])
    K.op(V, lambda: nc.vector.tensor_copy(out=ident_b[:], in_=ident_f[:]), r=[b_identf], w=[b_identb])

    modcol = K.sb("modcol", [128, 8, 4, 3], F32)
    Gb = K.sb("Gb", [128, 2, NB, D], F32)
    b_Gb = Buf("Gb")
    A1 = K.sb("A1", [128, 8, 3], F32)
    A2 = K.sb("A2", [128, 8, 3], F32)
    b_modrows = Buf("modrows")
    b_modcol = Buf("modcol")
    b_A = Buf("A12")
    win_st = ExitStack()
    win_sb = win_st.enter_context(nc.sbuf_tensor("win_sb", [128, 8, 2048], BF16))
    b_win = Buf()
    win_v = win_d.rearrange("(k p) n -> p k n", p=128)
    for hf in range(2):
        K.dma(G, win_sb[:, :, hf * 1024:(hf + 1) * 1024], win_v[:, :, hf * 1024:(hf + 1) * 1024],
              w=[b_win], acc=True)

    with ExitStack() as st:
        modrows = st.enter_context(nc.sbuf_tensor("modrows", [3, 6 * D], F32))
        sel = st.enter_context(nc.sbuf_tensor("sel_sb", [3, NB, 128], F32))
        b_sel = Buf()
        K.dma(S, sel[:], sel_d, w=[b_sel])
        cT = st.enter_context(nc.sbuf_tensor("cT_sb", [128, 8, 3], F32))
        scT = st.enter_context(nc.sbuf_tensor("scT", [128, 8, 3], F32))
        ones13 = st.enter_context(nc.sbuf_tensor("ones13", [1, 4], F32))
        bmod = st.enter_context(nc.sbuf_tensor("bmod", [1, 6 * D], F32))
        n1g = st.enter_context(nc.sbuf_tensor("n1g_sb", [128, 8], F32))
        n2g = st.enter_context(nc.sbuf_tensor("n2g_sb", [128, 8], F32))
        wm = [st.enter_context(nc.sbuf_tensor("wm%d" % i, [128, 8, 512], F32)) for i in range(2)]
        pm = [st.enter_context(nc.psum_tensor("pm%d" % i, [128, 512], F32)) for i in range(2)]
        ptc = st.enter_context(nc.psum_tensor("ptc", [128, 32, 4], F32))
        b_cT, b_scT, b_ones, b_bmod, b_ng = Buf(), Buf(), Buf(), Buf(), Buf()
        b_wm = [Buf(), Buf()]
        b_pm = [Buf(), Buf()]
        b_ptc = Buf()
        K.dma(S, cT[:], cT_d, w=[b_cT])
        K.dma(S, bmod[:], bmod_d, w=[b_bmod])
        K.dma(S, n1g[:], n1g_d, w=[b_ng], acc=True)
        K.dma(S, n2g[:], n2g_d, w=[b_ng], acc=True)
        K.op(A, lambda: nc.scalar.activation(out=scT[:], in_=cT[:], func=AF.Silu), r=[b_cT], w=[b_scT])
        K.op(V, lambda: nc.vector.memset(ones13[:], 1.0), w=[b_ones])
        wmod_v = wmod_d.rearrange("(k p) n -> p k n", p=128)
        for blk in range(12):
            wb = wm[blk % 2]
            K.dma(S, wb[:], wmod_v[:, :, blk * 512:(blk + 1) * 512], w=[b_wm[blk % 2]])

            def mm(blk=blk, wb=wb):
                for k in range(8):
                    nc.tensor.matmul(pm[blk % 2][0:3, :], lhsT=scT[:, k, :], rhs=wb[:, k, :],
                                     start=(k == 0), stop=False)
                return nc.tensor.matmul(pm[blk % 2][0:3, :], lhsT=ones13[0:1, 0:3],
                                        rhs=bmod[0:1, blk * 512:(blk + 1) * 512],
                                        start=False, stop=True)
            K.op(P, mm, r=[b_scT, b_wm[blk % 2], b_ones, b_bmod], w=[b_pm[blk % 2]])
            K.op(V, lambda blk=blk: nc.vector.tensor_copy(out=modrows[:, blk * 512:(blk + 1) * 512],
                                                          in_=pm[blk % 2][0:3, :]),
                 r=[b_pm[blk % 2]], w=[b_modrows], acc=True)
        def tr():
            inst = None
            for j, ch in enumerate((0, 1, 3, 4)):
                for k in range(8):
                    c0 = ch * D + k * 128
                    inst = nc.tensor.transpose(out=ptc[:, j * 8 + k, 0:3], in_=modrows[0:3, c0:c0 + 128],
                                               identity=ident_f[0:3, 0:3])
            return inst
        K.op(P, tr, r=[b_modrows, b_identf], w=[b_ptc])
        K.op(V, lambda: nc.vector.tensor_copy(
            out=modcol[:].rearrange("p k j r -> p j k r"),
            in_=ptc[:, :, 0:3].rearrange("p (j k) r -> p j k r", j=4)), r=[b_ptc], w=[b_modcol])
        for (Ax, ng, j) in ((A1, n1g, 1), (A2, n2g, 3)):
            K.op(V, lambda Ax=Ax, j=j: nc.vector.tensor_scalar(
                out=Ax[:], in0=modcol[:, :, j, :], scalar1=1.0, scalar2=None, op0=ALU.add),
                r=[b_modcol], w=[b_A], acc=True)
            K.op(V, lambda Ax=Ax, ng=ng: nc.vector.tensor_tensor(
                out=Ax[:], in0=Ax[:], in1=ng[:].rearrange("p (k o) -> p k o", o=1).to_broadcast([128, 8, 3]),
                op=ALU.mult), r=[b_A, b_ng], w=[b_A])
        gi = 0
        for gj, ch in enumerate((2, 5)):
            for b in range(NB):
                for hf in range(2):
                    c0 = ch * D + hf * 512
                    pp = pm[gi % 2]
                    K.op(P, lambda pp=pp, b=b, c0=c0: nc.tensor.matmul(
                        pp[:, :], lhsT=sel[0:3, b, :], rhs=modrows[0:3, c0:c0 + 512], start=True, stop=True),
                        r=[b_sel, b_modrows], w=[b_pm[gi % 2]])
                    K.op(V, lambda pp=pp, gj=gj, b=b, hf=hf: nc.vector.tensor_copy(
                        out=Gb[:, gj, b, hf * 512:(hf + 1) * 512], in_=pp[:, :]),
                        r=[b_pm[gi % 2]], w=[b_Gb], acc=True)
                    gi += 1
        if debug:
            K.dma(S, mod_s, modrows[:], r=[b_modrows], w=[db("mod_s")])
        K.barrier()

    if debug == 1:
        fin = K.sb("fin", [128, 8, 6], F32)
        b_fin = Buf()
        K.op(V, lambda: nc.vector.tensor_copy(out=fin[:, :, 0:3], in_=A1[:]), r=[b_A], w=[b_fin])
        K.op(V, lambda: nc.vector.tensor_copy(out=fin[:, :, 3:6], in_=A2[:]), r=[b_A, b_fin], w=[b_fin])
        dbg = nc.dram_tensor("dbg", [128, 8, 6], F32, kind="ExternalOutput").ap()
        K.dma(S, dbg, fin[:], r=[b_fin], w=[db("dbg")])
        K.barrier()
        return nc


    def norm_T1(env, j):
        xt, b_xt = env["xt"][j % 3], env["b_xt"][j % 3]
        xn, b_xn = env["xn"][j % 3], env["b_xn"][j % 3]
        st_, b_st = env["stat"][j % 3], env["b_stat"][j % 3]
        junk, b_junk = env["junk"], env["b_junk"]
        K.op(A, lambda: nc.scalar.activation(out=junk[:], in_=xt[:], func=AF.Square, accum_out=st_[:, 0:1]),
             r=[b_xt], w=[b_junk, b_st])
        K.op(A, lambda: nc.scalar.activation(out=st_[:, 1:2], in_=st_[:, 0:1], func=AF.Sqrt,
                                             scale=1.0 / D, bias=env["eps"][:, 0:1]), r=[b_st, env["b_eps"]], w=[b_st])
        K.op(V, lambda: nc.vector.reciprocal(out=st_[:, 2:3], in_=st_[:, 1:2]), r=[b_st], w=[b_st])
        if env.get("xn_on_dve"):
            K.op(V, lambda: nc.vector.tensor_scalar(out=xn[:], in0=xt[:], scalar1=st_[:, 2:3], scalar2=None, op0=ALU.mult),
                 r=[b_xt, b_st], w=[b_xn])
        else:
            K.op(A, lambda: nc.scalar.activation(out=xn[:], in_=xt[:], func=AF.Copy, scale=st_[:, 2:3]),
                 r=[b_xt, b_st], w=[b_xn])

    def norm_T2a(env, j):
        xn, b_xn = env["xn"][j % 3], env["b_xn"][j % 3]
        pT, b_pT = env["pT"], env["b_pT"]

        def tr():
            inst = None
            for k in range(8):
                inst = nc.tensor.transpose(out=pT[:, k, :], in_=xn[:, k * 128:(k + 1) * 128], identity=ident_b[:])
            return inst
        K.op(P, tr, r=[b_xn, b_identb], w=[b_pT])

    def norm_T2b(env, j, r, Acol, Bcol_j):
        hTt, b_hTt = env["hTt"][j % 3], env["b_hTt"][j % 3]
        pT, b_pT = env["pT"], env["b_pT"]
        for k in range(8):
            K.op(A, lambda k=k: nc.scalar.activation(
                out=hTt[:, k, :], in_=pT[:, k, :], func=AF.Identity, scale=Acol[:, k, r:r + 1],
                bias=modcol[:, k, Bcol_j, r:r + 1]),
                r=[b_pT, b_A, b_modcol], w=[b_hTt], acc=(k > 0))
        return hTt, b_hTt

    def norm_T(env, j, r, Acol, Bcol_j):
        norm_T1(env, j)
        norm_T2a(env, j)
        return norm_T2b(env, j, r, Acol, Bcol_j)

    envc = [0]

    def mk_norm_env(st):
        env = {}
        envc[0] += 1
        pf = "e%d_" % envc[0]
        env["xt"] = [st.enter_context(nc.sbuf_tensor(pf + "xt%d" % i, [128, D], F32)) for i in range(3)]
        env["xn"] = [st.enter_context(nc.sbuf_tensor(pf + "xn%d" % i, [128, D], BF16)) for i in range(3)]
        env["hTt"] = [st.enter_context(nc.sbuf_tensor(pf + "hTt%d" % i, [128, 8, 128], BF16)) for i in range(3)]
        env["stat"] = [st.enter_context(nc.sbuf_tensor(pf + "stat%d" % i, [128, 4], F32)) for i in range(3)]
        env["junk"] = st.enter_context(nc.sbuf_tensor(pf + "junk", [128, D], BF16))
        env["eps"] = st.enter_context(nc.sbuf_tensor(pf + "eps_t", [128, 1], F32))
        env["pT"] = st.enter_context(nc.psum_tensor(pf + "pT", [128, 8, 128], BF16))
        for nm in ("xt", "xn", "hTt", "stat"):
            env["b_" + nm] = [Buf(), Buf(), Buf()]
        env["b_junk"], env["b_pT"], env["b_eps"] = Buf(), Buf(), Buf()
        K.op(V, lambda: nc.vector.memset(env["eps"][:], EPS), w=[env["b_eps"]])
        return env

    def head_norm(env, src, b_src, H, dst, b_dst, CAt, SBt, j, tagbufs, part=0):
        sq, qn, T1, T2, hs = tagbufs["sq"], tagbufs["qn"], tagbufs["T1"], tagbufs["T2"], tagbufs["hs"]
        b_sq, b_qn, b_T1, b_T2, b_hs = tagbufs["b_sq"], tagbufs["b_qn"], tagbufs["b_T1"], tagbufs["b_T2"], tagbufs["b_hs"]
        W = H * 64
        if part in (0, 1):
            K.op(A, lambda: nc.scalar.activation(out=sq[:, 0:W], in_=src, func=AF.Square), r=[b_src], w=[b_sq])
        if part in (0, 2):
            K.op(V, lambda: nc.vector.tensor_reduce(out=hs[:, 0, 0:H], in_=sq[:, 0:W].rearrange("p (h e) -> p h e", e=64),
                                                    axis=AX.X, op=ALU.add), r=[b_sq], w=[b_hs])
        if part in (0, 3):
            K.op(A, lambda: nc.scalar.activation(out=hs[:, 1, 0:H], in_=hs[:, 0, 0:H], func=AF.Sqrt,
                                                 scale=1.0 / 64, bias=env["eps"][:, 0:1]), r=[b_hs, env["b_eps"]], w=[b_hs])
        if part not in (0, 4):
            return
        K.op(V, lambda: nc.vector.reciprocal(out=hs[:, 2, 0:H], in_=hs[:, 1, 0:H]), r=[b_hs], w=[b_hs])
        K.op(V, lambda: nc.vector.tensor_tensor(
            out=qn[:, 0:W].rearrange("p (h e) -> p h e", e=64), in0=src.rearrange("p (h e) -> p h e", e=64),
            in1=hs[:, 2, 0:H].rearrange("p (h o) -> p h o", o=1).to_broadcast([128, H, 64]), op=ALU.mult),
            r=[b_src, b_hs], w=[b_qn])
        cab = CAt.rearrange("p (o e) -> p o e", o=1).to_broadcast([128, H, 64])
        if SBt is None:
            K.op(V, lambda: nc.vector.tensor_tensor(out=dst.rearrange("p (h e) -> p h e", e=64),
                                                    in0=qn[:, 0:W].rearrange("p (h e) -> p h e", e=64),
                                                    in1=cab, op=ALU.mult), r=[b_qn, env["b_tab"]], w=[b_dst])
            return
        K.op(V, lambda: nc.vector.tensor_tensor(out=T1[:, 0:W].rearrange("p (h e) -> p h e", e=64),
                                                in0=qn[:, 0:W].rearrange("p (h e) -> p h e", e=64),
                                                in1=cab, op=ALU.mult), r=[b_qn, env["b_tab"]], w=[b_T1])
        for hf in range(2):
            sbv = SBt.rearrange("p (o a f r) -> p o a f r", o=1, a=2, f=2)[:, :, :, hf, :].to_broadcast([128, H, 2, 16])
            qv = qn[:, 0:W].rearrange("p (h a f r) -> p h a f r", a=2, f=2, r=16)[:, :, :, 1 - hf, :]
            ov = T2[:, 0:W].rearrange("p (h a f r) -> p h a f r", a=2, f=2, r=16)[:, :, :, hf, :]
            K.op(V, lambda sbv=sbv, qv=qv, ov=ov: nc.vector.tensor_tensor(out=ov, in0=qv, in1=sbv, op=ALU.mult),
                 r=[b_qn, env["b_tab"]], w=[b_T2], acc=(hf > 0))
        K.op(V, lambda: nc.vector.tensor_tensor(out=dst, in0=T1[:, 0:W], in1=T2[:, 0:W], op=ALU.add),
             r=[b_T1, b_T2], w=[b_dst])

    with ExitStack() as st:
        env = mk_norm_env(st)
        tabs = {}
        env["b_tab"] = Buf()
        gq = st.enter_context(nc.sbuf_tensor("gq_sb", [128, 2, 64], F32))
        gk = st.enter_context(nc.sbuf_tensor("gk_sb", [128, 2, 64], F32))
        b_g = Buf()
        K.dma(S, gq[:], qg_d, w=[b_g], acc=True)
        K.dma(S, gk[:], kg_d, w=[b_g], acc=True)
        for nm, src_d, gsel in (("CAq", ropeC_d, (gq, 0)), ("SBq", ropeS_d, (gq, 1)),
                                ("CAk", ropeC_d, (gk, 0)), ("SBk", ropeS_d, (gk, 1))):
            t = st.enter_context(nc.sbuf_tensor(nm, [128, 16, 64], F32))
            tabs[nm] = t
            bt = Buf()
            K.dma(S, t[:], src_d, w=[bt])
            gt, gi_ = gsel
            K.op(V, lambda t=t, gt=gt, gi_=gi_: nc.vector.tensor_tensor(
                out=t[:], in0=t[:], in1=gt[:, gi_:gi_ + 1, :].to_broadcast([128, 16, 64]), op=ALU.mult),
                r=[bt, b_g], w=[bt])
            K.op(V, lambda t=t: nc.vector.tensor_copy(out=t[:, 0:1, 0:1], in_=t[:, 0:1, 0:1]),
                 r=[bt], w=[env["b_tab"]], acc=True)
        tb = {}
        for nm, shp, dt in (("sq", [128, D], F32), ("qn", [128, D], F32), ("T1", [128, D], F32),
                            ("T2", [128, D], F32), ("hs", [128, 3, 16], F32)):
            tb[nm] = st.enter_context(nc.sbuf_tensor("s1_" + nm, shp, dt))
            tb["b_" + nm] = Buf()
        qr = [st.enter_context(nc.sbuf_tensor("qr%d" % i, [128, D], BF16)) for i in range(2)]
        kr = [st.enter_context(nc.sbuf_tensor("kr%d" % i, [128, 256], BF16)) for i in range(2)]
        vb = [st.enter_context(nc.sbuf_tensor("vb%d" % i, [128, 256], BF16)) for i in range(2)]
        ub = [st.enter_context(nc.sbuf_tensor("ub%d" % i, [128, 512], BF16)) for i in range(2)]
        qTt = [st.enter_context(nc.sbuf_tensor("qTt%d" % i, [128, 8, 128], BF16)) for i in range(2)]
        kTt = [st.enter_context(nc.sbuf_tensor("kTt%d" % i, [128, 2, 128], BF16)) for i in range(2)]
        b_qr, b_kr, b_vb, b_ub, b_qTt, b_kTt = ([Buf(), Buf()] for _ in range(6))
        pq = st.enter_context(nc.psum_tensor("pq", [128, D], F32))
        pkv = st.enter_context(nc.psum_tensor("pkv", [128, 512], F32))
        pu = st.enter_context(nc.psum_tensor("pu", [128, 512], F32))
        pT2 = st.enter_context(nc.psum_tensor("pT2", [128, 8, 128], BF16))
        pTk = st.enter_context(nc.psum_tensor("pTk", [128, 2, 128], BF16))
        b_pq, b_pkv, b_pu, b_pT2, b_pTk = Buf(), Buf(), Buf(), Buf(), Buf()

        tiles = []
        for b in range(NB):
            for ih in range(2):
                tiles.append(("c", b, ih, 0))
            for i in range(8):
                for cb in range(2):
                    tiles.append(("l", b, i, cb))
        hts = {}

        def load_x(j):
            kind, b, i, cb = tiles[j]
            xt, b_xt = env["xt"][j % 3], env["b_xt"][j % 3]
            if kind == "l":
                src = x_d[b].rearrange("(c i) d -> i c d", i=8)[i, cb * 128:(cb + 1) * 128, :]
                K.dma(A, xt[:], src, w=[b_xt])
            else:
                cv = ctx_d[b].rearrange("(c i) d -> i c d", i=8)
                for isub in range(4):
                    K.dma(A, xt[isub * 32:(isub + 1) * 32, :], cv[4 * i + isub, :, :], w=[b_xt], acc=(isub > 0))

        tbk = {}
        for nm, shp, dt in (("sq", [128, 256], F32), ("qn", [128, 256], F32), ("T1", [128, 256], F32),
                            ("T2", [128, 256], F32), ("hs", [128, 3, 16], F32)):
            tbk[nm] = st.enter_context(nc.sbuf_tensor("s1k_" + nm, shp, dt))
            tbk["b_" + nm] = Buf()

        def phaseA1(j):
            if j + 2 < len(tiles):
                load_x(j + 2)
            norm_T1(env, j)

        def phaseA2(j):
            kind, b, i, cb = tiles[j]
            norm_T2a(env, j)

        def phaseA3(j):
            kind, b, i, cb = tiles[j]
            hTt, b_hTt = norm_T2b(env, j, (b if kind == "l" else 2), A1, 0)
            hts[j] = (hTt, b_hTt)
            if kind == "l":
                col0 = b * SEQ + i * 256 + cb * 128
                K.dma(S, hT_s.rearrange("(k p) n -> p k n", p=128)[:, :, col0:col0 + 128], hTt[:],
                      r=[b_hTt], w=[db("hT_s")], acc=True)

        def phaseBmm(j):
            kind, b, i, cb = tiles[j]
            hTt, b_hTt = hts[j]

            def mm(dst, c0, n):
                def f():
                    inst = None
                    for blk in range(n):
                        for k in range(8):
                            inst = nc.tensor.matmul(dst[:, blk * 512:(blk + 1) * 512], lhsT=hTt[:, k, :],
                                                    rhs=win_sb[:, k, c0 + blk * 512:c0 + (blk + 1) * 512],
                                                    start=(k == 0), stop=(k == 7))
                    return inst
                return f
            if kind == "l":
                K.op(P, mm(pq, 0, 2), r=[b_hTt, b_win], w=[b_pq])
            K.op(P, mm(pkv, 1024, 1), r=[b_hTt, b_win], w=[b_pkv])
            K.op(P, mm(pu, 1536, 1), r=[b_hTt, b_win], w=[b_pu])

        def phaseBch(j, part):
            kind, b, i, cb = tiles[j]
            jj = j % 2
            kcol0 = b * KEYS + (i * 256 + cb * 128 if kind == "l" else SEQ + i * 128)
            if kind == "l":
                jt = i * 2 + cb
                head_norm(env, pq[:, :], b_pq, 16, qr[jj][:, :], b_qr[jj], tabs["CAq"][:, jt, :], tabs["SBq"][:, jt, :], j, tb, part)
                head_norm(env, pkv[:, 0:256], b_pkv, 4, kr[jj][:, :], b_kr[jj], tabs["CAk"][:, jt, :], tabs["SBk"][:, jt, :], j, tbk, part)
            else:
                head_norm(env, pkv[:, 0:256], b_pkv, 4, kr[jj][:, :], b_kr[jj], gk[:, 0, :], None, j, tbk, part)
            if part != 4:
                return
            K.op(A, lambda: nc.scalar.copy(out=vb[jj][:], in_=pkv[:, 256:512]), r=[b_pkv], w=[b_vb[jj]])
            K.dma(S, v_s[kcol0:kcol0 + 128, :], vb[jj][:], r=[b_vb[jj]], w=[db("v_s")], acc=True)
            K.op(A, lambda: nc.scalar.copy(out=ub[jj][:], in_=pu[:, :]), r=[b_pu], w=[b_ub[jj]])
            if kind == "l":
                K.dma(S, ul_s[b, cb * 128:(cb + 1) * 128, i, :], ub[jj][:], r=[b_ub[jj]], w=[db("ul_s")], acc=True)
            else:
                for isub in range(4):
                    K.dma(S, uc_s[b, :, 4 * i + isub, :], ub[jj][isub * 32:(isub + 1) * 32, :],
                          r=[b_ub[jj]], w=[db("uc_s")], acc=True)

        def phaseC(j):
            kind, b, i, cb = tiles[j]
            jj = j % 2
            kcol0 = b * KEYS + (i * 256 + cb * 128 if kind == "l" else SEQ + i * 128)
            if kind == "l":
                col0 = b * SEQ + i * 256 + cb * 128

                def trq():
                    inst = None
                    for k in range(8):
                        inst = nc.tensor.transpose(out=pT2[:, k, :], in_=qr[jj][:, k * 128:(k + 1) * 128], identity=ident_b[:])
                    return inst
                K.op(P, trq, r=[b_qr[jj], b_identb], w=[b_pT2])
                K.op(A, lambda: nc.scalar.copy(out=qTt[jj][:], in_=pT2[:]), r=[b_pT2], w=[b_qTt[jj]])
                K.dma(S, qT_s.rearrange("h p n -> p h n")[:, :, col0:col0 + 128], qTt[jj][:],
                      r=[b_qTt[jj]], w=[db("qT_s")], acc=True)

            def trk():
                inst = None
                for k in range(2):
                    inst = nc.tensor.transpose(out=pTk[:, k, :], in_=kr[jj][:, k * 128:(k + 1) * 128], identity=ident_b[:])
                return inst
            K.op(P, trk, r=[b_kr[jj], b_identb], w=[b_pTk])
            K.op(A, lambda: nc.scalar.copy(out=kTt[jj][:], in_=pTk[:]), r=[b_pTk], w=[b_kTt[jj]])
            K.dma(S, kT_s.rearrange("h p n -> p h n")[:, :, kcol0:kcol0 + 128], kTt[jj][:],
                  r=[b_kTt[jj]], w=[db("kT_s")], acc=True)

        nt_ = len(tiles)
        load_x(0)
        load_x(1)
        for j0 in range(2):
            phaseA1(j0)
            phaseA2(j0)
            phaseA3(j0)
        for j in range(nt_):
            if j + 2 < nt_:
                phaseA1(j + 2)
            phaseBmm(j)
            if j + 2 < nt_:
                phaseA2(j + 2)
            phaseBch(j, 1)
            phaseBch(j, 2)
            phaseBch(j, 3)
            phaseBch(j, 4)
            if j + 2 < nt_:
                phaseA3(j + 2)
            if j >= 1:
                phaseC(j - 1)
        phaseC(nt_ - 1)
        K.barrier()
    win_st.close()
    if debug == 2:
        return nc

    TWO_PI = 2.0 * math.pi
    CW1 = 6.28125
    CW2 = float(np.float32(TWO_PI - CW1))
    wd_s = dscr("wd_s", [128, 64, 128], BF16)
    wd2_s = dscr("wd2_s", [128, 64, 128], BF16)
    cwa_s = dscr("cwa_s", [128, 64, 128], BF16)
    cwb_s = dscr("cwb_s", [128, 64, 128], BF16)
    toep_s = dscr("toep_s", [128, 32, 128], BF16)
    rc_s = dscr("rc_s", [128, 2, 64], F32)
    prep_st = ExitStack()
    b_prep = Buf("prep")
    pw_shared = []

    def prep_body():
        sc_y = prep_st.enter_context(nc.sbuf_tensor("psc_y", [128, 576], F32))
        sc_ki = prep_st.enter_context(nc.sbuf_tensor("psc_ki", [128, 576], I32))
        sc_kf = prep_st.enter_context(nc.sbuf_tensor("psc_kf", [128, 576], F32))
        sc_r = prep_st.enter_context(nc.sbuf_tensor("psc_r", [128, 576], F32))

        def sincos(ang, n, sin_out, cos_out, bufr, bufw):
            y, ki, kf, r_ = sc_y[:, 0:n], sc_ki[:, 0:n], sc_kf[:, 0:n], sc_r[:, 0:n]
            rr, ww = [b_prep] + bufr, [b_prep] + bufw
            o = lambda fn: K.op(V, fn, r=rr, w=ww)
            o(lambda: nc.vector.tensor_scalar(out=y, in0=ang, scalar1=1.0 / TWO_PI, scalar2=None, op0=ALU.mult))
            o(lambda: nc.vector.tensor_copy(out=ki, in_=y))
            o(lambda: nc.vector.tensor_copy(out=kf, in_=ki))
            o(lambda: nc.vector.scalar_tensor_tensor(out=r_, in0=kf, scalar=-CW1, in1=ang, op0=ALU.mult, op1=ALU.add))
            o(lambda: nc.vector.scalar_tensor_tensor(out=r_, in0=kf, scalar=-CW2, in1=r_, op0=ALU.mult, op1=ALU.add))
            o(lambda: nc.vector.tensor_scalar(out=kf, in0=r_, scalar1=math.pi / 2, scalar2=-TWO_PI, op0=ALU.is_gt, op1=ALU.mult))
            o(lambda: nc.vector.scalar_tensor_tensor(out=y, in0=r_, scalar=math.pi / 2, in1=kf, op0=ALU.add, op1=ALU.add))
            o(lambda: nc.vector.tensor_scalar(out=y, in0=y, scalar1=-math.pi, scalar2=math.pi, op0=ALU.max, op1=ALU.min))
            o(lambda: nc.vector.tensor_scalar(out=r_, in0=r_, scalar1=-math.pi, scalar2=math.pi, op0=ALU.max, op1=ALU.min))
            K.op(A, lambda: nc.scalar.activation(out=cos_out, in_=y, func=AF.Sin), r=rr, w=ww)
            K.op(A, lambda: nc.scalar.activation(out=sin_out, in_=r_, func=AF.Sin), r=rr, w=ww)
        def pt(name, shape, dt=F32):
            return prep_st.enter_context(nc.sbuf_tensor("pp_" + name, shape, dt))
        lamr, lami, ldt = pt("lamr", [128, 64]), pt("lami", [128, 64]), pt("ldt", [128, 64])
        bre, bim = pt("bre", [128, 64, 16]), pt("bim", [128, 64, 16])
        cre, cim = pt("cre", [128, 64, 16]), pt("cim", [128, 64, 16])
        dcol = pt("dcol", [128, 32])
        mkf, mkb = pt("mkf", [128, 128]), pt("mkb", [128, 128])
        kv = pt("kv", [128, 9])
        for t_, d_ in ((lamr, lamr_d), (lami, lami_d), (ldt, ldt_d), (bre, bre_d), (bim, bim_d), (cre, cre_d),
                       (cim, cim_d), (dcol, dcol_d), (mkf, maskf_d), (mkb, maskb_d), (kv, kv_d)):
            K.dma(S, t_[:], d_, w=[b_prep], acc=True)
        dt_ = pt("dt", [128, 64]); xr = pt("xr", [128, 64]); th = pt("th", [128, 64])
        MX = pt("MX", [128, 9, 64]); ANG = pt("ANG", [128, 9, 64])
        EPOS = pt("EPOS", [128, 9, 64]); ENEG = pt("ENEG", [128, 9, 64])
        COSM = pt("COSM", [128, 9, 64]); SINM = pt("SINM", [128, 9, 64])
        PR = pt("PR", [128, 9, 64]); PI = pt("PI", [128, 9, 64]); NR = pt("NR", [128, 9, 64]); NI = pt("NI", [128, 9, 64])

        def vop(fn):
            K.op(V, fn, r=[b_prep], w=[b_prep])

        def aop(fn):
            K.op(A, fn, r=[b_prep], w=[b_prep])

        aop(lambda: nc.scalar.activation(out=dt_[:], in_=ldt[:], func=AF.Exp))
        vop(lambda: nc.vector.tensor_tensor(out=xr[:], in0=lamr[:], in1=dt_[:], op=ALU.mult))
        vop(lambda: nc.vector.tensor_tensor(out=th[:], in0=lami[:], in1=dt_[:], op=ALU.mult))
        kvb = kv[:].rearrange("p (m o) -> p m o", o=1).to_broadcast([128, 9, 64])
        vop(lambda: nc.vector.tensor_tensor(out=MX[:], in0=xr[:].rearrange("p (o t) -> p o t", o=1).to_broadcast([128, 9, 64]), in1=kvb, op=ALU.mult))
        vop(lambda: nc.vector.tensor_tensor(out=ANG[:], in0=th[:].rearrange("p (o t) -> p o t", o=1).to_broadcast([128, 9, 64]), in1=kvb, op=ALU.mult))
        aop(lambda: nc.scalar.activation(out=EPOS[:], in_=MX[:], func=AF.Exp))
        aop(lambda: nc.scalar.activation(out=ENEG[:], in_=MX[:], func=AF.Exp, scale=-1.0))
        fl = lambda t_: t_[:].rearrange("p m t -> p (m t)")
        sincos(fl(ANG), 576, fl(SINM), fl(COSM), [], [])
        vop(lambda: nc.vector.tensor_tensor(out=PR[:], in0=EPOS[:], in1=COSM[:], op=ALU.mult))
        vop(lambda: nc.vector.tensor_tensor(out=PI[:], in0=EPOS[:], in1=SINM[:], op=ALU.mult))
        vop(lambda: nc.vector.tensor_tensor(out=NR[:], in0=ENEG[:], in1=COSM[:], op=ALU.mult))
        vop(lambda: nc.vector.scalar_tensor_tensor(out=NI[:], in0=ENEG[:], scalar=-1.0, in1=SINM[:], op0=ALU.mult, op1=ALU.mult))
        K.dma(S, rc_s[:, 0, :], EPOS[:, 8, :], r=[b_prep], w=[db("rc_s")], acc=True)
        K.dma(S, rc_s[:, 1, :], ANG[:, 8, :], r=[b_prep], w=[db("rc_s")], acc=True)
        ar1 = pt("ar1", [128, 64]); den = pt("den", [128, 64]); tq = pt("tq", [128, 64]); tq2 = pt("tq2", [128, 64])
        cr = pt("cr", [128, 64]); ci = pt("ci", [128, 64])
        vop(lambda: nc.vector.tensor_scalar(out=ar1[:], in0=PR[:, 1, :], scalar1=-1.0, scalar2=None, op0=ALU.add))
        vop(lambda: nc.vector.tensor_tensor(out=den[:], in0=lamr[:], in1=lamr[:], op=ALU.mult))
        vop(lambda: nc.vector.tensor_tensor(out=tq[:], in0=lami[:], in1=lami[:], op=ALU.mult))
        vop(lambda: nc.vector.tensor_tensor(out=den[:], in0=den[:], in1=tq[:], op=ALU.add))
        vop(lambda: nc.vector.reciprocal(out=den[:], in_=den[:]))
        vop(lambda: nc.vector.tensor_tensor(out=tq[:], in0=ar1[:], in1=lamr[:], op=ALU.mult))
        vop(lambda: nc.vector.tensor_tensor(out=tq2[:], in0=PI[:, 1, :], in1=lami[:], op=ALU.mult))
        vop(lambda: nc.vector.tensor_tensor(out=tq[:], in0=tq[:], in1=tq2[:], op=ALU.add))
        vop(lambda: nc.vector.tensor_tensor(out=cr[:], in0=tq[:], in1=den[:], op=ALU.mult))
        vop(lambda: nc.vector.tensor_tensor(out=tq[:], in0=PI[:, 1, :], in1=lamr[:], op=ALU.mult))
        vop(lambda: nc.vector.tensor_tensor(out=tq2[:], in0=ar1[:], in1=lami[:], op=ALU.mult))
        vop(lambda: nc.vector.tensor_tensor(out=tq[:], in0=tq[:], in1=tq2[:], op=ALU.subtract))
        vop(lambda: nc.vector.tensor_tensor(out=ci[:], in0=tq[:], in1=den[:], op=ALU.mult))
        Bbr = pt("Bbr", [128, 64, 16]); Bbi = pt("Bbi", [128, 64, 16]); tb1 = pt("tb1", [128, 64, 16])
        crb = cr[:].rearrange("p (t o) -> p t o", o=1).to_broadcast([128, 64, 16])
        cib = ci[:].rearrange("p (t o) -> p t o", o=1).to_broadcast([128, 64, 16])
        vop(lambda: nc.vector.tensor_tensor(out=Bbr[:], in0=bre[:], in1=crb, op=ALU.mult))
        vop(lambda: nc.vector.tensor_tensor(out=tb1[:], in0=bim[:], in1=cib, op=ALU.mult))
        vop(lambda: nc.vector.tensor_tensor(out=Bbr[:], in0=Bbr[:], in1=tb1[:], op=ALU.subtract))
        vop(lambda: nc.vector.tensor_tensor(out=Bbi[:], in0=bim[:], in1=crb, op=ALU.mult))
        vop(lambda: nc.vector.tensor_tensor(out=tb1[:], in0=bre[:], in1=cib, op=ALU.mult))
        vop(lambda: nc.vector.tensor_tensor(out=Bbi[:], in0=Bbi[:], in1=tb1[:], op=ALU.add))
        EPr, EPi, FPr, FPi = (pt(n_, [128, 32, 2, 8]) for n_ in ("EPr", "EPi", "FPr", "FPi"))
        for dst_, src_ in ((EPr, PR), (EPi, PI), (FPr, NR), (FPi, NI)):
            sv = src_[:].rearrange("p m (g d) -> p m g d", d=2)
            vop(lambda dst_=dst_, sv=sv: nc.vector.tensor_copy(
                out=dst_[:, :, 0, :], in_=sv[:, 7::-1, :, 0].rearrange("p m g -> p g m")))
            vop(lambda dst_=dst_, sv=sv: nc.vector.tensor_copy(
                out=dst_[:, :, 1, :], in_=sv[:, 0:8, :, 1].rearrange("p m g -> p g m")))
        TN = 4
        Er, Ei, Fr, Fi, t1, CWr, CWi = (pt(n_, [128, TN, 8, 16]) for n_ in ("Er", "Ei", "Fr", "Fi", "t1", "CWr", "CWi"))
        ET, ET2, FT2 = (pt(n_, [128, TN, 128]) for n_ in ("ET", "ET2", "FT2"))
        ttmp = pt("ttmp", [128, 128]); ttmp2 = pt("ttmp2", [128, 128])
        pw0, b_pw0 = pw_shared[0], pw_shared[1]
        pw = [pw0, pw0]
        b_pw = [b_pw0, b_pw0]
        stg = {}
        for nm_ in ("WD", "WD2", "CWa", "CWb"):
            stg[nm_] = [pt("st_" + nm_ + str(i_), [128, TN, 128], BF16) for i_ in range(2)]
            stg["b_" + nm_] = [Buf(), Buf()]
        stg["Toep"] = [pt("st_Toep" + str(i_), [128, TN // 2, 128], BF16) for i_ in range(2)]
        stg["b_Toep"] = [Buf(), Buf()]
        pass
        v4 = lambda t_: t_[:].rearrange("p t (i h) -> p t i h", h=16)
        for ch in range(64 // TN):
            t0 = ch * TN
            cq = ch % 2
            ep = lambda t_: t_[:].rearrange("p g d i -> p (g d) i")[:, t0:t0 + TN, :].rearrange("p t (i o) -> p t i o", o=1).to_broadcast([128, TN, 8, 16])
            bb = lambda t_: t_[:, t0:t0 + TN, :].rearrange("p t (o h) -> p t o h", o=1).to_broadcast([128, TN, 8, 16])
            a8 = lambda t_: t_[:, 8, t0:t0 + TN].rearrange("p (t o q) -> p t o q", o=1, q=1).to_broadcast([128, TN, 8, 16])

            def cmul(outr, outi, ar_, ai_, br_, bi_):
                vop(lambda: nc.vector.tensor_tensor(out=outr[:], in0=ar_, in1=br_, op=ALU.mult))
                vop(lambda: nc.vector.tensor_tensor(out=t1[:], in0=ai_, in1=bi_, op=ALU.mult))
                vop(lambda: nc.vector.tensor_tensor(out=outr[:], in0=outr[:], in1=t1[:], op=ALU.subtract))
                vop(lambda: nc.vector.tensor_tensor(out=outi[:], in0=ar_, in1=bi_, op=ALU.mult))
                vop(lambda: nc.vector.tensor_tensor(out=t1[:], in0=ai_, in1=br_, op=ALU.mult))
                vop(lambda: nc.vector.tensor_tensor(out=outi[:], in0=outi[:], in1=t1[:], op=ALU.add))
            cmul(Er, Ei, ep(EPr), ep(EPi), bb(Bbr), bb(Bbi))
            cmul(Fr, Fi, ep(FPr), ep(FPi), bb(cre), bb(cim))
            cmul(CWr, CWi, a8(PR), a8(PI), Fr[:], Fi[:])
            lo, hi = slice(0, 64), slice(64, 128)

            def asm(dst, dsl, src, neg):
                if neg:
                    vop(lambda: nc.vector.tensor_scalar(out=dst, in0=src, scalar1=-1.0, scalar2=None, op0=ALU.mult))
                else:
                    vop(lambda: nc.vector.tensor_copy(out=dst, in_=src))
            asm(v4(ET)[lo], None, Er[lo], False); asm(v4(ET)[hi], None, Ei[hi], False)
            asm(v4(ET2)[lo], None, Ei[lo], False); asm(v4(ET2)[hi], None, Er[hi], True)
            asm(v4(FT2)[lo], None, Fr[lo], False); asm(v4(FT2)[hi], None, Fi[hi], True)
            K.op(V, lambda: nc.vector.tensor_copy(out=stg["CWa"][cq][lo].rearrange("p t (i h) -> p t i h", h=16), in_=CWr[lo]),
                 r=[b_prep], w=[stg["b_CWa"][cq]])
            K.op(V, lambda: nc.vector.tensor_scalar(out=stg["CWa"][cq][hi].rearrange("p t (i h) -> p t i h", h=16), in0=CWi[hi],
                                                    scalar1=-1.0, scalar2=None, op0=ALU.mult), r=[b_prep], w=[stg["b_CWa"][cq]], acc=True)
            K.op(V, lambda: nc.vector.tensor_scalar(out=stg["CWb"][cq][lo].rearrange("p t (i h) -> p t i h", h=16), in0=CWi[lo],
                                                    scalar1=-1.0, scalar2=None, op0=ALU.mult), r=[b_prep], w=[stg["b_CWb"][cq]])
            K.op(V, lambda: nc.vector.tensor_scalar(out=stg["CWb"][cq][hi].rearrange("p t (i h) -> p t i h", h=16), in0=CWr[hi],
                                                    scalar1=-1.0, scalar2=None, op0=ALU.mult), r=[b_prep], w=[stg["b_CWb"][cq]], acc=True)
            K.dma(S, cwa_s[:, t0:t0 + TN, :], stg["CWa"][cq][:], r=[stg["b_CWa"][cq]], w=[db("cwa_s")], acc=True)
            K.dma(S, cwb_s[:, t0:t0 + TN, :], stg["CWb"][cq][:], r=[stg["b_CWb"][cq]], w=[db("cwb_s")], acc=True)
            for src_, dstw in ((ET, "WD"), (ET2, "WD2")):
                for q in range(TN // 4):
                    pp, b_pp = pw[q % 2], b_pw[q % 2]

                    def trw(src_=src_, q=q, pp=pp):
                        inst = None
                        for u in range(4):
                            inst = nc.tensor.transpose(out=pp[:, u, :], in_=src_[:, q * 4 + u, :], identity=ident_f[:])
                        return inst
                    K.co.armed = False
                    K.op(P, trw, r=[b_prep, b_identf], w=[b_pp])
                    K.op(V, lambda dstw=dstw, q=q, pp=pp: nc.vector.tensor_copy(out=stg[dstw][cq][:, q * 4:q * 4 + 4, :], in_=pp[:]),
                         r=[b_pp], w=[stg["b_" + dstw][cq]], acc=(q > 0))
                    K.co.armed = True
                K.dma(S, {"WD": wd_s, "WD2": wd2_s}[dstw][:, t0:t0 + TN, :], stg[dstw][cq][:], r=[stg["b_" + dstw][cq]], w=[db(dstw + "_s")], acc=True)
            for gl in range(TN // 2):
                g = t0 // 2 + gl
                pp, b_pp = pw[gl % 2], b_pw[gl % 2]

                def tz(gl=gl, pp=pp):
                    inst = None
                    for d_ in range(2):
                        inst = nc.tensor.matmul(pp[:, d_, :], lhsT=ET[:, gl * 2 + d_, :], rhs=FT2[:, gl * 2 + d_, :], start=True, stop=True)
                    return inst
                K.co.armed = False
                K.op(P, tz, r=[b_prep], w=[b_pp])
                K.op(V, lambda pp=pp: nc.vector.tensor_tensor(out=ttmp[:], in0=pp[:, 0, :], in1=mkf[:], op=ALU.mult), r=[b_pp, b_prep], w=[b_prep])
                K.op(V, lambda pp=pp: nc.vector.tensor_tensor(out=ttmp2[:], in0=pp[:, 1, :], in1=mkb[:], op=ALU.mult), r=[b_pp, b_prep], w=[b_prep])
                K.co.armed = True
                vop(lambda: nc.vector.tensor_tensor(out=ttmp[:], in0=ttmp[:], in1=ttmp2[:], op=ALU.add))
                K.op(V, lambda g=g, gl=gl: nc.vector.scalar_tensor_tensor(out=stg["Toep"][cq][:, gl, :], in0=ident_f[:], scalar=dcol[:, g:g + 1], in1=ttmp[:],
                                                                  op0=ALU.mult, op1=ALU.add), r=[b_prep, b_identf], w=[stg["b_Toep"][cq]], acc=(gl > 0))
            K.dma(S, toep_s[:, t0 // 2:t0 // 2 + TN // 2, :], stg["Toep"][cq][:], r=[stg["b_Toep"][cq]], w=[db("toep_s")], acc=True)

    K.co = Interleaver(prep_body)
    K.co.armed = True

    OT_s = dscr("OT_s", [D, NL], BF16)
    with ExitStack() as st:
        esel_f = st.enter_context(nc.sbuf_tensor("esel_sb", [128, 128], F32))
        esel = st.enter_context(nc.sbuf_tensor("esel_bf", [128, 128], BF16))
        b_esel = Buf()
        K.dma(S, esel_f[:], esel_d, w=[b_esel])
        K.op(V, lambda: nc.vector.tensor_copy(out=esel[:], in_=esel_f[:]), r=[b_esel], w=[b_esel])
        Ohi = [st.enter_context(nc.sbuf_tensor("Ohi%d" % i, [128, 512], BF16)) for i in range(2)]
        Olo = [st.enter_context(nc.sbuf_tensor("Olo%d" % i, [128, 512], BF16)) for i in range(2)]
        b_Ohl = [Buf(), Buf()]
        for i_ in range(2):
            K.op(V, lambda i_=i_: nc.vector.memset(Ohi[i_][:], 0.0), w=[b_Ohl[i_]])
            K.op(V, lambda i_=i_: nc.vector.memset(Olo[i_][:], 0.0), w=[b_Ohl[i_]], acc=True)
        kT2 = st.enter_context(nc.sbuf_tensor("kT2", [128, 2, KEYS], BF16))
        Vb = st.enter_context(nc.sbuf_tensor("Vb", [128, 18, 4 * 65 + 64], BF16))
        qTp = [st.enter_context(nc.sbuf_tensor("qTp%d" % i, [128, SEQ], BF16)) for i in range(4)]
        pTs = [st.enter_context(nc.sbuf_tensor("pTs%d" % i, [128, 2, 512], BF16)) for i in range(4)]
        Osb = [st.enter_context(nc.sbuf_tensor("Osb%d" % i, [128, 512], F32)) for i in range(2)]
        On = [st.enter_context(nc.sbuf_tensor("On%d" % i, [64, 512], BF16)) for i in range(2)]
        LA = 2
        pS = [st.enter_context(nc.psum_tensor("pS%d" % i, [128, 2, 512], F32)) for i in range(3)]
        pO0 = st.enter_context(nc.psum_tensor("pO0", [128, 512], F32))
        pO = [pO0, pO0]
        pB4 = st.enter_context(nc.psum_tensor("pB", [128, 4, 128], F32))
        pB = pB4[:].rearrange("p u c -> p (u c)")
        b_kT2, b_Vb, b_pB = Buf(), Buf(), Buf()
        pw_shared.extend([pB4, b_pB])
        b_Osb, b_On = ([Buf(), Buf()] for _ in range(2))
        b_pO0 = Buf()
        b_pO = [b_pO0, b_pO0]
        b_qTp = [Buf() for _ in range(4)]
        b_pTs = [Buf() for _ in range(4)]
        b_pS = [Buf() for _ in range(3)]
        K.op(V, lambda: nc.vector.memset(Vb[:], 0.0), w=[b_Vb])
        K.op(V, lambda: nc.vector.memset(Vb[:, :, 0:260].rearrange("p t (g e) -> p t g e", e=65)[:, :, :, 64:65], 1.0), r=[b_Vb], w=[b_Vb])
        for i_ in range(4):
            K.op(V, lambda i_=i_: nc.vector.memset(qTp[i_][:], 0.0), w=[b_qTp[i_]])
        for i_ in range(2):
            K.op(V, lambda i_=i_: nc.vector.memset(Osb[i_][:], 0.0), w=[b_Osb[i_]])

        def load_batch(b):
            for gp_ in range(2):
                K.dma(S, kT2[:, gp_, :], kT_s[gp_, :, b * KEYS:(b + 1) * KEYS], r=[db("kT_s")], w=[b_kT2], acc=(gp_ > 0))
            for g in range(4):
                K.dma(S, Vb[:, :, g * 65:g * 65 + 64],
                      v_s[b * KEYS:(b + 1) * KEYS, g * 64:(g + 1) * 64].rearrange("(t p) e -> p t e", p=128),
                      r=[db("v_s")], w=[b_Vb], acc=(g > 0))

        def load_q(b, hp):
            for h2 in range(2):
                h = hp * 2 + h2
                half = (h // 4) % 2
                qi = (hp % 2) * 2 + h2
                oh = 1 - half
                K.op(V, lambda qi=qi, oh=oh: nc.vector.memset(qTp[qi][oh * 64:(oh + 1) * 64, :], 0.0), w=[b_qTp[qi]])
                K.dma(S, qTp[qi][half * 64:(half + 1) * 64, :], qT_s[hp, h2 * 64:(h2 + 1) * 64, b * SEQ:(b + 1) * SEQ],
                      r=[db("qT_s")], w=[b_qTp[qi]], acc=True)

        jobs = []
        for b in range(NB):
            for hp in range(8):
                for h2 in range(2):
                    for qb in range(4):
                        for ktp in range(9):
                            jobs.append((b, hp, h2, qb, ktp))
        nj = len(jobs)
        pend = []

        def emit_front(idx):
            b, hp, h2, qb, ktp = jobs[idx]
            if hp == 0 and h2 == 0 and qb == 0 and ktp == 0:
                load_batch(b)
                load_q(b, 0)
            if h2 == 0 and qb == 0 and ktp == 0:
                if hp + 1 < 8:
                    load_q(b, hp + 1)
            h = hp * 2 + h2
            g = h // 4
            qi = (hp % 2) * 2 + h2
            qt, b_qt = qTp[qi], b_qTp[qi]
            ps, b_ps = pS[idx % 3], b_pS[idx % 3]
            pt, b_pt = pTs[idx % 4], b_pTs[idx % 4]

            def qk():
                inst = None
                for u in range(2):
                    kt = ktp * 2 + u
                    inst = nc.tensor.matmul(ps[:, u, :], lhsT=kT2[:, g // 2, kt * 128:(kt + 1) * 128],
                                            rhs=qt[:, qb * 512:(qb + 1) * 512], start=True, stop=True)
                return inst
            K.op(P, qk, r=[b_kT2, b_qt], w=[b_ps])
            K.op(A, lambda: nc.scalar.activation(out=pt[:], in_=ps[:], func=AF.Exp, scale=0.125), r=[b_ps], w=[b_pt])

        def emit_back(idx):
            b, hp, h2, qb, ktp = jobs[idx]
            h = hp * 2 + h2
            g = h // 4
            it = idx // 9
            pt, b_pt = pTs[idx % 4], b_pTs[idx % 4]
            po, b_po = pO[it % 2], b_pO[it % 2]

            def pv():
                inst = None
                for u in range(2):
                    kt = ktp * 2 + u
                    inst = nc.tensor.matmul(po[:, :], lhsT=Vb[:, kt, g * 65:g * 65 + 128], rhs=pt[:, u, :], start=(kt == 0), stop=(kt == 17))
                return inst
            K.op(P, pv, r=[b_Vb, b_pt], w=[b_po], acc=(ktp > 0))
            if ktp == 8:
                osb, b_osb = Osb[it % 2], b_Osb[it % 2]
                on, b_on = On[it % 2], b_On[it % 2]
                K.op(V, lambda: nc.vector.tensor_copy(out=osb[0:64, :], in_=po[0:64, :]), r=[b_po], w=[b_osb])
                if K.co.finished:
                    K.op(V, lambda: nc.vector.reciprocal(out=osb[64:65, :], in_=po[64:65, :]), r=[b_po], w=[b_osb], acc=True)
                else:
                    K.op(A, lambda: nc.scalar.activation(out=osb[64:65, :], in_=po[64:65, :], func=AF.Ln), r=[b_po], w=[b_osb], acc=True)
                    K.op(A, lambda: nc.scalar.activation(out=osb[64:65, :], in_=osb[64:65, :], func=AF.Exp, scale=-1.0), r=[b_osb], w=[b_osb], acc=True)

                ohi, olo, b_ohl = Ohi[it % 2], Olo[it % 2], b_Ohl[it % 2]
                K.op(V, lambda: nc.vector.tensor_copy(out=ohi[64:65, :], in_=osb[64:65, :]), r=[b_osb], w=[b_ohl])
                K.op(V, lambda: nc.vector.tensor_tensor(out=olo[64:65, :], in0=osb[64:65, :], in1=ohi[64:65, :], op=ALU.subtract),
                     r=[b_osb, b_ohl], w=[b_ohl], acc=True)

                def fin2():
                    def em():
                        nc.tensor.matmul(pB, lhsT=esel[:, :], rhs=ohi[:, :], start=True, stop=False)
                        return nc.tensor.matmul(pB, lhsT=esel[:, :], rhs=olo[:, :], start=False, stop=True)
                    K.op(P, em, r=[b_esel, b_ohl], w=[b_pB])
                    K.op(V, lambda: nc.vector.tensor_tensor(out=on[:], in0=osb[0:64, :], in1=pB4[0:64, :, :].rearrange("p u c -> p (u c)"), op=ALU.mult),
                         r=[b_osb, b_pB], w=[b_on])
                    K.dma(S, OT_s[h * 64:(h + 1) * 64, b * SEQ + qb * 512:b * SEQ + (qb + 1) * 512], on[:],
                          r=[b_on], w=[db("OT_s")], acc=True)
                pend.append((idx + 7, fin2))

        nback = 0
        for idx in range(nj + LA):
            newb = idx < nj and idx >= 1 and jobs[idx][1:] == (0, 0, 0, 0)
            if newb:
                while nback < idx:
                    emit_back(nback)
                    nback += 1
            if idx < nj:
                emit_front(idx)
            if idx >= LA and nback <= idx - LA:
                emit_back(nback)
                nback += 1
            while pend and pend[0][0] <= idx:
                pend.pop(0)[1]()
            K.co.step(1)
        while pend:
            pend.pop(0)[1]()
        K.co.finish()
        K.barrier()
        prep_st.close()
    if debug == 3:
        return nc

    ygT_s = dscr("ygT_s", [512, NL], BF16)
    with ExitStack() as st:
        WD = st.enter_context(nc.sbuf_tensor("WD", [128, 64, 128], BF16))
        WD2 = st.enter_context(nc.sbuf_tensor("WD2", [128, 64, 128], BF16))
        CWa = st.enter_context(nc.sbuf_tensor("CWa", [128, 64, 128], BF16))
        CWb = st.enter_context(nc.sbuf_tensor("CWb", [128, 64, 128], BF16))
        Toep = st.enter_context(nc.sbuf_tensor("Toep", [128, 32, 128], BF16))
        RT = st.enter_context(nc.sbuf_tensor("RT", [128, 2, 64], F32))
        Rcol = RT[:, 0, :]
        TH8 = RT[:, 1, :]
        iot = st.enter_context(nc.sbuf_tensor("iot", [128, 2, 288], F32))
        b_w = Buf("ssm_w")
        sc_y = st.enter_context(nc.sbuf_tensor("sc_y", [128, 576], F32))
        sc_ki = st.enter_context(nc.sbuf_tensor("sc_ki", [128, 576], I32))
        sc_kf = st.enter_context(nc.sbuf_tensor("sc_kf", [128, 576], F32))
        sc_r = st.enter_context(nc.sbuf_tensor("sc_r", [128, 576], F32))
        K.dma(S, iot[:], iota_d, w=[b_w], acc=True)
        for t_, d_, nm_ in ((WD, wd_s, "WD_s"), (WD2, wd2_s, "WD2_s"), (CWa, cwa_s, "cwa_s"), (CWb, cwb_s, "cwb_s")):
            for hh in range(2):
                K.dma(S if hh == 0 else A, t_[:, hh * 32:(hh + 1) * 32, :], d_[:, hh * 32:(hh + 1) * 32, :], r=[db(nm_)], w=[b_w], acc=True)
        K.dma(S, Toep[:], toep_s, r=[db("toep_s")], w=[b_w], acc=True)
        K.dma(S, RT[:], rc_s, r=[db("rc_s")], w=[b_w], acc=True)
        if debug == 4:
            for nm, t_ in (("WD", WD), ("WD2", WD2), ("CWa", CWa), ("CWb", CWb)):
                dd = nc.dram_tensor("dbg_" + nm, [128, 64, 128], BF16, kind="ExternalOutput").ap()
                K.dma(S, dd, t_[:], r=[b_w], w=[db("dbg_" + nm)])
            dd = nc.dram_tensor("dbg_Toep", [128, 32, 128], BF16, kind="ExternalOutput").ap()
            K.dma(S, dd, Toep[:], r=[b_w], w=[db("dbg_Toep")])
            dd = nc.dram_tensor("dbg_R", [128, 2, 64], F32, kind="ExternalOutput").ap()
            K.dma(S, dd, RT[:], r=[b_w], w=[db("dbg_R")], acc=True)
            K.barrier()
            return nc

        Ug = st.enter_context(nc.sbuf_tensor("Ug", [128, NB, 2, 4096], BF16))
        Ucg = st.enter_context(nc.sbuf_tensor("Ucg", [128, NB, 4096], BF16))
        b_U = Buf("U")
        K.op(V, lambda: nc.vector.memset(Ucg[:], 0.0), w=[b_U])
        with ExitStack() as su:
            Utmp = [su.enter_context(nc.sbuf_tensor("Utmp%d" % i, [128, 4096], BF16)) for i in range(2)]
            b_Utmp = [Buf(), Buf()]
            ui = 0
            for b in range(NB):
                for cbk in range(3):
                    tmp, b_tmp = Utmp[ui % 2], b_Utmp[ui % 2]
                    ui += 1
                    if cbk < 2:
                        K.dma(S, tmp[:], ul_s[b, cbk * 128:(cbk + 1) * 128].rearrange("p i n -> p (i n)"), r=[db("ul_s")], w=[b_tmp])
                        K.op(V, lambda: nc.vector.tensor_copy(out=Ug[:, b, cbk, :].rearrange("p (g i h) -> p g i h", g=32, i=8),
                                                              in_=tmp[:].rearrange("p (i g h) -> p g i h", i=8, g=32)),
                             r=[b_tmp], w=[b_U], acc=True)
                    else:
                        K.dma(S, tmp[0:32, :], uc_s[b].rearrange("c i n -> c (i n)"), r=[db("uc_s")], w=[b_tmp])
                        K.op(A, lambda: nc.scalar.copy(out=Ucg[0:32, b, :].rearrange("p (g i h) -> p g i h", g=32, i=8),
                                                       in_=tmp[0:32, :].rearrange("p (i g h) -> p g i h", i=8, g=32)),
                             r=[b_tmp], w=[b_U], acc=True)
            K.barrier()
        UText = [st.enter_context(nc.sbuf_tensor("UText%d" % i, [128, NB, 320], BF16)) for i in range(2)]
        tabCS = [st.enter_context(nc.sbuf_tensor("tabCS%d" % i, [128, 2, 2, 288], F32)) for i in range(2)]
        tabC = [t_[:, :, 0, :] for t_ in tabCS]
        tabS = [t_[:, :, 1, :] for t_ in tabCS]
        TH8s = st.enter_context(nc.sbuf_tensor("TH8s", [128, 64], F32))
        Wt = [st.enter_context(nc.sbuf_tensor("Wt%d" % i, [128, 288], F32)) for i in range(2)]
        Wt2 = [st.enter_context(nc.sbuf_tensor("Wt2%d" % i, [128, 288], F32)) for i in range(2)]
        Zt = [st.enter_context(nc.sbuf_tensor("Zt%d" % i, [128, 288], F32)) for i in range(2)]
        ZCS = [[[st.enter_context(nc.sbuf_tensor("ZCS%d%d%d" % (i, b, d_), [128, 2, 288], BF16)) for d_ in range(2)] for b in range(NB)] for i in range(2)]
        ZC = [[[ZCS[i][b][d_][:, 0, :] for d_ in range(2)] for b in range(NB)] for i in range(2)]
        ZS = [[[ZCS[i][b][d_][:, 1, :] for d_ in range(2)] for b in range(NB)] for i in range(2)]
        y_sb = st.enter_context(nc.sbuf_tensor("y_sb", [128, 4, 8, 128], F32))
        gx = [st.enter_context(nc.sbuf_tensor("gx%d" % i, [128, 512], F32)) for i in range(2)]
        gt1 = [st.enter_context(nc.sbuf_tensor("gt1%d" % i, [128, 512], F32)) for i in range(2)]
        gsg = [st.enter_context(nc.sbuf_tensor("gsg%d" % i, [128, 512], F32)) for i in range(2)]
        ygt = [st.enter_context(nc.sbuf_tensor("ygt%d" % i, [128, 4, 128], BF16)) for i in range(2)]
        pU = st.enter_context(nc.psum_tensor("pU", [128, 320], F32))
        pD = [st.enter_context(nc.psum_tensor("pD%d" % i, [128, 288], F32)) for i in range(2)]
        pD2 = [st.enter_context(nc.psum_tensor("pD2%d" % i, [128, 288], F32)) for i in range(2)]
        py = [st.enter_context(nc.psum_tensor("py%d" % i, [128, 4, 128], F32)) for i in range(2)]
        pTy = st.enter_context(nc.psum_tensor("pTy", [128, 4, 128], F32))
        b_UText, b_tab, b_Wt, b_Wt2, b_Zt, b_pD, b_pD2, b_py, b_gx, b_gt1, b_gsg, b_ygt = ([Buf(), Buf()] for _ in range(12))
        b_ZC = [[[Buf(), Buf()] for _ in range(NB)] for _ in range(2)]
        b_ZS = [[[Buf(), Buf()] for _ in range(NB)] for _ in range(2)]
        b_pU, b_ysb, b_pTy, b_ths = Buf(), Buf(), Buf(), Buf()
        GELU_S = 2.0 * math.sqrt(2.0 / math.pi)
        K.op(V, lambda: nc.vector.tensor_scalar(out=TH8s[:], in0=TH8, scalar1=1.0 / TWO_PI, scalar2=None, op0=ALU.mult),
             r=[b_w], w=[b_ths])
        SIN_SC = TWO_PI * (1.0 - 2e-7)
        hpi = st.enter_context(nc.sbuf_tensor("hpi", [128, 1], F32))
        b_hpi = Buf()
        K.op(V, lambda: nc.vector.memset(hpi[:], math.pi / 2 * (1.0 - 2e-7)), w=[b_hpi])
        one_c = st.enter_context(nc.sbuf_tensor("one_c", [128, 1], F32))
        K.op(V, lambda: nc.vector.memset(one_c[:], 1.0), w=[b_hpi], acc=True)

        def tables(g):
            gp = g % 2
            y, ki, kf, r_ = sc_y[:, 0:576], sc_ki[:, 0:576], sc_kf[:, 0:576], sc_r[:, 0:576]
            bs = [b_prep]
            for d_ in range(2):
                T = g * 2 + d_
                K.op(V, lambda d_=d_, T=T: nc.vector.tensor_scalar(out=sc_y[:, d_ * 288:(d_ + 1) * 288], in0=iot[:, d_, :],
                                                                 scalar1=TH8s[:, T:T + 1], scalar2=None, op0=ALU.mult),
                     r=[b_w, b_ths, b_prep], w=[b_prep], acc=(d_ > 0))
            K.op(A, lambda: nc.scalar.copy(out=ki, in_=y), r=bs, w=bs)
            K.op(A, lambda: nc.scalar.copy(out=kf, in_=ki), r=bs, w=bs)
            K.op(V, lambda: nc.vector.tensor_tensor(out=r_, in0=y, in1=kf, op=ALU.subtract), r=bs, w=bs)
            K.op(A, lambda: nc.scalar.activation(out=tabS[gp], in_=r_.rearrange("p (d n) -> p d n", d=2), func=AF.Sin, scale=SIN_SC),
                 r=bs, w=[b_tab[gp]])
            K.op(A, lambda: nc.scalar.activation(out=kf, in_=r_, func=AF.Abs), r=bs, w=bs)
            K.op(A, lambda: nc.scalar.activation(out=tabC[gp], in_=kf.rearrange("p (d n) -> p d n", d=2), func=AF.Sin, scale=-SIN_SC,
                                                 bias=hpi[:, 0:1]), r=bs + [b_hpi], w=[b_tab[gp]], acc=True)

        lvc = [0]

        def chain(g, b, d_, ut, b_ut):
            gp = g % 2
            T = g * 2 + d_
            off = 0 if d_ == 0 else 32
            l2 = lvc[0] % 2
            lvc[0] += 1
            K.op(P, lambda: nc.tensor.matmul(pD[l2][:, :], lhsT=WD[:, T, :], rhs=ut[:, b, off:off + 288], start=True, stop=True),
                 r=[b_w, b_ut], w=[b_pD[l2]])
            K.op(P, lambda: nc.tensor.matmul(pD2[l2][:, :], lhsT=WD2[:, T, :], rhs=ut[:, b, off:off + 288], start=True, stop=True),
                 r=[b_w, b_ut], w=[b_pD2[l2]])
            K.op(V, lambda: nc.vector.tensor_tensor(out=Wt[l2][:], in0=pD[l2][:, :], in1=tabC[gp][:, d_, :], op=ALU.mult),
                 r=[b_pD[l2], b_tab[gp]], w=[b_Wt[l2]])
            K.op(V, lambda: nc.vector.tensor_tensor(out=Wt2[l2][:], in0=pD2[l2][:, :], in1=tabS[gp][:, d_, :], op=ALU.mult),
                 r=[b_pD2[l2], b_tab[gp]], w=[b_Wt2[l2]])
            K.op(V, lambda: nc.vector.tensor_tensor(out=Wt[l2][:], in0=Wt[l2][:], in1=Wt2[l2][:], op=ALU.add),
                 r=[b_Wt[l2], b_Wt2[l2]], w=[b_Wt[l2]])
            rv = (lambda a_: a_) if d_ == 0 else (lambda a_: a_[:, ::-1])
            K.op(V, lambda: nc.vector.tensor_tensor_scan(out=rv(Zt[l2][:]), data0=Rcol[:, T:T + 1].to_broadcast([128, 288]),
                                                         data1=rv(Wt[l2][:]), initial=0.0, op0=ALU.mult, op1=ALU.add),
                 r=[b_Wt[l2], b_w], w=[b_Zt[l2]])
            K.op(V, lambda: nc.vector.tensor_tensor(
                out=ZCS[gp][b][d_][:], in0=Zt[l2][:].rearrange("p (o n) -> p o n", o=1).to_broadcast([128, 2, 288]),
                in1=tabCS[gp][:, d_, :, :], op=ALU.mult),
                r=[b_Zt[l2], b_tab[gp]], w=[b_ZC[gp][b][d_], b_ZS[gp][b][d_]])

        def ymm_group(g):
            gp = g % 2
            ut, b_ut = UText[gp], b_UText[gp]
            pyt, b_pyt = py[gp], b_py[gp]
            for b in range(NB):
                def ymm():
                    inst = None
                    for cbk in range(2):
                        o_ = pyt[:, b * 2 + cbk, :]
                        nc.tensor.matmul(o_, lhsT=ut[:, b, 32 + cbk * 128:32 + (cbk + 1) * 128], rhs=Toep[:, g, :], start=True, stop=False)
                        for d_ in range(2):
                            T = g * 2 + d_
                            o2 = (31 if d_ == 0 else 1) + cbk * 128
                            nc.tensor.matmul(o_, lhsT=ZC[gp][b][d_][:, o2:o2 + 128], rhs=CWa[:, T, :], start=False, stop=False)
                            inst = nc.tensor.matmul(o_, lhsT=ZS[gp][b][d_][:, o2:o2 + 128], rhs=CWb[:, T, :], start=False, stop=(d_ == 1))
                    return inst
                K.op(P, ymm, r=[b_ut, b_w, b_ZC[gp][b][0], b_ZC[gp][b][1], b_ZS[gp][b][0], b_ZS[gp][b][1]], w=[b_pyt], acc=(b > 0))
            K.op(A, lambda: nc.scalar.copy(
                out=y_sb[:].rearrange("p q i (l h) -> p q i l h", h=16)[:, :, :, g % 8, :],
                in_=pyt[:].rearrange("p q (i h) -> p q i h", h=16)), r=[b_pyt], w=[b_ysb], acc=(g % 8 > 0))

        gcnt = [0]

        def gelu_out(gt_):
            for b in range(NB):
                for cbk in range(2):
                    for ih in range(2):
                        q = gcnt[0] % 2
                        gcnt[0] += 1

                        def try_():
                            inst = None
                            for u in range(4):
                                i_ = ih * 4 + u
                                inst = nc.tensor.transpose(out=pTy[:, u, :], in_=y_sb[:, b * 2 + cbk, i_, :], identity=ident_f[:])
                            return inst
                        K.op(P, try_, r=[b_ysb, b_identf], w=[b_pTy])
                        pf = pTy[:].rearrange("p u c -> p (u c)")
                        K.op(A, lambda: nc.scalar.copy(out=gx[q][:], in_=pf), r=[b_pTy], w=[b_gx[q]])
                        K.op(A, lambda: nc.scalar.activation(out=gt1[q][:], in_=pf, func=AF.Square), r=[b_pTy], w=[b_gt1[q]])
                        K.op(A, lambda: nc.scalar.activation(out=gt1[q][:], in_=gt1[q][:], func=AF.Identity, scale=0.044715, bias=one_c[:, 0:1]),
                             r=[b_gt1[q], b_hpi], w=[b_gt1[q]])
                        K.op(V, lambda: nc.vector.tensor_tensor(out=gt1[q][:], in0=gt1[q][:], in1=gx[q][:], op=ALU.mult),
                             r=[b_gt1[q], b_gx[q]], w=[b_gt1[q]])
                        K.op(A, lambda: nc.scalar.activation(out=gsg[q][:], in_=gt1[q][:], func=AF.Sigmoid, scale=GELU_S),
                             r=[b_gt1[q]], w=[b_gsg[q]])
                        K.op(V, lambda: nc.vector.tensor_tensor(out=ygt[q][:].rearrange("p u c -> p (u c)"), in0=gx[q][:], in1=gsg[q][:],
                                                                op=ALU.mult), r=[b_gx[q], b_gsg[q]], w=[b_ygt[q]])
                        c0 = b * SEQ + ih * 4 * 256
                        dst = ygT_s[gt_ * 128:(gt_ + 1) * 128, c0:c0 + 4 * 256].rearrange(
                            "p (u c) -> p u c", c=256)[:, :, cbk * 128:(cbk + 1) * 128]
                        K.dma(S, dst, ygt[q][:], r=[b_ygt[q]], w=[db("ygT_s")], acc=True)

        tables(0)
        for g in range(32):
            gp = g % 2
            ut, b_ut = UText[gp], b_UText[gp]
            for b in range(NB):
                def mkU():
                    nc.tensor.matmul(pU[:, 0:32], lhsT=Ucg[:, b, g * 128:(g + 1) * 128], rhs=ident_b[:, 0:32], start=True, stop=True)
                    inst = None
                    for cbk in range(2):
                        inst = nc.tensor.matmul(pU[:, 32 + cbk * 128:32 + (cbk + 1) * 128], lhsT=Ug[:, b, cbk, g * 128:(g + 1) * 128],
                                                rhs=ident_b[:, :], start=True, stop=True)
                    return inst
                K.op(P, mkU, r=[b_U, b_identb], w=[b_pU])
                K.op(A, lambda: nc.scalar.copy(out=ut[:, b, 0:288], in_=pU[:, 0:288]), r=[b_pU], w=[b_ut], acc=(b > 0))
                K.op(A, lambda: nc.scalar.copy(out=ut[:, b, 288:320], in_=pU[:, 0:32]), r=[b_pU], w=[b_ut], acc=True)
            chain(g, 0, 0, ut, b_ut)
            chain(g, 0, 1, ut, b_ut)
            if g + 1 < 32:
                tables(g + 1)
            chain(g, 1, 0, ut, b_ut)
            chain(g, 1, 1, ut, b_ut)
            if g >= 1:
                ymm_group(g - 1)
                if (g - 1) % 8 == 7:
                    gelu_out((g - 1) // 8)
        ymm_group(31)
        gelu_out(3)
        K.barrier()
    if debug == 5:
        return nc

    x1_s = dscr("x1_s", [NB, SEQ, D], F32)

    def load_w(dst, src_rows_view, ncols, b_dst):
        for c0 in range(0, ncols, 1024):
            c1 = min(ncols, c0 + 1024)
            K.dma(G, dst[:, :, c0:c1], src_rows_view[:, :, c0:c1], w=[b_dst], acc=True)

    with ExitStack() as st:
        wglu = st.enter_context(nc.sbuf_tensor("wglu", [128, 4, 512], BF16))
        wbra = st.enter_context(nc.sbuf_tensor("wbra", [128, 8, 1024], BF16))
        wbrs = st.enter_context(nc.sbuf_tensor("wbrs", [128, 4, 1024], BF16))
        wout = st.enter_context(nc.sbuf_tensor("wout", [128, 8, 1024], BF16))
        wing = st.enter_context(nc.sbuf_tensor("wing", [128, 8, 2048], BF16))
        bglu = st.enter_context(nc.sbuf_tensor("bglu_sb", [128, 4], F32))
        b_wglu, b_winga, b_wbra, b_wings, b_wbrs, b_wout = (Buf() for _ in range(6))
        winv = win_d.rearrange("(k p) n -> p k n", p=128)
        K.dma(S, bglu[:], bglu_d, w=[b_wglu], acc=True)
        load_w(wglu, wglu_d.rearrange("(k p) n -> p k n", p=128), 512, b_wglu)
        K.dma(G, wing[:, :, 0:1024], winv[:, :, 2048:3072], w=[b_winga], acc=True)
        load_w(wbra, wbra_d.rearrange("(k p) n -> p k n", p=128), 1024, b_wbra)
        K.dma(G, wing[:, :, 1024:2048], winv[:, :, 3072:4096], w=[b_wings], acc=True)
        load_w(wbrs, wbrs_d.rearrange("(k p) n -> p k n", p=128), 1024, b_wbrs)
        load_w(wout, wout_d.rearrange("(k p) n -> p k n", p=128), 1024, b_wout)
        ygb = [st.enter_context(nc.sbuf_tensor("ygb%d" % i, [128, 4, 512], BF16)) for i in range(2)]
        otb = [st.enter_context(nc.sbuf_tensor("otb%d" % i, [128, 8, 512], BF16)) for i in range(2)]
        hb = [st.enter_context(nc.sbuf_tensor("hb%d" % i, [128, 8, 512], BF16)) for i in range(2)]
        y2T = st.enter_context(nc.sbuf_tensor("y2T", [128, 4, 512], BF16))
        mT = st.enter_context(nc.sbuf_tensor("mT", [128, 8, 512], BF16))
        m1s = st.enter_context(nc.sbuf_tensor("m1s", [128, 8, 512], F32))
        sg = [st.enter_context(nc.sbuf_tensor("sg%d" % i, [128, 512], F32)) for i in range(2)]
        m2 = [st.enter_context(nc.sbuf_tensor("m2%d" % i, [128, 512], F32)) for i in range(2)]
        xr_ = [st.enter_context(nc.sbuf_tensor("xr%d" % i, [128, D], F32)) for i in range(4)]
        xo = [st.enter_context(nc.sbuf_tensor("xo%d" % i, [128, D], F32)) for i in range(2)]
        accA = [st.enter_context(nc.psum_tensor("accA%d" % i, [128, 512], F32)) for i in range(2)]
        accB = [st.enter_context(nc.psum_tensor("accB%d" % i, [128, 512], F32)) for i in range(2)]
        px1 = [st.enter_context(nc.psum_tensor("px1%d" % i, [128, 512], F32)) for i in range(2)]
        b_ygb, b_otb, b_hb, b_sg, b_m2, b_xo, b_accA, b_accB, b_px1 = ([Buf(), Buf()] for _ in range(9))
        b_xr = [Buf() for _ in range(4)]
        b_y2T, b_mT, b_m1s = Buf(), Buf(), Buf()
        tix = 0
        stp = 0

        def load_blk(blk):
            q = blk % 2
            c0 = blk * 512
            K.dma(A, ygb[q][:], ygT_s.rearrange("(k p) n -> p k n", p=128)[:, :, c0:c0 + 512], r=[db("ygT_s")], w=[b_ygb[q]])
            K.dma(A, otb[q][:], OT_s.rearrange("(k p) n -> p k n", p=128)[:, :, c0:c0 + 512], r=[db("OT_s")], w=[b_otb[q]])
            K.dma(A, hb[q][:], hT_s.rearrange("(k p) n -> p k n", p=128)[:, :, c0:c0 + 512], r=[db("hT_s")], w=[b_hb[q]])

        def load_xr(blk):
            b = blk // 4
            c0 = blk * 512
            for tt_ in range(4):
                cc = (c0 % SEQ) + tt_ * 128
                i_, cb_ = cc // 256, (cc % 256) // 128
                K.dma(A, xr_[tt_][:], x_d[b].rearrange("(c i) d -> i c d", i=8)[i_, cb_ * 128:(cb_ + 1) * 128, :], w=[b_xr[tt_]])

        def acc_mm(dst, wt, nk, c0_, rhs_t):
            def f():
                inst = None
                for k in range(nk):
                    inst = nc.tensor.matmul(dst[:, :], lhsT=wt[:, k, c0_:c0_ + 128], rhs=rhs_t[:, k, :], start=(k == 0), stop=(k == nk - 1))
                return inst
            return f
        load_blk(0)
        for blk in range(NL // 512):
            b = blk // 4
            q = blk % 2
            c0 = blk * 512
            if blk + 1 < NL // 512:
                load_blk(blk + 1)
            load_xr(blk)
            for nt in range(4):
                w2 = stp % 2
                stp += 1
                K.op(P, acc_mm(accA[w2], wglu, 4, nt * 128, ygb[q]), r=[b_wglu, b_ygb[q]], w=[b_accA[w2]])
                K.op(A, lambda: nc.scalar.activation(out=sg[w2][:], in_=accA[w2][:, :], func=AF.Sigmoid, bias=bglu[:, nt:nt + 1]),
                     r=[b_accA[w2], b_wglu], w=[b_sg[w2]])
                K.op(V, lambda: nc.vector.tensor_tensor(out=y2T[:, nt, :], in0=ygb[q][:, nt, :], in1=sg[w2][:], op=ALU.mult),
                     r=[b_ygb[q], b_sg[w2]], w=[b_y2T], acc=(nt > 0))
            for nt in range(8):
                w2 = stp % 2
                stp += 1
                K.op(P, acc_mm(accA[w2], wing, 8, nt * 128, hb[q]), r=[b_winga, b_hb[q]], w=[b_accA[w2]])
                K.op(P, acc_mm(accB[w2], wbra, 8, nt * 128, otb[q]), r=[b_wbra, b_otb[q]], w=[b_accB[w2]])
                K.op(A, lambda: nc.scalar.activation(out=sg[w2][:], in_=accA[w2][:, :], func=AF.Sigmoid), r=[b_accA[w2]], w=[b_sg[w2]])
                K.op(V, lambda: nc.vector.tensor_tensor(out=m1s[:, nt, :], in0=sg[w2][:], in1=accB[w2][:, :], op=ALU.mult),
                     r=[b_sg[w2], b_accB[w2]], w=[b_m1s], acc=(nt > 0))
            for nt in range(8):
                w2 = stp % 2
                stp += 1
                K.op(P, acc_mm(accA[w2], wing, 8, 1024 + nt * 128, hb[q]), r=[b_wings, b_hb[q]], w=[b_accA[w2]])
                K.op(P, acc_mm(accB[w2], wbrs, 4, nt * 128, y2T), r=[b_wbrs, b_y2T], w=[b_accB[w2]])
                K.op(A, lambda: nc.scalar.activation(out=sg[w2][:], in_=accA[w2][:, :], func=AF.Sigmoid), r=[b_accA[w2]], w=[b_sg[w2]])
                K.op(V, lambda: nc.vector.tensor_tensor(out=m2[w2][:], in0=sg[w2][:], in1=accB[w2][:, :], op=ALU.mult),
                     r=[b_sg[w2], b_accB[w2]], w=[b_m2[w2]])
                K.op(V, lambda: nc.vector.tensor_tensor(out=mT[:, nt, :], in0=m1s[:, nt, :], in1=m2[w2][:], op=ALU.add),
                     r=[b_m1s, b_m2[w2]], w=[b_mT], acc=(nt > 0))
            for tt_ in range(4):
                cc = (c0 % SEQ) + tt_ * 128
                i_, cb_ = cc // 256, (cc % 256) // 128
                w2 = tix % 2
                tix += 1
                rows = lambda dten: dten[b].rearrange("(c i) d -> i c d", i=8)[i_, cb_ * 128:(cb_ + 1) * 128, :]
                for hf in range(2):
                    def mmo():
                        inst = None
                        for k in range(8):
                            inst = nc.tensor.matmul(px1[hf][:, :], lhsT=mT[:, k, tt_ * 128:(tt_ + 1) * 128],
                                                    rhs=wout[:, k, hf * 512:(hf + 1) * 512], start=(k == 0), stop=(k == 7))
                        return inst
                    K.op(P, mmo, r=[b_wout, b_mT], w=[b_px1[hf]])
                    hs_ = slice(hf * 512, (hf + 1) * 512)
                    K.op(V, lambda: nc.vector.tensor_tensor(out=xo[w2][:, hs_], in0=px1[hf][:, :], in1=Gb[:, 0, b, hs_], op=ALU.mult),
                         r=[b_px1[hf], b_Gb], w=[b_xo[w2]], acc=(hf > 0))
                K.op(V, lambda: nc.vector.tensor_tensor(out=xo[w2][:], in0=xo[w2][:], in1=xr_[tt_][:], op=ALU.add),
                     r=[b_xo[w2], b_xr[tt_]], w=[b_xo[w2]])
                K.dma(S, rows(x1_s), xo[w2][:], r=[b_xo[w2]], w=[db("x1_s")], acc=True)
        K.barrier()
    if debug == 6:
        return nc

    h2T_s = dscr("h2T_s", [D, NL], BF16)
    part_s = dscr("part_s", [NL, D], F32)
    with ExitStack() as st:
        w1h = [st.enter_context(nc.sbuf_tensor("w1h%d" % i, [128, 8, 2048], BF16)) for i in range(2)]
        w2h0 = st.enter_context(nc.sbuf_tensor("w2h0", [128, 16, 1024], BF16))
        w2h = [w2h0, w2h0]
        b_w1 = [[Buf() for _ in range(4)] for _ in range(2)]
        b_w2_0 = Buf()
        b_w2 = [b_w2_0, b_w2_0]
        b1c = st.enter_context(nc.sbuf_tensor("b1c_sb", [128, 32], F32))
        b2r = st.enter_context(nc.sbuf_tensor("b2r_sb", [1, D], F32))
        ones1 = st.enter_context(nc.sbuf_tensor("ones1", [1, 128], F32))
        b_cst = Buf()
        K.dma(S, b1c[:], b1c_d, w=[b_cst], acc=True)
        K.dma(S, b2r[:], b2r_d, w=[b_cst], acc=True)
        K.op(V, lambda: nc.vector.memset(ones1[:], 1.0), w=[b_cst], acc=True)
        h2b = [st.enter_context(nc.sbuf_tensor("h2b%d" % i, [128, 8, 512], BF16)) for i in range(2)]
        hid = st.enter_context(nc.sbuf_tensor("hid", [128, 16, 512], BF16))
        rl = [st.enter_context(nc.sbuf_tensor("rl%d" % i, [128, 512], F32)) for i in range(2)]
        ph = [st.enter_context(nc.psum_tensor("ph%d" % i, [128, 512], F32)) for i in range(2)]
        po = [st.enter_context(nc.psum_tensor("po%d" % i, [128, D], F32)) for i in range(2)]
        b_h2b, b_rl, b_ph, b_po = ([Buf(), Buf()] for _ in range(4))
        b_hid = Buf()
        env = None
        pa_ = None
        xa = [st.enter_context(nc.sbuf_tensor("xa%d" % i, [128, D], F32)) for i in range(2)] * 2
        pb_ = [st.enter_context(nc.sbuf_tensor("pb%d" % i, [128, D], F32)) for i in range(2)] * 2
        b_xa = [Buf(), Buf()] * 2
        b_pa = [Buf() for _ in range(4)]
        w1v = w1_d.rearrange("(k p) n -> p k n", p=128)
        jn = 0
        jn_box = [0]
        tix = 0

        def load_x1(n_):
            b_ = (n_ * 128) // SEQ
            cc = (n_ * 128) % SEQ
            i_, cb_ = cc // 256, (cc % 256) // 128
            K.dma(A, env["xt"][n_ % 3][:], x1_s[b_].rearrange("(c i) d -> i c d", i=8)[i_, cb_ * 128:(cb_ + 1) * 128, :],
                  r=[db("x1_s")], w=[env["b_xt"][n_ % 3]])

        def load_h2(blk_):
            K.dma(A, h2b[blk_ % 2][:], h2T_s.rearrange("(k p) n -> p k n", p=128)[:, :, blk_ * 512:(blk_ + 1) * 512],
                  r=[db("h2T_s")], w=[b_h2b[blk_ % 2]])
        def load_mlp_w1(hf):
            for cblk in range(4):
                K.dma(G, w1h[hf][:, :, cblk * 512:(cblk + 1) * 512], w1v[:, :, hf * 2048 + cblk * 512:hf * 2048 + (cblk + 1) * 512],
                      w=[b_w1[hf][cblk]], acc=True)

        def load_mlp_w2(hf):
            w2v = w2_d[hf * 2048:(hf + 1) * 2048, :].rearrange("(k p) n -> p k n", p=128)
            for kh in range(2):
                K.dma(G, w2h[hf][:, kh * 8:(kh + 1) * 8, :], w2v[:, kh * 8:(kh + 1) * 8, :], w=[b_w2[hf]], acc=(kh > 0))
        load_mlp_w1(0)
        load_mlp_w2(0)
        for hf in range(2):
            hst = ExitStack()
            if hf == 0:
                env = mk_norm_env(hst)
                env["xn_on_dve"] = True
                pa_ = [hst.enter_context(nc.sbuf_tensor("pa0", [128, D], F32))] * 4
                b_pa = [Buf()] * 4
                load_x1(0)
                load_x1(1)
                load_mlp_w1(1)
            else:
                load_mlp_w2(1)
                pa_ = pb_
                b_pa = [Buf(), Buf()] * 2
            for blk in range(NL // 512):
                b = blk // 4
                q = blk % 2
                c0 = blk * 512
                tiles_ = []
                for tt_ in range(4):
                    cc = (c0 % SEQ) + tt_ * 128
                    tiles_.append((cc // 256, (cc % 256) // 128))
                rows = lambda dten, i_, cb_: dten[b].rearrange("(c i) d -> i c d", i=8)[i_, cb_ * 128:(cb_ + 1) * 128, :]
                def norm_tile_1(blk_, tt2):
                    n_ = blk_ * 4 + tt2
                    if n_ + 2 < NL // 128:
                        load_x1(n_ + 2)
                    norm_T1(env, n_)

                def norm_tile_2(blk_, tt2):
                    n_ = blk_ * 4 + tt2
                    norm_T2a(env, n_)

                def norm_tile_3(blk_, tt2):
                    n_ = blk_ * 4 + tt2
                    b_ = blk_ // 4
                    q_ = blk_ % 2
                    pT, b_pT = env["pT"], env["b_pT"]
                    for k in range(8):
                        K.op(V, lambda k=k: nc.vector.tensor_scalar(
                            out=h2b[q_][:, k, tt2 * 128:(tt2 + 1) * 128], in0=pT[:, k, :], scalar1=A2[:, k, b_:b_ + 1],
                            scalar2=modcol[:, k, 2, b_:b_ + 1], op0=ALU.mult, op1=ALU.add),
                            r=[b_pT, b_A, b_modcol], w=[b_h2b[q_]], acc=(tt2 > 0 or k > 0))
                    if tt2 == 3:
                        K.dma(S, h2T_s.rearrange("(k p) n -> p k n", p=128)[:, :, blk_ * 512:(blk_ + 1) * 512], h2b[q_][:],
                              r=[b_h2b[q_]], w=[db("h2T_s")], acc=True)

                def norm_blk(blk_):
                    for tt2 in range(4):
                        norm_tile_1(blk_, tt2)
                        norm_tile_2(blk_, tt2)
                        norm_tile_3(blk_, tt2)
                if hf == 0:
                    if blk == 0:
                        norm_blk(0)
                else:
                    if blk == 0:
                        load_h2(0)
                    if blk + 1 < NL // 512:
                        load_h2(blk + 1)
                    def load_tail(tt2):
                        i2, cb2 = tiles_[tt2]
                        K.dma(A, pa_[tt2][:], part_s[c0 + tt2 * 128:c0 + (tt2 + 1) * 128, :], r=[db("part_s")], w=[b_pa[tt2]])
                        K.dma(A, xa[tt2][:], rows(x1_s, i2, cb2), r=[db("x1_s")], w=[b_xa[tt2]])
                    load_tail(0)
                    load_tail(1)
                for nt in range(16):
                    w2 = nt % 2

                    def mmh():
                        inst = None
                        for k in range(8):
                            inst = nc.tensor.matmul(ph[w2][:, :], lhsT=w1h[hf][:, k, nt * 128:(nt + 1) * 128], rhs=h2b[q][:, k, :],
                                                    start=(k == 0), stop=(k == 7))
                        return inst
                    K.op(P, mmh, r=[b_w1[hf][nt // 4], b_h2b[q]], w=[b_ph[w2]])
                    K.op(A, lambda: nc.scalar.activation(out=rl[w2][:], in_=ph[w2][:, :], func=AF.Relu,
                                                         bias=b1c[:, hf * 16 + nt:hf * 16 + nt + 1]), r=[b_ph[w2], b_cst], w=[b_rl[w2]])
                    K.op(V, lambda: nc.vector.tensor_tensor(out=hid[:, nt, :], in0=rl[w2][:], in1=rl[w2][:], op=ALU.mult),
                         r=[b_rl[w2]], w=[b_hid], acc=(nt > 0))
                nxt_norm = hf == 0 and blk + 1 < NL // 512
                for tt_, (i_, cb_) in enumerate(tiles_):
                    w2 = tix % 2
                    tix += 1
                    prow = part_s[c0 + tt_ * 128:c0 + (tt_ + 1) * 128, :]
                    if nxt_norm:
                        norm_tile_1(blk + 1, tt_)

                    def mmo2():
                        inst = None
                        for h2_ in range(2):
                            for k in range(16):
                                inst = nc.tensor.matmul(po[w2][:, h2_ * 512:(h2_ + 1) * 512], lhsT=hid[:, k, tt_ * 128:(tt_ + 1) * 128],
                                                        rhs=w2h[hf][:, k, h2_ * 512:(h2_ + 1) * 512], start=(k == 0), stop=(k == 15 and hf == 0))
                            if hf == 1:
                                inst = nc.tensor.matmul(po[w2][:, h2_ * 512:(h2_ + 1) * 512], lhsT=ones1[0:1, :],
                                                        rhs=b2r[0:1, h2_ * 512:(h2_ + 1) * 512], start=False, stop=True)
                        return inst
                    K.op(P, mmo2, r=[b_w2[hf], b_hid, b_cst], w=[b_po[w2]])
                    if nxt_norm:
                        norm_tile_2(blk + 1, tt_)
                    if hf == 0:
                        K.op(A, lambda: nc.scalar.copy(out=pa_[tt_][:], in_=po[w2][:, :]), r=[b_po[w2]], w=[b_pa[tt_]])
                        K.dma(S, prow, pa_[tt_][:], r=[b_pa[tt_]], w=[db("part_s")], acc=True)
                        if nxt_norm:
                            norm_tile_3(blk + 1, tt_)
                    else:
                        K.op(V, lambda: nc.vector.tensor_tensor(out=pa_[tt_][:], in0=po[w2][:, :], in1=pa_[tt_][:], op=ALU.add),
                             r=[b_po[w2], b_pa[tt_]], w=[b_pa[tt_]])
                        K.op(V, lambda: nc.vector.tensor_tensor(out=pa_[tt_][:], in0=pa_[tt_][:], in1=Gb[:, 1, b, :], op=ALU.mult),
                             r=[b_pa[tt_], b_Gb], w=[b_pa[tt_]])
                        K.op(V, lambda: nc.vector.tensor_tensor(out=xa[tt_][:], in0=xa[tt_][:], in1=pa_[tt_][:], op=ALU.add),
                             r=[b_pa[tt_], b_xa[tt_]], w=[b_xa[tt_]])
                        K.dma(S, rows(out_d, i_, cb_), xa[tt_][:], r=[b_xa[tt_]], w=[db("out")], acc=True)
                        if tt_ + 2 < 4:
                            load_tail(tt_ + 2)
            if hf == 1:
                K.barrier()
            hst.close()

    K.barrier()
    return nc


def _rope_tables():
    half = 16
    inv_freq = (np.float32(10000.0) ** (-np.arange(half, dtype=np.float32) / np.float32(half))).astype(np.float32)
    C = np.zeros((128, 16, 64), np.float32)
    Sg = np.zeros((128, 16, 64), np.float32)
    p = np.arange(128)
    for i in range(8):
        for cb in range(2):
            t = 8 * (128 * cb + p) + i
            row = (t // 64).astype(np.float32)
            col = (t % 64).astype(np.float32)
            ar = row[:, None] * inv_freq[None, :]
            ac = col[:, None] * inv_freq[None, :]
            cr, sr = np.cos(ar).astype(np.float32), np.sin(ar).astype(np.float32)
            cc, sc = np.cos(ac).astype(np.float32), np.sin(ac).astype(np.float32)
            j = i * 2 + cb
            C[:, j, 0:16] = cr
            C[:, j, 16:32] = cr
            C[:, j, 32:48] = cc
            C[:, j, 48:64] = cc
            Sg[:, j, 0:16] = -sr
            Sg[:, j, 16:32] = sr
            Sg[:, j, 32:48] = -sc
            Sg[:, j, 48:64] = sc
    return C, Sg


_PARTNER = np.concatenate([np.arange(16, 32), np.arange(0, 16), np.arange(48, 64), np.arange(32, 48)])


def make_in_maps(inp):
    f = lambda a: np.ascontiguousarray(np.asarray(a, dtype=np.float32))
    x = f(inp["x"]); c = f(inp["c"]); ctx = f(inp["ctx"]); c_ctx = f(inp["c_ctx"])
    ropeC, ropeS = _rope_tables()
    col = lambda v: np.ascontiguousarray(v.reshape(8, 128).T)
    qg = f(inp["q_norm_g"])[0]
    kg = f(inp["k_norm_g"])[0]
    qg2 = np.ascontiguousarray(np.broadcast_to(np.stack([qg, qg[_PARTNER]])[None], (128, 2, 64)))
    kg2 = np.ascontiguousarray(np.broadcast_to(np.stack([kg, kg[_PARTNER]])[None], (128, 2, 64)))
    dup = lambda a: np.ascontiguousarray(np.concatenate([a, a], 0))
    tl = lambda a: dup(f(a)[0].transpose(2, 1, 0).reshape(64, 64))
    lamr, lami = tl(inp["ssm_lambda_re"]), tl(inp["ssm_lambda_im"])
    ldt = np.ascontiguousarray(np.broadcast_to(f(inp["ssm_log_dt"])[0].T.reshape(1, 64), (128, 64)))
    bt = lambda a: dup(f(a)[0].transpose(2, 1, 0, 3).reshape(64, 64, 16))
    ct = lambda a: dup(f(a)[0].transpose(3, 1, 0, 2).reshape(64, 64, 16))
    dvec = f(inp["ssm_d"])[0].reshape(32, 16)
    dcol = np.ascontiguousarray(np.broadcast_to(dvec.T[None], (8, 16, 32)).reshape(128, 32))
    ii = np.arange(128) // 16
    maskf = (ii[None, :] >= ii[:, None]).astype(np.float32)
    maskb = (ii[None, :] <= ii[:, None]).astype(np.float32)
    ar = np.arange(288, dtype=np.float32)
    iota2 = np.ascontiguousarray(np.broadcast_to(np.stack([ar, 287 - ar])[None], (128, 2, 288)))
    shared = {
        "lamr": lamr, "lami": lami, "ldt": ldt, "bre": bt(inp["ssm_b_re"]), "bim": bt(inp["ssm_b_im"]),
        "cre": ct(inp["ssm_c_re"]), "cim": ct(inp["ssm_c_im"]), "dcol": dcol, "maskf": maskf, "maskb": maskb,
        "kv": np.ascontiguousarray(np.broadcast_to(np.arange(9, dtype=np.float32)[None], (128, 9))),
        "w_glu": f(inp["w_glu"])[0], "bglu": np.ascontiguousarray(f(inp["b_glu"])[0].reshape(4, 128).T),
        "w_br_attn": f(inp["w_br_attn"])[0], "w_br_ssm": f(inp["w_br_ssm"])[0], "w_out": f(inp["w_out"])[0],
        "w_mlp1": f(inp["w_mlp1"])[0], "b1c": np.ascontiguousarray(f(inp["b_mlp1"])[0].reshape(32, 128).T),
        "w_mlp2": f(inp["w_mlp2"])[0], "b2r": f(inp["b_mlp2"])[0][None, :],
        "iota2": iota2, "j128": np.ascontiguousarray(np.eye(128, dtype=np.float32)[::-1]),
        "w_mod": f(inp["w_mod"])[0], "b_mod": f(inp["b_mod"])[0][None, :],
        "n1g": col(f(inp["norm1_g"])[0]), "n2g": col(f(inp["norm2_g"])[0]),
        "w_in": f(inp["w_in"])[0], "qg": qg2, "kg": kg2,
        "ropeC": ropeC, "ropeS": ropeS, "ident": np.eye(128, dtype=np.float32),
        "esel": np.concatenate([np.zeros((64, 128), np.float32), np.ones((1, 128), np.float32), np.zeros((63, 128), np.float32)], 0),
        "sel": np.ascontiguousarray(np.broadcast_to(np.eye(3, dtype=np.float32)[:, :NB, None], (3, NB, 128))),
    }
    maps = []
    for core in range(NCORES):
        b0 = core * NB
        cv = np.stack([c[b0], c[b0 + 1], c_ctx])
        cT = np.ascontiguousarray(cv.reshape(3, 8, 128).transpose(2, 1, 0))
        m = dict(shared)
        m["x"] = x[b0:b0 + NB]
        m["ctx"] = ctx[b0:b0 + NB]
        m["cT"] = cT
        maps.append(m)
    return maps


def kernel(**inputs):
    nc = build_nc(0)
    maps = make_in_maps(inputs)
    res = run_bass_kernel_spmd(nc, maps, core_ids=list(range(NCORES)))
    return np.concatenate([r["out"] for r in res.results], axis=0)
```

```python
import math
import threading
from contextlib import ExitStack

import numpy as np
import concourse.bass as bass
import concourse.mybir as mybir
from concourse.bass_utils import run_bass_kernel_spmd

F32 = mybir.dt.float32
BF16 = mybir.dt.bfloat16
I32 = mybir.dt.int32
AF = mybir.ActivationFunctionType
ALU = mybir.AluOpType
AX = mybir.AxisListType

D = 1024
SEQ = 2048
CTXL = 256
NB = 2
NCORES = 8
EPS = 1e-6
NL = NB * SEQ
NCX = NB * CTXL
KEYS = SEQ + CTXL


class Buf:
    __slots__ = ("name", "writes", "reads", "wbar")

    def __init__(self, name=""):
        self.name = name
        self.writes = {}
        self.reads = {}
        self.wbar = {}


class Eng:
    def __init__(self, K, name, h, nring):
        self.K = K
        self.name = name
        self.h = h
        self.sem = K.new_sem("s_" + name)
        self.count = 0
        self.seen = {}
        self.ring = [[K.new_sem("d_%s%d" % (name, i)), 0] for i in range(nring)]
        self.rpos = 0

    def wait(self, ev):
        sem, val = ev
        k = id(sem)
        if self.seen.get(k, 0) >= val:
            return
        self.h.wait_ge(sem, val)
        self.seen[k] = val


class Interleaver:
    def __init__(self, fn):
        self.go = threading.Semaphore(0)
        self.back = threading.Semaphore(0)
        self.finished = False
        self.started = False
        self.err = None
        self.t = threading.Thread(target=self._run, args=(fn,), daemon=True)

    def _run(self, fn):
        self.go.acquire()
        try:
            fn()
        except BaseException as e:
            self.err = e
        self.finished = True
        self.back.release()

    def pause(self):
        self.back.release()
        self.go.acquire()

    def step(self, n=1):
        for _ in range(n):
            if self.finished:
                break
            if not self.started:
                self.started = True
                self.t.start()
            self.go.release()
            self.back.acquire()
        if self.err is not None:
            raise self.err

    def finish(self):
        while not self.finished:
            self.step(1)
        if self.err is not None:
            raise self.err


class Kern:
    def __init__(self, nc):
        self.nc = nc
        self.co = None
        self.es = ExitStack()
        self.nsem = 0
        self.pe = Eng(self, "pe", nc.tensor, 0)
        self.act = Eng(self, "act", nc.scalar, 6)
        self.dve = Eng(self, "dve", nc.vector, 0)
        self.pool = Eng(self, "pool", nc.gpsimd, 8)
        self.sp = Eng(self, "sp", nc.sync, 12)
        self.engs = [self.pe, self.act, self.dve, self.pool, self.sp]
        self.dma_events = []

    def new_sem(self, name):
        self.nsem += 1
        return self.es.enter_context(self.nc.semaphore(name))

    def sb(self, name, shape, dt):
        return self.es.enter_context(self.nc.sbuf_tensor(name, shape, dt))

    def _deps(self, r, w, acc):
        deps = {}
        for b in r:
            for k, ev in b.writes.items():
                if deps.get(k, (None, 0))[1] < ev[1]:
                    deps[k] = ev
        for b in w:
            for k, ev in b.reads.items():
                if deps.get(k, (None, 0))[1] < ev[1]:
                    deps[k] = ev
            if not acc:
                for k, ev in b.writes.items():
                    if deps.get(k, (None, 0))[1] < ev[1]:
                        deps[k] = ev
            else:
                for k, ev in b.wbar.items():
                    if deps.get(k, (None, 0))[1] < ev[1]:
                        deps[k] = ev
        return deps

    def _record(self, ev, r, w, acc):
        k = id(ev[0])
        for b in r:
            b.reads[k] = ev
        for b in w:
            if not acc:
                wb = dict(b.reads)
                for kk, e2 in b.writes.items():
                    if wb.get(kk, (None, 0))[1] < e2[1]:
                        wb[kk] = e2
                b.wbar = wb
                b.writes = {}
                b.reads = {}
            b.writes[k] = ev

    def op(self, eng, fn, r=(), w=(), acc=False):
        for ev in self._deps(r, w, acc).values():
            eng.wait(ev)
        inst = fn()
        eng.count += 1
        inst.then_inc(eng.sem, 1)
        ev = (eng.sem, eng.count)
        eng.seen[id(eng.sem)] = max(eng.seen.get(id(eng.sem), 0), 0)
        self._record(ev, r, w, acc)
        self._maybe_pause()
        return ev

    def _maybe_pause(self):
        co = self.co
        if co is not None and co.started and not co.finished and threading.current_thread() is co.t and getattr(co, "armed", False):
            co.pause()

    def dma(self, eng, out, in_, r=(), w=(), acc=False, **kw):
        for ev in self._deps(r, w, acc).values():
            eng.wait(ev)
        slot = eng.ring[eng.rpos]
        eng.rpos = (eng.rpos + 1) % len(eng.ring)
        if slot[1] > 0:
            eng.wait((slot[0], slot[1]))
        slot[1] += 16
        eng.h.dma_start(out=out, in_=in_, **kw).then_inc(slot[0], 16)
        ev = (slot[0], slot[1])
        self._record(ev, r, w, acc)
        self.dma_events.append(ev)
        self._maybe_pause()
        return ev

    def barrier(self):
        evs = [(e.sem, e.count) for e in self.engs if e.count > 0]
        for e in self.engs:
            for s in e.ring:
                if s[1] > 0:
                    evs.append((s[0], s[1]))
        for e in self.engs:
            for ev in evs:
                if ev[0] is e.sem:
                    continue
                e.wait(ev)


def tt(eng, out, in0, in1, op):
    return eng.h.tensor_tensor(out=out, in0=in0, in1=in1, op=op)


def build_nc(debug=0):
    nc = bass.Bass("TRN2", target_bir_lowering=False)
    K = Kern(nc)
    P = K.pe
    A = K.act
    V = K.dve
    G = K.pool
    S = K.sp

    def din(name, shape, dt=F32):
        return nc.dram_tensor(name, list(shape), dt, kind="ExternalInput").ap()

    def dscr(name, shape, dt):
        kind = "ExternalOutput" if debug else "Internal"
        return nc.dram_tensor(name, list(shape), dt, kind=kind).ap()

    x_d = din("x", [NB, SEQ, D])
    ctx_d = din("ctx", [NB, CTXL, D])
    cT_d = din("cT", [128, 8, 3])
    wmod_d = din("w_mod", [D, 6 * D])
    bmod_d = din("b_mod", [1, 6 * D])
    n1g_d = din("n1g", [128, 8])
    n2g_d = din("n2g", [128, 8])
    win_d = din("w_in", [D, 4096])
    qg_d = din("qg", [128, 2, 64])
    kg_d = din("kg", [128, 2, 64])
    ropeC_d = din("ropeC", [128, 16, 64])
    ropeS_d = din("ropeS", [128, 16, 64])
    ident_d = din("ident", [128, 128])
    sel_d = din("sel", [3, NB, 128])
    esel_d = din("esel", [128, 128])
    lamr_d, lami_d, ldt_d = din("lamr", [128, 64]), din("lami", [128, 64]), din("ldt", [128, 64])
    bre_d, bim_d = din("bre", [128, 64, 16]), din("bim", [128, 64, 16])
    cre_d, cim_d = din("cre", [128, 64, 16]), din("cim", [128, 64, 16])
    dcol_d = din("dcol", [128, 32])
    maskf_d, maskb_d = din("maskf", [128, 128]), din("maskb", [128, 128])
    kv_d = din("kv", [128, 9])
    iota_d = din("iota2", [128, 2, 288])
    j128_d = din("j128", [128, 128])
    wglu_d = din("w_glu", [512, 512])
    bglu_d = din("bglu", [128, 4])
    wbra_d = din("w_br_attn", [D, D])
    wbrs_d = din("w_br_ssm", [512, D])
    wout_d = din("w_out", [D, D])
    w1_d = din("w_mlp1", [D, 4096])
    b1c_d = din("b1c", [128, 32])
    w2_d = din("w_mlp2", [4096, D])
    b2r_d = din("b2r", [1, D])
    out_d = nc.dram_tensor("out", [NB, SEQ, D], F32, kind="ExternalOutput").ap()

    hT_s = dscr("hT_s", [D, NL], BF16)
    qT_s = dscr("qT_s", [8, 128, NL], BF16)
    kT_s = dscr("kT_s", [2, 128, NB * KEYS], BF16)
    v_s = dscr("v_s", [NB * KEYS, 256], BF16)
    ul_s = dscr("ul_s", [NB, 256, 8, 512], BF16)
    uc_s = dscr("uc_s", [NB, 32, 8, 512], BF16)
    mod_s = dscr("mod_s", [3, 6 * D], F32)
    dram_bufs = {}

    def db(name):
        if name not in dram_bufs:
            dram_bufs[name] = Buf(name)
        return dram_bufs[name]

    ident_f = K.sb("ident_f", [128, 128], F32)
    ident_b = K.sb("ident_b", [128, 128], BF16)
    b_identf = Buf("identf")
    b_identb = Buf("identb")
    K.dma(S, ident_f[:], ident_d, w=[b_identf])
    K.op(V, lambda: nc.vector.tensor_copy(out=ident_b[:], in_=ident_f[:]), r=[b_identf], w=[b_identb])

    modcol = K.sb("modcol", [128, 8, 4, 3], F32)
    Gb = K.sb("Gb", [128, 2, NB, D], F32)
    b_Gb = Buf("Gb")
    A1 = K.sb("A1", [128, 8, 3], F32)
    A2 = K.sb("A2", [128, 8, 3], F32)
    b_modrows = Buf("modrows")
    b_modcol = Buf("modcol")
    b_A = Buf("A12")
    win_st = ExitStack()
    win_sb = win_st.enter_context(nc.sbuf_tensor("win_sb", [128, 8, 2048], BF16))
    b_win = Buf()
    win_v = win_d.rearrange("(k p) n -> p k n", p=128)
    for hf in range(2):
        K.dma(G, win_sb[:, :, hf * 1024:(hf + 1) * 1024], win_v[:, :, hf * 1024:(hf + 1) * 1024],
              w=[b_win], acc=True)

    with ExitStack() as st:
        modrows = st.enter_context(nc.sbuf_tensor("modrows", [3, 6 * D], F32))
        sel = st.enter_context(nc.sbuf_tensor("sel_sb", [3, NB, 128], F32))
        b_sel = Buf()
        K.dma(S, sel[:], sel_d, w=[b_sel])
        cT = st.enter_context(nc.sbuf_tensor("cT_sb", [128, 8, 3], F32))
        scT = st.enter_context(nc.sbuf_tensor("scT", [128, 8, 3], F32))
        ones13 = st.enter_context(nc.sbuf_tensor("ones13", [1, 4], F32))
        bmod = st.enter_context(nc.sbuf_tensor("bmod", [1, 6 * D], F32))
        n1g = st.enter_context(nc.sbuf_tensor("n1g_sb", [128, 8], F32))
        n2g = st.enter_context(nc.sbuf_tensor("n2g_sb", [128, 8], F32))
        wm = [st.enter_context(nc.sbuf_tensor("wm%d" % i, [128, 8, 512], F32)) for i in range(2)]
        pm = [st.enter_context(nc.psum_tensor("pm%d" % i, [128, 512], F32)) for i in range(2)]
        ptc = st.enter_context(nc.psum_tensor("ptc", [128, 32, 4], F32))
        b_cT, b_scT, b_ones, b_bmod, b_ng = Buf(), Buf(), Buf(), Buf(), Buf()
        b_wm = [Buf(), Buf()]
        b_pm = [Buf(), Buf()]
        b_ptc = Buf()
        K.dma(S, cT[:], cT_d, w=[b_cT])
        K.dma(S, bmod[:], bmod_d, w=[b_bmod])
        K.dma(S, n1g[:], n1g_d, w=[b_ng], acc=True)
        K.dma(S, n2g[:], n2g_d, w=[b_ng], acc=True)
        K.op(A, lambda: nc.scalar.activation(out=scT[:], in_=cT[:], func=AF.Silu), r=[b_cT], w=[b_scT])
        K.op(V, lambda: nc.vector.memset(ones13[:], 1.0), w=[b_ones])
        wmod_v = wmod_d.rearrange("(k p) n -> p k n", p=128)
        for blk in range(12):
            wb = wm[blk % 2]
            K.dma(S, wb[:], wmod_v[:, :, blk * 512:(blk + 1) * 512], w=[b_wm[blk % 2]])

            def mm(blk=blk, wb=wb):
                for k in range(8):
                    nc.tensor.matmul(pm[blk % 2][0:3, :], lhsT=scT[:, k, :], rhs=wb[:, k, :],
                                     start=(k == 0), stop=False)
                return nc.tensor.matmul(pm[blk % 2][0:3, :], lhsT=ones13[0:1, 0:3],
                                        rhs=bmod[0:1, blk * 512:(blk + 1) * 512],
                                        start=False, stop=True)
            K.op(P, mm, r=[b_scT, b_wm[blk % 2], b_ones, b_bmod], w=[b_pm[blk % 2]])
            K.op(V, lambda blk=blk: nc.vector.tensor_copy(out=modrows[:, blk * 512:(blk + 1) * 512],
                                                          in_=pm[blk % 2][0:3, :]),
                 r=[b_pm[blk % 2]], w=[b_modrows], acc=True)
        def tr():
            inst = None
            for j, ch in enumerate((0, 1, 3, 4)):
                for k in range(8):
                    c0 = ch * D + k * 128
                    inst = nc.tensor.transpose(out=ptc[:, j * 8 + k, 0:3], in_=modrows[0:3, c0:c0 + 128],
                                               identity=ident_f[0:3, 0:3])
            return inst
        K.op(P, tr, r=[b_modrows, b_identf], w=[b_ptc])
        K.op(V, lambda: nc.vector.tensor_copy(
            out=modcol[:].rearrange("p k j r -> p j k r"),
            in_=ptc[:, :, 0:3].rearrange("p (j k) r -> p j k r", j=4)), r=[b_ptc], w=[b_modcol])
        for (Ax, ng, j) in ((A1, n1g, 1), (A2, n2g, 3)):
            K.op(V, lambda Ax=Ax, j=j: nc.vector.tensor_scalar(
                out=Ax[:], in0=modcol[:, :, j, :], scalar1=1.0, scalar2=None, op0=ALU.add),
                r=[b_modcol], w=[b_A], acc=True)
            K.op(V, lambda Ax=Ax, ng=ng: nc.vector.tensor_tensor(
                out=Ax[:], in0=Ax[:], in1=ng[:].rearrange("p (k o) -> p k o", o=1).to_broadcast([128, 8, 3]),
                op=ALU.mult), r=[b_A, b_ng], w=[b_A])
        gi = 0
        for gj, ch in enumerate((2, 5)):
            for b in range(NB):
                for hf in range(2):
                    c0 = ch * D + hf * 512
                    pp = pm[gi % 2]
                    K.op(P, lambda pp=pp, b=b, c0=c0: nc.tensor.matmul(
                        pp[:, :], lhsT=sel[0:3, b, :], rhs=modrows[0:3, c0:c0 + 512], start=True, stop=True),
                        r=[b_sel, b_modrows], w=[b_pm[gi % 2]])
                    K.op(V, lambda pp=pp, gj=gj, b=b, hf=hf: nc.vector.tensor_copy(
                        out=Gb[:, gj, b, hf * 512:(hf + 1) * 512], in_=pp[:, :]),
                        r=[b_pm[gi % 2]], w=[b_Gb], acc=True)
                    gi += 1
        if debug:
            K.dma(S, mod_s, modrows[:], r=[b_modrows], w=[db("mod_s")])
        K.barrier()

    if debug == 1:
        fin = K.sb("fin", [128, 8, 6], F32)
        b_fin = Buf()
        K.op(V, lambda: nc.vector.tensor_copy(out=fin[:, :, 0:3], in_=A1[:]), r=[b_A], w=[b_fin])
        K.op(V, lambda: nc.vector.tensor_copy(out=fin[:, :, 3:6], in_=A2[:]), r=[b_A, b_fin], w=[b_fin])
        dbg = nc.dram_tensor("dbg", [128, 8, 6], F32, kind="ExternalOutput").ap()
        K.dma(S, dbg, fin[:], r=[b_fin], w=[db("dbg")])
        K.barrier()
        return nc


    def norm_T1(env, j):
        xt, b_xt = env["xt"][j % 3], env["b_xt"][j % 3]
        xn, b_xn = env["xn"][j % 3], env["b_xn"][j % 3]
        st_, b_st = env["stat"][j % 3], env["b_stat"][j % 3]
        junk, b_junk = env["junk"], env["b_junk"]
        K.op(A, lambda: nc.scalar.activation(out=junk[:], in_=xt[:], func=AF.Square, accum_out=st_[:, 0:1]),
             r=[b_xt], w=[b_junk, b_st])
        K.op(A, lambda: nc.scalar.activation(out=st_[:, 1:2], in_=st_[:, 0:1], func=AF.Sqrt,
                                             scale=1.0 / D, bias=env["eps"][:, 0:1]), r=[b_st, env["b_eps"]], w=[b_st])
        K.op(V, lambda: nc.vector.reciprocal(out=st_[:, 2:3], in_=st_[:, 1:2]), r=[b_st], w=[b_st])
        if env.get("xn_on_dve"):
            K.op(V, lambda: nc.vector.tensor_scalar(out=xn[:], in0=xt[:], scalar1=st_[:, 2:3], scalar2=None, op0=ALU.mult),
                 r=[b_xt, b_st], w=[b_xn])
        else:
            K.op(A, lambda: nc.scalar.activation(out=xn[:], in_=xt[:], func=AF.Copy, scale=st_[:, 2:3]),
                 r=[b_xt, b_st], w=[b_xn])

    def norm_T2a(env, j):
        xn, b_xn = env["xn"][j % 3], env["b_xn"][j % 3]
        pT, b_pT = env["pT"], env["b_pT"]

        def tr():
            inst = None
            for k in range(8):
                inst = nc.tensor.transpose(out=pT[:, k, :], in_=xn[:, k * 128:(k + 1) * 128], identity=ident_b[:])
            return inst
        K.op(P, tr, r=[b_xn, b_identb], w=[b_pT])

    def norm_T2b(env, j, r, Acol, Bcol_j):
        hTt, b_hTt = env["hTt"][j % 3], env["b_hTt"][j % 3]
        pT, b_pT = env["pT"], env["b_pT"]
        for k in range(8):
            K.op(A, lambda k=k: nc.scalar.activation(
                out=hTt[:, k, :], in_=pT[:, k, :], func=AF.Identity, scale=Acol[:, k, r:r + 1],
                bias=modcol[:, k, Bcol_j, r:r + 1]),
                r=[b_pT, b_A, b_modcol], w=[b_hTt], acc=(k > 0))
        return hTt, b_hTt

    def norm_T(env, j, r, Acol, Bcol_j):
        norm_T1(env, j)
        norm_T2a(env, j)
        return norm_T2b(env, j, r, Acol, Bcol_j)

    envc = [0]

    def mk_norm_env(st):
        env = {}
        envc[0] += 1
        pf = "e%d_" % envc[0]
        env["xt"] = [st.enter_context(nc.sbuf_tensor(pf + "xt%d" % i, [128, D], F32)) for i in range(3)]
        env["xn"] = [st.enter_context(nc.sbuf_tensor(pf + "xn%d" % i, [128, D], BF16)) for i in range(3)]
        env["hTt"] = [st.enter_context(nc.sbuf_tensor(pf + "hTt%d" % i, [128, 8, 128], BF16)) for i in range(3)]
        env["stat"] = [st.enter_context(nc.sbuf_tensor(pf + "stat%d" % i, [128, 4], F32)) for i in range(3)]
        env["junk"] = st.enter_context(nc.sbuf_tensor(pf + "junk", [128, D], BF16))
        env["eps"] = st.enter_context(nc.sbuf_tensor(pf + "eps_t", [128, 1], F32))
        env["pT"] = st.enter_context(nc.psum_tensor(pf + "pT", [128, 8, 128], BF16))
        for nm in ("xt", "xn", "hTt", "stat"):
            env["b_" + nm] = [Buf(), Buf(), Buf()]
        env["b_junk"], env["b_pT"], env["b_eps"] = Buf(), Buf(), Buf()
        K.op(V, lambda: nc.vector.memset(env["eps"][:], EPS), w=[env["b_eps"]])
        return env

    def head_norm(env, src, b_src, H, dst, b_dst, CAt, SBt, j, tagbufs, part=0):
        sq, qn, T1, T2, hs = tagbufs["sq"], tagbufs["qn"], tagbufs["T1"], tagbufs["T2"], tagbufs["hs"]
        b_sq, b_qn, b_T1, b_T2, b_hs = tagbufs["b_sq"], tagbufs["b_qn"], tagbufs["b_T1"], tagbufs["b_T2"], tagbufs["b_hs"]
        W = H * 64
        if part in (0, 1):
            K.op(A, lambda: nc.scalar.activation(out=sq[:, 0:W], in_=src, func=AF.Square), r=[b_src], w=[b_sq])
        if part in (0, 2):
            K.op(V, lambda: nc.vector.tensor_reduce(out=hs[:, 0, 0:H], in_=sq[:, 0:W].rearrange("p (h e) -> p h e", e=64),
                                                    axis=AX.X, op=ALU.add), r=[b_sq], w=[b_hs])
        if part in (0, 3):
            K.op(A, lambda: nc.scalar.activation(out=hs[:, 1, 0:H], in_=hs[:, 0, 0:H], func=AF.Sqrt,
                                                 scale=1.0 / 64, bias=env["eps"][:, 0:1]), r=[b_hs, env["b_eps"]], w=[b_hs])
        if part not in (0, 4):
            return
        K.op(V, lambda: nc.vector.reciprocal(out=hs[:, 2, 0:H], in_=hs[:, 1, 0:H]), r=[b_hs], w=[b_hs])
        K.op(V, lambda: nc.vector.tensor_tensor(
            out=qn[:, 0:W].rearrange("p (h e) -> p h e", e=64), in0=src.rearrange("p (h e) -> p h e", e=64),
            in1=hs[:, 2, 0:H].rearrange("p (h o) -> p h o", o=1).to_broadcast([128, H, 64]), op=ALU.mult),
            r=[b_src, b_hs], w=[b_qn])
        cab = CAt.rearrange("p (o e) -> p o e", o=1).to_broadcast([128, H, 64])
        if SBt is None:
            K.op(V, lambda: nc.vector.tensor_tensor(out=dst.rearrange("p (h e) -> p h e", e=64),
                                                    in0=qn[:, 0:W].rearrange("p (h e) -> p h e", e=64),
                                                    in1=cab, op=ALU.mult), r=[b_qn, env["b_tab"]], w=[b_dst])
            return
        K.op(V, lambda: nc.vector.tensor_tensor(out=T1[:, 0:W].rearrange("p (h e) -> p h e", e=64),
                                                in0=qn[:, 0:W].rearrange("p (h e) -> p h e", e=64),
                                                in1=cab, op=ALU.mult), r=[b_qn, env["b_tab"]], w=[b_T1])
        for hf in range(2):
            sbv = SBt.rearrange("p (o a f r) -> p o a f r", o=1, a=2, f=2)[:, :, :, hf, :].to_broadcast([128, H, 2, 16])
            qv = qn[:, 0:W].rearrange("p (h a f r) -> p h a f r", a=2, f=2, r=16)[:, :, :, 1 - hf, :]
            ov = T2[:, 0:W].rearrange("p (h a f r) -> p h a f r", a=2, f=2, r=16)[:, :, :, hf, :]
            K.op(V, lambda sbv=sbv, qv=qv, ov=ov: nc.vector.tensor_tensor(out=ov, in0=qv, in1=sbv, op=ALU.mult),
                 r=[b_qn, env["b_tab"]], w=[b_T2], acc=(hf > 0))
        K.op(V, lambda: nc.vector.tensor_tensor(out=dst, in0=T1[:, 0:W], in1=T2[:, 0:W], op=ALU.add),
             r=[b_T1, b_T2], w=[b_dst])

    with ExitStack() as st:
        env = mk_norm_env(st)
        tabs = {}
        env["b_tab"] = Buf()
        gq = st.enter_context(nc.sbuf_tensor("gq_sb", [128, 2, 64], F32))
        gk = st.enter_context(nc.sbuf_tensor("gk_sb", [128, 2, 64], F32))
        b_g = Buf()
        K.dma(S, gq[:], qg_d, w=[b_g], acc=True)
        K.dma(S, gk[:], kg_d, w=[b_g], acc=True)
        for nm, src_d, gsel in (("CAq", ropeC_d, (gq, 0)), ("SBq", ropeS_d, (gq, 1)),
                                ("CAk", ropeC_d, (gk, 0)), ("SBk", ropeS_d, (gk, 1))):
            t = st.enter_context(nc.sbuf_tensor(nm, [128, 16, 64], F32))
            tabs[nm] = t
            bt = Buf()
            K.dma(S, t[:], src_d, w=[bt])
            gt, gi_ = gsel
            K.op(V, lambda t=t, gt=gt, gi_=gi_: nc.vector.tensor_tensor(
                out=t[:], in0=t[:], in1=gt[:, gi_:gi_ + 1, :].to_broadcast([128, 16, 64]), op=ALU.mult),
                r=[bt, b_g], w=[bt])
            K.op(V, lambda t=t: nc.vector.tensor_copy(out=t[:, 0:1, 0:1], in_=t[:, 0:1, 0:1]),
                 r=[bt], w=[env["b_tab"]], acc=True)
        tb = {}
        for nm, shp, dt in (("sq", [128, D], F32), ("qn", [128, D], F32), ("T1", [128, D], F32),
                            ("T2", [128, D], F32), ("hs", [128, 3, 16], F32)):
            tb[nm] = st.enter_context(nc.sbuf_tensor("s1_" + nm, shp, dt))
            tb["b_" + nm] = Buf()
        qr = [st.enter_context(nc.sbuf_tensor("qr%d" % i, [128, D], BF16)) for i in range(2)]
        kr = [st.enter_context(nc.sbuf_tensor("kr%d" % i, [128, 256], BF16)) for i in range(2)]
        vb = [st.enter_context(nc.sbuf_tensor("vb%d" % i, [128, 256], BF16)) for i in range(2)]
        ub = [st.enter_context(nc.sbuf_tensor("ub%d" % i, [128, 512], BF16)) for i in range(2)]
        qTt = [st.enter_context(nc.sbuf_tensor("qTt%d" % i, [128, 8, 128], BF16)) for i in range(2)]
        kTt = [st.enter_context(nc.sbuf_tensor("kTt%d" % i, [128, 2, 128], BF16)) for i in range(2)]
        b_qr, b_kr, b_vb, b_ub, b_qTt, b_kTt = ([Buf(), Buf()] for _ in range(6))
        pq = st.enter_context(nc.psum_tensor("pq", [128, D], F32))
        pkv = st.enter_context(nc.psum_tensor("pkv", [128, 512], F32))
        pu = st.enter_context(nc.psum_tensor("pu", [128, 512], F32))
        pT2 = st.enter_context(nc.psum_tensor("pT2", [128, 8, 128], BF16))
        pTk = st.enter_context(nc.psum_tensor("pTk", [128, 2, 128], BF16))
        b_pq, b_pkv, b_pu, b_pT2, b_pTk = Buf(), Buf(), Buf(), Buf(), Buf()

        tiles = []
        for b in range(NB):
            for ih in range(2):
                tiles.append(("c", b, ih, 0))
            for i in range(8):
                for cb in range(2):
                    tiles.append(("l", b, i, cb))
        hts = {}

        def load_x(j):
            kind, b, i, cb = tiles[j]
            xt, b_xt = env["xt"][j % 3], env["b_xt"][j % 3]
            if kind == "l":
                src = x_d[b].rearrange("(c i) d -> i c d", i=8)[i, cb * 128:(cb + 1) * 128, :]
                K.dma(A, xt[:], src, w=[b_xt])
            else:
                cv = ctx_d[b].rearrange("(c i) d -> i c d", i=8)
                for isub in range(4):
                    K.dma(A, xt[isub * 32:(isub + 1) * 32, :], cv[4 * i + isub, :, :], w=[b_xt], acc=(isub > 0))

        tbk = {}
        for nm, shp, dt in (("sq", [128, 256], F32), ("qn", [128, 256], F32), ("T1", [128, 256], F32),
                            ("T2", [128, 256], F32), ("hs", [128, 3, 16], F32)):
            tbk[nm] = st.enter_context(nc.sbuf_tensor("s1k_" + nm, shp, dt))
            tbk["b_" + nm] = Buf()

        def phaseA1(j):
            if j + 2 < len(tiles):
                load_x(j + 2)
            norm_T1(env, j)

        def phaseA2(j):
            kind, b, i, cb = tiles[j]
            norm_T2a(env, j)

        def phaseA3(j):
            kind, b, i, cb = tiles[j]
            hTt, b_hTt = norm_T2b(env, j, (b if kind == "l" else 2), A1, 0)
            hts[j] = (hTt, b_hTt)
            if kind == "l":
                col0 = b * SEQ + i * 256 + cb * 128
                K.dma(S, hT_s.rearrange("(k p) n -> p k n", p=128)[:, :, col0:col0 + 128], hTt[:],
                      r=[b_hTt], w=[db("hT_s")], acc=True)

        def phaseBmm(j):
            kind, b, i, cb = tiles[j]
            hTt, b_hTt = hts[j]

            def mm(dst, c0, n):
                def f():
                    inst = None
                    for blk in range(n):
                        for k in range(8):
                            inst = nc.tensor.matmul(dst[:, blk * 512:(blk + 1) * 512], lhsT=hTt[:, k, :],
                                                    rhs=win_sb[:, k, c0 + blk * 512:c0 + (blk + 1) * 512],
                                                    start=(k == 0), stop=(k == 7))
                    return inst
                return f
            if kind == "l":
                K.op(P, mm(pq, 0, 2), r=[b_hTt, b_win], w=[b_pq])
            K.op(P, mm(pkv, 1024, 1), r=[b_hTt, b_win], w=[b_pkv])
            K.op(P, mm(pu, 1536, 1), r=[b_hTt, b_win], w=[b_pu])

        def phaseBch(j, part):
            kind, b, i, cb = tiles[j]
            jj = j % 2
            kcol0 = b * KEYS + (i * 256 + cb * 128 if kind == "l" else SEQ + i * 128)
            if kind == "l":
                jt = i * 2 + cb
                head_norm(env, pq[:, :], b_pq, 16, qr[jj][:, :], b_qr[jj], tabs["CAq"][:, jt, :], tabs["SBq"][:, jt, :], j, tb, part)
                head_norm(env, pkv[:, 0:256], b_pkv, 4, kr[jj][:, :], b_kr[jj], tabs["CAk"][:, jt, :], tabs["SBk"][:, jt, :], j, tbk, part)
            else:
                head_norm(env, pkv[:, 0:256], b_pkv, 4, kr[jj][:, :], b_kr[jj], gk[:, 0, :], None, j, tbk, part)
            if part != 4:
                return
            K.op(A, lambda: nc.scalar.copy(out=vb[jj][:], in_=pkv[:, 256:512]), r=[b_pkv], w=[b_vb[jj]])
            K.dma(S, v_s[kcol0:kcol0 + 128, :], vb[jj][:], r=[b_vb[jj]], w=[db("v_s")], acc=True)
            K.op(A, lambda: nc.scalar.copy(out=ub[jj][:], in_=pu[:, :]), r=[b_pu], w=[b_ub[jj]])
            if kind == "l":
                K.dma(S, ul_s[b, cb * 128:(cb + 1) * 128, i, :], ub[jj][:], r=[b_ub[jj]], w=[db("ul_s")], acc=True)
            else:
                for isub in range(4):
                    K.dma(S, uc_s[b, :, 4 * i + isub, :], ub[jj][isub * 32:(isub + 1) * 32, :],
                          r=[b_ub[jj]], w=[db("uc_s")], acc=True)

        def phaseC(j):
            kind, b, i, cb = tiles[j]
            jj = j % 2
            kcol0 = b * KEYS + (i * 256 + cb * 128 if kind == "l" else SEQ + i * 128)
            if kind == "l":
                col0 = b * SEQ + i * 256 + cb * 128

                def trq():
                    inst = None
                    for k in range(8):
                        inst = nc.tensor.transpose(out=pT2[:, k, :], in_=qr[jj][:, k * 128:(k + 1) * 128], identity=ident_b[:])
                    return inst
                K.op(P, trq, r=[b_qr[jj], b_identb], w=[b_pT2])
                K.op(A, lambda: nc.scalar.copy(out=qTt[jj][:], in_=pT2[:]), r=[b_pT2], w=[b_qTt[jj]])
                K.dma(S, qT_s.rearrange("h p n -> p h n")[:, :, col0:col0 + 128], qTt[jj][:],
                      r=[b_qTt[jj]], w=[db("qT_s")], acc=True)

            def trk():
                inst = None
                for k in range(2):
                    inst = nc.tensor.transpose(out=pTk[:, k, :], in_=kr[jj][:, k * 128:(k + 1) * 128], identity=ident_b[:])
                return inst
            K.op(P, trk, r=[b_kr[jj], b_identb], w=[b_pTk])
            K.op(A, lambda: nc.scalar.copy(out=kTt[jj][:], in_=pTk[:]), r=[b_pTk], w=[b_kTt[jj]])
            K.dma(S, kT_s.rearrange("h p n -> p h n")[:, :, kcol0:kcol0 + 128], kTt[jj][:],
                  r=[b_kTt[jj]], w=[db("kT_s")], acc=True)

        nt_ = len(tiles)
        load_x(0)
        load_x(1)
        for j0 in range(2):
            phaseA1(j0)
            phaseA2(j0)
            phaseA3(j0)
        for j in range(nt_):
            if j + 2 < nt_:
                phaseA1(j + 2)
            phaseBmm(j)
            if j + 2 < nt_:
                phaseA2(j + 2)
            phaseBch(j, 1)
            phaseBch(j, 2)
            phaseBch(j, 3)
            phaseBch(j, 4)
            if j + 2 < nt_:
                phaseA3(j + 2)
            if j >= 1:
                phaseC(j - 1)
        phaseC(nt_ - 1)
        K.barrier()
    win_st.close()
    if debug == 2:
        return nc

    TWO_PI = 2.0 * math.pi
    CW1 = 6.28125
    CW2 = float(np.float32(TWO_PI - CW1))
    wd_s = dscr("wd_s", [128, 64, 128], BF16)
    wd2_s = dscr("wd2_s", [128, 64, 128], BF16)
    cwa_s = dscr("cwa_s", [128, 64, 128], BF16)
    cwb_s = dscr("cwb_s", [128, 64, 128], BF16)
    toep_s = dscr("toep_s", [128, 32, 128], BF16)
    rc_s = dscr("rc_s", [128, 2, 64], F32)
    prep_st = ExitStack()
    b_prep = Buf("prep")
    pw_shared = []

    def prep_body():
        sc_y = prep_st.enter_context(nc.sbuf_tensor("psc_y", [128, 576], F32))
        sc_ki = prep_st.enter_context(nc.sbuf_tensor("psc_ki", [128, 576], I32))
        sc_kf = prep_st.enter_context(nc.sbuf_tensor("psc_kf", [128, 576], F32))
        sc_r = prep_st.enter_context(nc.sbuf_tensor("psc_r", [128, 576], F32))

        def sincos(ang, n, sin_out, cos_out, bufr, bufw):
            y, ki, kf, r_ = sc_y[:, 0:n], sc_ki[:, 0:n], sc_kf[:, 0:n], sc_r[:, 0:n]
            rr, ww = [b_prep] + bufr, [b_prep] + bufw
            o = lambda fn: K.op(V, fn, r=rr, w=ww)
            o(lambda: nc.vector.tensor_scalar(out=y, in0=ang, scalar1=1.0 / TWO_PI, scalar2=None, op0=ALU.mult))
            o(lambda: nc.vector.tensor_copy(out=ki, in_=y))
            o(lambda: nc.vector.tensor_copy(out=kf, in_=ki))
            o(lambda: nc.vector.scalar_tensor_tensor(out=r_, in0=kf, scalar=-CW1, in1=ang, op0=ALU.mult, op1=ALU.add))
            o(lambda: nc.vector.scalar_tensor_tensor(out=r_, in0=kf, scalar=-CW2, in1=r_, op0=ALU.mult, op1=ALU.add))
            o(lambda: nc.vector.tensor_scalar(out=kf, in0=r_, scalar1=math.pi / 2, scalar2=-TWO_PI, op0=ALU.is_gt, op1=ALU.mult))
            o(lambda: nc.vector.scalar_tensor_tensor(out=y, in0=r_, scalar=math.pi / 2, in1=kf, op0=ALU.add, op1=ALU.add))
            o(lambda: nc.vector.tensor_scalar(out=y, in0=y, scalar1=-math.pi, scalar2=math.pi, op0=ALU.max, op1=ALU.min))
            o(lambda: nc.vector.tensor_scalar(out=r_, in0=r_, scalar1=-math.pi, scalar2=math.pi, op0=ALU.max, op1=ALU.min))
            K.op(A, lambda: nc.scalar.activation(out=cos_out, in_=y, func=AF.Sin), r=rr, w=ww)
            K.op(A, lambda: nc.scalar.activation(out=sin_out, in_=r_, func=AF.Sin), r=rr, w=ww)
        def pt(name, shape, dt=F32):
            return prep_st.enter_context(nc.sbuf_tensor("pp_" + name, shape, dt))
        lamr, lami, ldt = pt("lamr", [128, 64]), pt("lami", [128, 64]), pt("ldt", [128, 64])
        bre, bim = pt("bre", [128, 64, 16]), pt("bim", [128, 64, 16])
        cre, cim = pt("cre", [128, 64, 16]), pt("cim", [128, 64, 16])
        dcol = pt("dcol", [128, 32])
        mkf, mkb = pt("mkf", [128, 128]), pt("mkb", [128, 128])
        kv = pt("kv", [128, 9])
        for t_, d_ in ((lamr, lamr_d), (lami, lami_d), (ldt, ldt_d), (bre, bre_d), (bim, bim_d), (cre, cre_d),
                       (cim, cim_d), (dcol, dcol_d), (mkf, maskf_d), (mkb, maskb_d), (kv, kv_d)):
            K.dma(S, t_[:], d_, w=[b_prep], acc=True)
        dt_ = pt("dt", [128, 64]); xr = pt("xr", [128, 64]); th = pt("th", [128, 64])
        MX = pt("MX", [128, 9, 64]); ANG = pt("ANG", [128, 9, 64])
        EPOS = pt("EPOS", [128, 9, 64]); ENEG = pt("ENEG", [128, 9, 64])
        COSM = pt("COSM", [128, 9, 64]); SINM = pt("SINM", [128, 9, 64])
        PR = pt("PR", [128, 9, 64]); PI = pt("PI", [128, 9, 64]); NR = pt("NR", [128, 9, 64]); NI = pt("NI", [128, 9, 64])

        def vop(fn):
            K.op(V, fn, r=[b_prep], w=[b_prep])

        def aop(fn):
            K.op(A, fn, r=[b_prep], w=[b_prep])

        aop(lambda: nc.scalar.activation(out=dt_[:], in_=ldt[:], func=AF.Exp))
        vop(lambda: nc.vector.tensor_tensor(out=xr[:], in0=lamr[:], in1=dt_[:], op=ALU.mult))
        vop(lambda: nc.vector.tensor_tensor(out=th[:], in0=lami[:], in1=dt_[:], op=ALU.mult))
        kvb = kv[:].rearrange("p (m o) -> p m o", o=1).to_broadcast([128, 9, 64])
        vop(lambda: nc.vector.tensor_tensor(out=MX[:], in0=xr[:].rearrange("p (o t) -> p o t", o=1).to_broadcast([128, 9, 64]), in1=kvb, op=ALU.mult))
        vop(lambda: nc.vector.tensor_tensor(out=ANG[:], in0=th[:].rearrange("p (o t) -> p o t", o=1).to_broadcast([128, 9, 64]), in1=kvb, op=ALU.mult))
        aop(lambda: nc.scalar.activation(out=EPOS[:], in_=MX[:], func=AF.Exp))
        aop(lambda: nc.scalar.activation(out=ENEG[:], in_=MX[:], func=AF.Exp, scale=-1.0))
        fl = lambda t_: t_[:].rearrange("p m t -> p (m t)")
        sincos(fl(ANG), 576, fl(SINM), fl(COSM), [], [])
        vop(lambda: nc.vector.tensor_tensor(out=PR[:], in0=EPOS[:], in1=COSM[:], op=ALU.mult))
        vop(lambda: nc.vector.tensor_tensor(out=PI[:], in0=EPOS[:], in1=SINM[:], op=ALU.mult))
        vop(lambda: nc.vector.tensor_tensor(out=NR[:], in0=ENEG[:], in1=COSM[:], op=ALU.mult))
        vop(lambda: nc.vector.scalar_tensor_tensor(out=NI[:], in0=ENEG[:], scalar=-1.0, in1=SINM[:], op0=ALU.mult, op1=ALU.mult))
        K.dma(S, rc_s[:, 0, :], EPOS[:, 8, :], r=[b_prep], w=[db("rc_s")], acc=True)
        K.dma(S, rc_s[:, 1, :], ANG[:, 8, :], r=[b_prep], w=[db("rc_s")], acc=True)
        ar1 = pt("ar1", [128, 64]); den = pt("den", [128, 64]); tq = pt("tq", [128, 64]); tq2 = pt("tq2", [128, 64])
        cr = pt("cr", [128, 64]); ci = pt("ci", [128, 64])
        vop(lambda: nc.vector.tensor_scalar(out=ar1[:], in0=PR[:, 1, :], scalar1=-1.0, scalar2=None, op0=ALU.add))
        vop(lambda: nc.vector.tensor_tensor(out=den[:], in0=lamr[:], in1=lamr[:], op=ALU.mult))
        vop(lambda: nc.vector.tensor_tensor(out=tq[:], in0=lami[:], in1=lami[:], op=ALU.mult))
        vop(lambda: nc.vector.tensor_tensor(out=den[:], in0=den[:], in1=tq[:], op=ALU.add))
        vop(lambda: nc.vector.reciprocal(out=den[:], in_=den[:]))
        vop(lambda: nc.vector.tensor_tensor(out=tq[:], in0=ar1[:], in1=lamr[:], op=ALU.mult))
        vop(lambda: nc.vector.tensor_tensor(out=tq2[:], in0=PI[:, 1, :], in1=lami[:], op=ALU.mult))
        vop(lambda: nc.vector.tensor_tensor(out=tq[:], in0=tq[:], in1=tq2[:], op=ALU.add))
        vop(lambda: nc.vector.tensor_tensor(out=cr[:], in0=tq[:], in1=den[:], op=ALU.mult))
        vop(lambda: nc.vector.tensor_tensor(out=tq[:], in0=PI[:, 1, :], in1=lamr[:], op=ALU.mult))
        vop(lambda: nc.vector.tensor_tensor(out=tq2[:], in0=ar1[:], in1=lami[:], op=ALU.mult))
        vop(lambda: nc.vector.tensor_tensor(out=tq[:], in0=tq[:], in1=tq2[:], op=ALU.subtract))
        vop(lambda: nc.vector.tensor_tensor(out=ci[:], in0=tq[:], in1=den[:], op=ALU.mult))
        Bbr = pt("Bbr", [128, 64, 16]); Bbi = pt("Bbi", [128, 64, 16]); tb1 = pt("tb1", [128, 64, 16])
        crb = cr[:].rearrange("p (t o) -> p t o", o=1).to_broadcast([128, 64, 16])
        cib = ci[:].rearrange("p (t o) -> p t o", o=1).to_broadcast([128, 64, 16])
        vop(lambda: nc.vector.tensor_tensor(out=Bbr[:], in0=bre[:], in1=crb, op=ALU.mult))
        vop(lambda: nc.vector.tensor_tensor(out=tb1[:], in0=bim[:], in1=cib, op=ALU.mult))
        vop(lambda: nc.vector.tensor_tensor(out=Bbr[:], in0=Bbr[:], in1=tb1[:], op=ALU.subtract))
        vop(lambda: nc.vector.tensor_tensor(out=Bbi[:], in0=bim[:], in1=crb, op=ALU.mult))
        vop(lambda: nc.vector.tensor_tensor(out=tb1[:], in0=bre[:], in1=cib, op=ALU.mult))
        vop(lambda: nc.vector.tensor_tensor(out=Bbi[:], in0=Bbi[:], in1=tb1[:], op=ALU.add))
        EPr, EPi, FPr, FPi = (pt(n_, [128, 32, 2, 8]) for n_ in ("EPr", "EPi", "FPr", "FPi"))
        for dst_, src_ in ((EPr, PR), (EPi, PI), (FPr, NR), (FPi, NI)):
            sv = src_[:].rearrange("p m (g d) -> p m g d", d=2)
            vop(lambda dst_=dst_, sv=sv: nc.vector.tensor_copy(
                out=dst_[:, :, 0, :], in_=sv[:, 7::-1, :, 0].rearrange("p m g -> p g m")))
            vop(lambda dst_=dst_, sv=sv: nc.vector.tensor_copy(
                out=dst_[:, :, 1, :], in_=sv[:, 0:8, :, 1].rearrange("p m g -> p g m")))
        TN = 4
        Er, Ei, Fr, Fi, t1, CWr, CWi = (pt(n_, [128, TN, 8, 16]) for n_ in ("Er", "Ei", "Fr", "Fi", "t1", "CWr", "CWi"))
        ET, ET2, FT2 = (pt(n_, [128, TN, 128]) for n_ in ("ET", "ET2", "FT2"))
        ttmp = pt("ttmp", [128, 128]); ttmp2 = pt("ttmp2", [128, 128])
        pw0, b_pw0 = pw_shared[0], pw_shared[1]
        pw = [pw0, pw0]
        b_pw = [b_pw0, b_pw0]
        stg = {}
        for nm_ in ("WD", "WD2", "CWa", "CWb"):
            stg[nm_] = [pt("st_" + nm_ + str(i_), [128, TN, 128], BF16) for i_ in range(2)]
            stg["b_" + nm_] = [Buf(), Buf()]
        stg["Toep"] = [pt("st_Toep" + str(i_), [128, TN // 2, 128], BF16) for i_ in range(2)]
        stg["b_Toep"] = [Buf(), Buf()]
        pass
        v4 = lambda t_: t_[:].rearrange("p t (i h) -> p t i h", h=16)
        for ch in range(64 // TN):
            t0 = ch * TN
            cq = ch % 2
            ep = lambda t_: t_[:].rearrange("p g d i -> p (g d) i")[:, t0:t0 + TN, :].rearrange("p t (i o) -> p t i o", o=1).to_broadcast([128, TN, 8, 16])
            bb = lambda t_: t_[:, t0:t0 + TN, :].rearrange("p t (o h) -> p t o h", o=1).to_broadcast([128, TN, 8, 16])
            a8 = lambda t_: t_[:, 8, t0:t0 + TN].rearrange("p (t o q) -> p t o q", o=1, q=1).to_broadcast([128, TN, 8, 16])

            def cmul(outr, outi, ar_, ai_, br_, bi_):
                vop(lambda: nc.vector.tensor_tensor(out=outr[:], in0=ar_, in1=br_, op=ALU.mult))
                vop(lambda: nc.vector.tensor_tensor(out=t1[:], in0=ai_, in1=bi_, op=ALU.mult))
                vop(lambda: nc.vector.tensor_tensor(out=outr[:], in0=outr[:], in1=t1[:], op=ALU.subtract))
                vop(lambda: nc.vector.tensor_tensor(out=outi[:], in0=ar_, in1=bi_, op=ALU.mult))
                vop(lambda: nc.vector.tensor_tensor(out=t1[:], in0=ai_, in1=br_, op=ALU.mult))
                vop(lambda: nc.vector.tensor_tensor(out=outi[:], in0=outi[:], in1=t1[:], op=ALU.add))
            cmul(Er, Ei, ep(EPr), ep(EPi), bb(Bbr), bb(Bbi))
            cmul(Fr, Fi, ep(FPr), ep(FPi), bb(cre), bb(cim))
            cmul(CWr, CWi, a8(PR), a8(PI), Fr[:], Fi[:])
            lo, hi = slice(0, 64), slice(64, 128)

            def asm(dst, dsl, src, neg):
                if neg:
                    vop(lambda: nc.vector.tensor_scalar(out=dst, in0=src, scalar1=-1.0, scalar2=None, op0=ALU.mult))
                else:
                    vop(lambda: nc.vector.tensor_copy(out=dst, in_=src))
            asm(v4(ET)[lo], None, Er[lo], False); asm(v4(ET)[hi], None, Ei[hi], False)
            asm(v4(ET2)[lo], None, Ei[lo], False); asm(v4(ET2)[hi], None, Er[hi], True)
            asm(v4(FT2)[lo], None, Fr[lo], False); asm(v4(FT2)[hi], None, Fi[hi], True)
            K.op(V, lambda: nc.vector.tensor_copy(out=stg["CWa"][cq][lo].rearrange("p t (i h) -> p t i h", h=16), in_=CWr[lo]),
                 r=[b_prep], w=[stg["b_CWa"][cq]])
            K.op(V, lambda: nc.vector.tensor_scalar(out=stg["CWa"][cq][hi].rearrange("p t (i h) -> p t i h", h=16), in0=CWi[hi],
                                                    scalar1=-1.0, scalar2=None, op0=ALU.mult), r=[b_prep], w=[stg["b_CWa"][cq]], acc=True)
            K.op(V, lambda: nc.vector.tensor_scalar(out=stg["CWb"][cq][lo].rearrange("p t (i h) -> p t i h", h=16), in0=CWi[lo],
                                                    scalar1=-1.0, scalar2=None, op0=ALU.mult), r=[b_prep], w=[stg["b_CWb"][cq]])
            K.op(V, lambda: nc.vector.tensor_scalar(out=stg["CWb"][cq][hi].rearrange("p t (i h) -> p t i h", h=16), in0=CWr[hi],
                                                    scalar1=-1.0, scalar2=None, op0=ALU.mult), r=[b_prep], w=[stg["b_CWb"][cq]], acc=True)
            K.dma(S, cwa_s[:, t0:t0 + TN, :], stg["CWa"][cq][:], r=[stg["b_CWa"][cq]], w=[db("cwa_s")], acc=True)
            K.dma(S, cwb_s[:, t0:t0 + TN, :], stg["CWb"][cq][:], r=[stg["b_CWb"][cq]], w=[db("cwb_s")], acc=True)
            for src_, dstw in ((ET, "WD"), (ET2, "WD2")):
                for q in range(TN // 4):
                    pp, b_pp = pw[q % 2], b_pw[q % 2]

                    def trw(src_=src_, q=q, pp=pp):
                        inst = None
                        for u in range(4):
                            inst = nc.tensor.transpose(out=pp[:, u, :], in_=src_[:, q * 4 + u, :], identity=ident_f[:])
                        return inst
                    K.co.armed = False
                    K.op(P, trw, r=[b_prep, b_identf], w=[b_pp])
                    K.op(V, lambda dstw=dstw, q=q, pp=pp: nc.vector.tensor_copy(out=stg[dstw][cq][:, q * 4:q * 4 + 4, :], in_=pp[:]),
                         r=[b_pp], w=[stg["b_" + dstw][cq]], acc=(q > 0))
                    K.co.armed = True
                K.dma(S, {"WD": wd_s, "WD2": wd2_s}[dstw][:, t0:t0 + TN, :], stg[dstw][cq][:], r=[stg["b_" + dstw][cq]], w=[db(dstw + "_s")], acc=True)
            for gl in range(TN // 2):
                g = t0 // 2 + gl
                pp, b_pp = pw[gl % 2], b_pw[gl % 2]

                def tz(gl=gl, pp=pp):
                    inst = None
                    for d_ in range(2):
                        inst = nc.tensor.matmul(pp[:, d_, :], lhsT=ET[:, gl * 2 + d_, :], rhs=FT2[:, gl * 2 + d_, :], start=True, stop=True)
                    return inst
                K.co.armed = False
                K.op(P, tz, r=[b_prep], w=[b_pp])
                K.op(V, lambda pp=pp: nc.vector.tensor_tensor(out=ttmp[:], in0=pp[:, 0, :], in1=mkf[:], op=ALU.mult), r=[b_pp, b_prep], w=[b_prep])
                K.op(V, lambda pp=pp: nc.vector.tensor_tensor(out=ttmp2[:], in0=pp[:, 1, :], in1=mkb[:], op=ALU.mult), r=[b_pp, b_prep], w=[b_prep])
                K.co.armed = True
                vop(lambda: nc.vector.tensor_tensor(out=ttmp[:], in0=ttmp[:], in1=ttmp2[:], op=ALU.add))
                K.op(V, lambda g=g, gl=gl: nc.vector.scalar_tensor_tensor(out=stg["Toep"][cq][:, gl, :], in0=ident_f[:], scalar=dcol[:, g:g + 1], in1=ttmp[:],
                                                                  op0=ALU.mult, op1=ALU.add), r=[b_prep, b_identf], w=[stg["b_Toep"][cq]], acc=(gl > 0))
            K.dma(S, toep_s[:, t0 // 2:t0 // 2 + TN // 2, :], stg["Toep"][cq][:], r=[stg["b_Toep"][cq]], w=[db("toep_s")], acc=True)

    K.co = Interleaver(prep_body)
    K.co.armed = True

    OT_s = dscr("OT_s", [D, NL], BF16)
    with ExitStack() as st:
        esel_f = st.enter_context(nc.sbuf_tensor("esel_sb", [128, 128], F32))
        esel = st.enter_context(nc.sbuf_tensor("esel_bf", [128, 128], BF16))
        b_esel = Buf()
        K.dma(S, esel_f[:], esel_d, w=[b_esel])
        K.op(V, lambda: nc.vector.tensor_copy(out=esel[:], in_=esel_f[:]), r=[b_esel], w=[b_esel])
        Ohi = [st.enter_context(nc.sbuf_tensor("Ohi%d" % i, [128, 512], BF16)) for i in range(2)]
        Olo = [st.enter_context(nc.sbuf_tensor("Olo%d" % i, [128, 512], BF16)) for i in range(2)]
        b_Ohl = [Buf(), Buf()]
        for i_ in range(2):
            K.op(V, lambda i_=i_: nc.vector.memset(Ohi[i_][:], 0.0), w=[b_Ohl[i_]])
            K.op(V, lambda i_=i_: nc.vector.memset(Olo[i_][:], 0.0), w=[b_Ohl[i_]], acc=True)
        kT2 = st.enter_context(nc.sbuf_tensor("kT2", [128, 2, KEYS], BF16))
        Vb = st.enter_context(nc.sbuf_tensor("Vb", [128, 18, 4 * 65 + 64], BF16))
        qTp = [st.enter_context(nc.sbuf_tensor("qTp%d" % i, [128, SEQ], BF16)) for i in range(4)]
        pTs = [st.enter_context(nc.sbuf_tensor("pTs%d" % i, [128, 2, 512], BF16)) for i in range(4)]
        Osb = [st.enter_context(nc.sbuf_tensor("Osb%d" % i, [128, 512], F32)) for i in range(2)]
        On = [st.enter_context(nc.sbuf_tensor("On%d" % i, [64, 512], BF16)) for i in range(2)]
        LA = 2
        pS = [st.enter_context(nc.psum_tensor("pS%d" % i, [128, 2, 512], F32)) for i in range(3)]
        pO0 = st.enter_context(nc.psum_tensor("pO0", [128, 512], F32))
        pO = [pO0, pO0]
        pB4 = st.enter_context(nc.psum_tensor("pB", [128, 4, 128], F32))
        pB = pB4[:].rearrange("p u c -> p (u c)")
        b_kT2, b_Vb, b_pB = Buf(), Buf(), Buf()
        pw_shared.extend([pB4, b_pB])
        b_Osb, b_On = ([Buf(), Buf()] for _ in range(2))
        b_pO0 = Buf()
        b_pO = [b_pO0, b_pO0]
        b_qTp = [Buf() for _ in range(4)]
        b_pTs = [Buf() for _ in range(4)]
        b_pS = [Buf() for _ in range(3)]
        K.op(V, lambda: nc.vector.memset(Vb[:], 0.0), w=[b_Vb])
        K.op(V, lambda: nc.vector.memset(Vb[:, :, 0:260].rearrange("p t (g e) -> p t g e", e=65)[:, :, :, 64:65], 1.0), r=[b_Vb], w=[b_Vb])
        for i_ in range(4):
            K.op(V, lambda i_=i_: nc.vector.memset(qTp[i_][:], 0.0), w=[b_qTp[i_]])
        for i_ in range(2):
            K.op(V, lambda i_=i_: nc.vector.memset(Osb[i_][:], 0.0), w=[b_Osb[i_]])

        def load_batch(b):
            for gp_ in range(2):
                K.dma(S, kT2[:, gp_, :], kT_s[gp_, :, b * KEYS:(b + 1) * KEYS], r=[db("kT_s")], w=[b_kT2], acc=(gp_ > 0))
            for g in range(4):
                K.dma(S, Vb[:, :, g * 65:g * 65 + 64],
                      v_s[b * KEYS:(b + 1) * KEYS, g * 64:(g + 1) * 64].rearrange("(t p) e -> p t e", p=128),
                      r=[db("v_s")], w=[b_Vb], acc=(g > 0))

        def load_q(b, hp):
            for h2 in range(2):
                h = hp * 2 + h2
                half = (h // 4) % 2
                qi = (hp % 2) * 2 + h2
                oh = 1 - half
                K.op(V, lambda qi=qi, oh=oh: nc.vector.memset(qTp[qi][oh * 64:(oh + 1) * 64, :], 0.0), w=[b_qTp[qi]])
                K.dma(S, qTp[qi][half * 64:(half + 1) * 64, :], qT_s[hp, h2 * 64:(h2 + 1) * 64, b * SEQ:(b + 1) * SEQ],
                      r=[db("qT_s")], w=[b_qTp[qi]], acc=True)

        jobs = []
        for b in range(NB):
            for hp in range(8):
                for h2 in range(2):
                    for qb in range(4):
                        for ktp in range(9):
                            jobs.append((b, hp, h2, qb, ktp))
        nj = len(jobs)
        pend = []

        def emit_front(idx):
            b, hp, h2, qb, ktp = jobs[idx]
            if hp == 0 and h2 == 0 and qb == 0 and ktp == 0:
                load_batch(b)
                load_q(b, 0)
            if h2 == 0 and qb == 0 and ktp == 0:
                if hp + 1 < 8:
                    load_q(b, hp + 1)
            h = hp * 2 + h2
            g = h // 4
            qi = (hp % 2) * 2 + h2
            qt, b_qt = qTp[qi], b_qTp[qi]
            ps, b_ps = pS[idx % 3], b_pS[idx % 3]
            pt, b_pt = pTs[idx % 4], b_pTs[idx % 4]

            def qk():
                inst = None
                for u in range(2):
                    kt = ktp * 2 + u
                    inst = nc.tensor.matmul(ps[:, u, :], lhsT=kT2[:, g // 2, kt * 128:(kt + 1) * 128],
                                            rhs=qt[:, qb * 512:(qb + 1) * 512], start=True, stop=True)
                return inst
            K.op(P, qk, r=[b_kT2, b_qt], w=[b_ps])
            K.op(A, lambda: nc.scalar.activation(out=pt[:], in_=ps[:], func=AF.Exp, scale=0.125), r=[b_ps], w=[b_pt])

        def emit_back(idx):
            b, hp, h2, qb, ktp = jobs[idx]
            h = hp * 2 + h2
            g = h // 4
            it = idx // 9
            pt, b_pt = pTs[idx % 4], b_pTs[idx % 4]
            po, b_po = pO[it % 2], b_pO[it % 2]

            def pv():
                inst = None
                for u in range(2):
                    kt = ktp * 2 + u
                    inst = nc.tensor.matmul(po[:, :], lhsT=Vb[:, kt, g * 65:g * 65 + 128], rhs=pt[:, u, :], start=(kt == 0), stop=(kt == 17))
                return inst
            K.op(P, pv, r=[b_Vb, b_pt], w=[b_po], acc=(ktp > 0))
            if ktp == 8:
                osb, b_osb = Osb[it % 2], b_Osb[it % 2]
                on, b_on = On[it % 2], b_On[it % 2]
                fin_dve = K.co.finished
                nrow = 65 if fin_dve else 64
                K.op(V, lambda: nc.vector.tensor_copy(out=osb[0:nrow, :], in_=po[0:nrow, :]), r=[b_po], w=[b_osb])
                if fin_dve:
                    K.op(V, lambda: nc.vector.reciprocal(out=osb[64:65, :], in_=osb[64:65, :]), r=[b_osb], w=[b_osb])
                else:
                    K.op(A, lambda: nc.scalar.activation(out=osb[64:65, :], in_=po[64:65, :], func=AF.Ln), r=[b_po], w=[b_osb], acc=True)
                    K.op(A, lambda: nc.scalar.activation(out=osb[64:65, :], in_=osb[64:65, :], func=AF.Exp, scale=-1.0), r=[b_osb], w=[b_osb], acc=True)

                ohi, olo, b_ohl = Ohi[it % 2], Olo[it % 2], b_Ohl[it % 2]
                K.op(V, lambda: nc.vector.tensor_copy(out=ohi[64:65, :], in_=osb[64:65, :]), r=[b_osb], w=[b_ohl])
                K.op(V, lambda: nc.vector.tensor_tensor(out=olo[64:65, :], in0=osb[64:65, :], in1=ohi[64:65, :], op=ALU.subtract),
                     r=[b_osb, b_ohl], w=[b_ohl], acc=True)

                def fin2():
                    def em():
                        nc.tensor.matmul(pB, lhsT=esel[:, :], rhs=ohi[:, :], start=True, stop=False)
                        return nc.tensor.matmul(pB, lhsT=esel[:, :], rhs=olo[:, :], start=False, stop=True)
                    K.op(P, em, r=[b_esel, b_ohl], w=[b_pB])
                    K.op(V, lambda: nc.vector.tensor_tensor(out=on[:], in0=osb[0:64, :], in1=pB4[0:64, :, :].rearrange("p u c -> p (u c)"), op=ALU.mult),
                         r=[b_osb, b_pB], w=[b_on])
                    K.dma(S, OT_s[h * 64:(h + 1) * 64, b * SEQ + qb * 512:b * SEQ + (qb + 1) * 512], on[:],
                          r=[b_on], w=[db("OT_s")], acc=True)
                pend.append((idx + 7, fin2))

        nback = 0
        for idx in range(nj + LA):
            newb = idx < nj and idx >= 1 and jobs[idx][1:] == (0, 0, 0, 0)
            if newb:
                while nback < idx:
                    emit_back(nback)
                    nback += 1
            if idx < nj:
                emit_front(idx)
            if idx >= LA and nback <= idx - LA:
                emit_back(nback)
                nback += 1
            while pend and pend[0][0] <= idx:
                pend.pop(0)[1]()
            K.co.step(1)
        while pend:
            pend.pop(0)[1]()
        K.co.finish()
        K.barrier()
        prep_st.close()
    if debug == 3:
        return nc

    ygT_s = dscr("ygT_s", [512, NL], BF16)
    with ExitStack() as st:
        WD = st.enter_context(nc.sbuf_tensor("WD", [128, 64, 128], BF16))
        WD2 = st.enter_context(nc.sbuf_tensor("WD2", [128, 64, 128], BF16))
        CWa = st.enter_context(nc.sbuf_tensor("CWa", [128, 64, 128], BF16))
        CWb = st.enter_context(nc.sbuf_tensor("CWb", [128, 64, 128], BF16))
        Toep = st.enter_context(nc.sbuf_tensor("Toep", [128, 32, 128], BF16))
        RT = st.enter_context(nc.sbuf_tensor("RT", [128, 2, 64], F32))
        Rcol = RT[:, 0, :]
        TH8 = RT[:, 1, :]
        iot = st.enter_context(nc.sbuf_tensor("iot", [128, 2, 288], F32))
        b_w = Buf("ssm_w")
        sc_y = st.enter_context(nc.sbuf_tensor("sc_y", [128, 576], F32))
        sc_ki = st.enter_context(nc.sbuf_tensor("sc_ki", [128, 576], I32))
        sc_kf = st.enter_context(nc.sbuf_tensor("sc_kf", [128, 576], F32))
        sc_r = st.enter_context(nc.sbuf_tensor("sc_r", [128, 576], F32))
        K.dma(S, iot[:], iota_d, w=[b_w], acc=True)
        for t_, d_, nm_ in ((WD, wd_s, "WD_s"), (WD2, wd2_s, "WD2_s"), (CWa, cwa_s, "cwa_s"), (CWb, cwb_s, "cwb_s")):
            for hh in range(2):
                K.dma(S if hh == 0 else A, t_[:, hh * 32:(hh + 1) * 32, :], d_[:, hh * 32:(hh + 1) * 32, :], r=[db(nm_)], w=[b_w], acc=True)
        K.dma(S, Toep[:], toep_s, r=[db("toep_s")], w=[b_w], acc=True)
        K.dma(S, RT[:], rc_s, r=[db("rc_s")], w=[b_w], acc=True)
        if debug == 4:
            for nm, t_ in (("WD", WD), ("WD2", WD2), ("CWa", CWa), ("CWb", CWb)):
                dd = nc.dram_tensor("dbg_" + nm, [128, 64, 128], BF16, kind="ExternalOutput").ap()
                K.dma(S, dd, t_[:], r=[b_w], w=[db("dbg_" + nm)])
            dd = nc.dram_tensor("dbg_Toep", [128, 32, 128], BF16, kind="ExternalOutput").ap()
            K.dma(S, dd, Toep[:], r=[b_w], w=[db("dbg_Toep")])
            dd = nc.dram_tensor("dbg_R", [128, 2, 64], F32, kind="ExternalOutput").ap()
            K.dma(S, dd, RT[:], r=[b_w], w=[db("dbg_R")], acc=True)
            K.barrier()
            return nc

        Ug = st.enter_context(nc.sbuf_tensor("Ug", [128, NB, 2, 4096], BF16))
        Ucg = st.enter_context(nc.sbuf_tensor("Ucg", [128, NB, 4096], BF16))
        b_U = Buf("U")
        K.op(V, lambda: nc.vector.memset(Ucg[:], 0.0), w=[b_U])
        with ExitStack() as su:
            Utmp = [su.enter_context(nc.sbuf_tensor("Utmp%d" % i, [128, 4096], BF16)) for i in range(2)]
            b_Utmp = [Buf(), Buf()]
            ui = 0
            for b in range(NB):
                for cbk in range(3):
                    tmp, b_tmp = Utmp[ui % 2], b_Utmp[ui % 2]
                    ui += 1
                    if cbk < 2:
                        K.dma(S, tmp[:], ul_s[b, cbk * 128:(cbk + 1) * 128].rearrange("p i n -> p (i n)"), r=[db("ul_s")], w=[b_tmp])
                        K.op(V, lambda: nc.vector.tensor_copy(out=Ug[:, b, cbk, :].rearrange("p (g i h) -> p g i h", g=32, i=8),
                                                              in_=tmp[:].rearrange("p (i g h) -> p g i h", i=8, g=32)),
                             r=[b_tmp], w=[b_U], acc=True)
                    else:
                        K.dma(S, tmp[0:32, :], uc_s[b].rearrange("c i n -> c (i n)"), r=[db("uc_s")], w=[b_tmp])
                        K.op(A, lambda: nc.scalar.copy(out=Ucg[0:32, b, :].rearrange("p (g i h) -> p g i h", g=32, i=8),
                                                       in_=tmp[0:32, :].rearrange("p (i g h) -> p g i h", i=8, g=32)),
                             r=[b_tmp], w=[b_U], acc=True)
            K.barrier()
        UText = [st.enter_context(nc.sbuf_tensor("UText%d" % i, [128, NB, 320], BF16)) for i in range(2)]
        tabCS = [st.enter_context(nc.sbuf_tensor("tabCS%d" % i, [128, 2, 2, 288], F32)) for i in range(2)]
        tabC = [t_[:, :, 0, :] for t_ in tabCS]
        tabS = [t_[:, :, 1, :] for t_ in tabCS]
        TH8s = st.enter_context(nc.sbuf_tensor("TH8s", [128, 64], F32))
        Wt = [st.enter_context(nc.sbuf_tensor("Wt%d" % i, [128, 288], F32)) for i in range(2)]
        Wt2 = [st.enter_context(nc.sbuf_tensor("Wt2%d" % i, [128, 288], F32)) for i in range(2)]
        Zt = [st.enter_context(nc.sbuf_tensor("Zt%d" % i, [128, 288], F32)) for i in range(2)]
        ZCS = [[[st.enter_context(nc.sbuf_tensor("ZCS%d%d%d" % (i, b, d_), [128, 2, 288], BF16)) for d_ in range(2)] for b in range(NB)] for i in range(2)]
        ZC = [[[ZCS[i][b][d_][:, 0, :] for d_ in range(2)] for b in range(NB)] for i in range(2)]
        ZS = [[[ZCS[i][b][d_][:, 1, :] for d_ in range(2)] for b in range(NB)] for i in range(2)]
        y_sb = st.enter_context(nc.sbuf_tensor("y_sb", [128, 4, 8, 128], F32))
        gx = [st.enter_context(nc.sbuf_tensor("gx%d" % i, [128, 512], F32)) for i in range(2)]
        gt1 = [st.enter_context(nc.sbuf_tensor("gt1%d" % i, [128, 512], F32)) for i in range(2)]
        gsg = [st.enter_context(nc.sbuf_tensor("gsg%d" % i, [128, 512], F32)) for i in range(2)]
        ygt = [st.enter_context(nc.sbuf_tensor("ygt%d" % i, [128, 4, 128], BF16)) for i in range(2)]
        pU = st.enter_context(nc.psum_tensor("pU", [128, 320], F32))
        pD = [st.enter_context(nc.psum_tensor("pD%d" % i, [128, 288], F32)) for i in range(2)]
        pD2 = [st.enter_context(nc.psum_tensor("pD2%d" % i, [128, 288], F32)) for i in range(2)]
        py = [st.enter_context(nc.psum_tensor("py%d" % i, [128, 4, 128], F32)) for i in range(2)]
        pTy = st.enter_context(nc.psum_tensor("pTy", [128, 4, 128], F32))
        b_UText, b_tab, b_Wt, b_Wt2, b_Zt, b_pD, b_pD2, b_py, b_gx, b_gt1, b_gsg, b_ygt = ([Buf(), Buf()] for _ in range(12))
        b_ZC = [[[Buf(), Buf()] for _ in range(NB)] for _ in range(2)]
        b_ZS = [[[Buf(), Buf()] for _ in range(NB)] for _ in range(2)]
        b_pU, b_ysb, b_pTy, b_ths = Buf(), Buf(), Buf(), Buf()
        GELU_S = 2.0 * math.sqrt(2.0 / math.pi)
        K.op(V, lambda: nc.vector.tensor_scalar(out=TH8s[:], in0=TH8, scalar1=1.0 / TWO_PI, scalar2=None, op0=ALU.mult),
             r=[b_w], w=[b_ths])
        SIN_SC = TWO_PI * (1.0 - 2e-7)
        hpi = st.enter_context(nc.sbuf_tensor("hpi", [128, 1], F32))
        b_hpi = Buf()
        K.op(V, lambda: nc.vector.memset(hpi[:], math.pi / 2 * (1.0 - 2e-7)), w=[b_hpi])
        one_c = st.enter_context(nc.sbuf_tensor("one_c", [128, 1], F32))
        K.op(V, lambda: nc.vector.memset(one_c[:], 1.0), w=[b_hpi], acc=True)

        def tables(g):
            gp = g % 2
            y, ki, kf, r_ = sc_y[:, 0:576], sc_ki[:, 0:576], sc_kf[:, 0:576], sc_r[:, 0:576]
            bs = [b_prep]
            for d_ in range(2):
                T = g * 2 + d_
                K.op(V, lambda d_=d_, T=T: nc.vector.tensor_scalar(out=sc_y[:, d_ * 288:(d_ + 1) * 288], in0=iot[:, d_, :],
                                                                 scalar1=TH8s[:, T:T + 1], scalar2=None, op0=ALU.mult),
                     r=[b_w, b_ths, b_prep], w=[b_prep], acc=(d_ > 0))
            K.op(A, lambda: nc.scalar.copy(out=ki, in_=y), r=bs, w=bs)
            K.op(A, lambda: nc.scalar.copy(out=kf, in_=ki), r=bs, w=bs)
            K.op(V, lambda: nc.vector.tensor_tensor(out=r_, in0=y, in1=kf, op=ALU.subtract), r=bs, w=bs)
            K.op(A, lambda: nc.scalar.activation(out=tabS[gp], in_=r_.rearrange("p (d n) -> p d n", d=2), func=AF.Sin, scale=SIN_SC),
                 r=bs, w=[b_tab[gp]])
            K.op(A, lambda: nc.scalar.activation(out=kf, in_=r_, func=AF.Abs), r=bs, w=bs)
            K.op(A, lambda: nc.scalar.activation(out=tabC[gp], in_=kf.rearrange("p (d n) -> p d n", d=2), func=AF.Sin, scale=-SIN_SC,
                                                 bias=hpi[:, 0:1]), r=bs + [b_hpi], w=[b_tab[gp]], acc=True)

        lvc = [0]

        def chain(g, b, d_, ut, b_ut):
            gp = g % 2
            T = g * 2 + d_
            off = 0 if d_ == 0 else 32
            l2 = lvc[0] % 2
            lvc[0] += 1
            K.op(P, lambda: nc.tensor.matmul(pD[l2][:, :], lhsT=WD[:, T, :], rhs=ut[:, b, off:off + 288], start=True, stop=True),
                 r=[b_w, b_ut], w=[b_pD[l2]])
            K.op(P, lambda: nc.tensor.matmul(pD2[l2][:, :], lhsT=WD2[:, T, :], rhs=ut[:, b, off:off + 288], start=True, stop=True),
                 r=[b_w, b_ut], w=[b_pD2[l2]])
            K.op(V, lambda: nc.vector.tensor_tensor(out=Wt[l2][:], in0=pD[l2][:, :], in1=tabC[gp][:, d_, :], op=ALU.mult),
                 r=[b_pD[l2], b_tab[gp]], w=[b_Wt[l2]])
            K.op(V, lambda: nc.vector.tensor_tensor(out=Wt2[l2][:], in0=pD2[l2][:, :], in1=tabS[gp][:, d_, :], op=ALU.mult),
                 r=[b_pD2[l2], b_tab[gp]], w=[b_Wt2[l2]])
            K.op(V, lambda: nc.vector.tensor_tensor(out=Wt[l2][:], in0=Wt[l2][:], in1=Wt2[l2][:], op=ALU.add),
                 r=[b_Wt[l2], b_Wt2[l2]], w=[b_Wt[l2]])
            rv = (lambda a_: a_) if d_ == 0 else (lambda a_: a_[:, ::-1])
            K.op(V, lambda: nc.vector.tensor_tensor_scan(out=rv(Zt[l2][:]), data0=Rcol[:, T:T + 1].to_broadcast([128, 288]),
                                                         data1=rv(Wt[l2][:]), initial=0.0, op0=ALU.mult, op1=ALU.add),
                 r=[b_Wt[l2], b_w], w=[b_Zt[l2]])
            K.op(V, lambda: nc.vector.tensor_tensor(
                out=ZCS[gp][b][d_][:], in0=Zt[l2][:].rearrange("p (o n) -> p o n", o=1).to_broadcast([128, 2, 288]),
                in1=tabCS[gp][:, d_, :, :], op=ALU.mult),
                r=[b_Zt[l2], b_tab[gp]], w=[b_ZC[gp][b][d_], b_ZS[gp][b][d_]])

        def ymm_group(g):
            gp = g % 2
            ut, b_ut = UText[gp], b_UText[gp]
            pyt, b_pyt = py[gp], b_py[gp]
            for b in range(NB):
                def ymm():
                    inst = None
                    for cbk in range(2):
                        o_ = pyt[:, b * 2 + cbk, :]
                        nc.tensor.matmul(o_, lhsT=ut[:, b, 32 + cbk * 128:32 + (cbk + 1) * 128], rhs=Toep[:, g, :], start=True, stop=False)
                        for d_ in range(2):
                            T = g * 2 + d_
                            o2 = (31 if d_ == 0 else 1) + cbk * 128
                            nc.tensor.matmul(o_, lhsT=ZC[gp][b][d_][:, o2:o2 + 128], rhs=CWa[:, T, :], start=False, stop=False)
                            inst = nc.tensor.matmul(o_, lhsT=ZS[gp][b][d_][:, o2:o2 + 128], rhs=CWb[:, T, :], start=False, stop=(d_ == 1))
                    return inst
                K.op(P, ymm, r=[b_ut, b_w, b_ZC[gp][b][0], b_ZC[gp][b][1], b_ZS[gp][b][0], b_ZS[gp][b][1]], w=[b_pyt], acc=(b > 0))
            K.op(A, lambda: nc.scalar.copy(
                out=y_sb[:].rearrange("p q i (l h) -> p q i l h", h=16)[:, :, :, g % 8, :],
                in_=pyt[:].rearrange("p q (i h) -> p q i h", h=16)), r=[b_pyt], w=[b_ysb], acc=(g % 8 > 0))

        gcnt = [0]

        def gelu_out(gt_):
            for b in range(NB):
                for cbk in range(2):
                    for ih in range(2):
                        q = gcnt[0] % 2
                        gcnt[0] += 1

                        def try_():
                            inst = None
                            for u in range(4):
                                i_ = ih * 4 + u
                                inst = nc.tensor.transpose(out=pTy[:, u, :], in_=y_sb[:, b * 2 + cbk, i_, :], identity=ident_f[:])
                            return inst
                        K.op(P, try_, r=[b_ysb, b_identf], w=[b_pTy])
                        pf = pTy[:].rearrange("p u c -> p (u c)")
                        K.op(A, lambda: nc.scalar.copy(out=gx[q][:], in_=pf), r=[b_pTy], w=[b_gx[q]])
                        K.op(A, lambda: nc.scalar.activation(out=gt1[q][:], in_=pf, func=AF.Square), r=[b_pTy], w=[b_gt1[q]])
                        K.op(A, lambda: nc.scalar.activation(out=gt1[q][:], in_=gt1[q][:], func=AF.Identity, scale=0.044715, bias=one_c[:, 0:1]),
                             r=[b_gt1[q], b_hpi], w=[b_gt1[q]])
                        K.op(V, lambda: nc.vector.tensor_tensor(out=gt1[q][:], in0=gt1[q][:], in1=gx[q][:], op=ALU.mult),
                             r=[b_gt1[q], b_gx[q]], w=[b_gt1[q]])
                        K.op(A, lambda: nc.scalar.activation(out=gsg[q][:], in_=gt1[q][:], func=AF.Sigmoid, scale=GELU_S),
                             r=[b_gt1[q]], w=[b_gsg[q]])
                        K.op(V, lambda: nc.vector.tensor_tensor(out=ygt[q][:].rearrange("p u c -> p (u c)"), in0=gx[q][:], in1=gsg[q][:],
                                                                op=ALU.mult), r=[b_gx[q], b_gsg[q]], w=[b_ygt[q]])
                        c0 = b * SEQ + ih * 4 * 256
                        dst = ygT_s[gt_ * 128:(gt_ + 1) * 128, c0:c0 + 4 * 256].rearrange(
                            "p (u c) -> p u c", c=256)[:, :, cbk * 128:(cbk + 1) * 128]
                        K.dma(S, dst, ygt[q][:], r=[b_ygt[q]], w=[db("ygT_s")], acc=True)

        tables(0)
        for g in range(32):
            gp = g % 2
            ut, b_ut = UText[gp], b_UText[gp]
            for b in range(NB):
                def mkU():
                    nc.tensor.matmul(pU[:, 0:32], lhsT=Ucg[:, b, g * 128:(g + 1) * 128], rhs=ident_b[:, 0:32], start=True, stop=True)
                    inst = None
                    for cbk in range(2):
                        inst = nc.tensor.matmul(pU[:, 32 + cbk * 128:32 + (cbk + 1) * 128], lhsT=Ug[:, b, cbk, g * 128:(g + 1) * 128],
                                                rhs=ident_b[:, :], start=True, stop=True)
                    return inst
                K.op(P, mkU, r=[b_U, b_identb], w=[b_pU])
                K.op(A, lambda: nc.scalar.copy(out=ut[:, b, 0:288], in_=pU[:, 0:288]), r=[b_pU], w=[b_ut], acc=(b > 0))
                K.op(A, lambda: nc.scalar.copy(out=ut[:, b, 288:320], in_=pU[:, 0:32]), r=[b_pU], w=[b_ut], acc=True)
            chain(g, 0, 0, ut, b_ut)
            chain(g, 0, 1, ut, b_ut)
            if g + 1 < 32:
                tables(g + 1)
            chain(g, 1, 0, ut, b_ut)
            chain(g, 1, 1, ut, b_ut)
            if g >= 1:
                ymm_group(g - 1)
                if (g - 1) % 8 == 7:
                    gelu_out((g - 1) // 8)
        ymm_group(31)
        gelu_out(3)
        K.barrier()
    if debug == 5:
        return nc

    x1_s = dscr("x1_s", [NB, SEQ, D], F32)

    def load_w(dst, src_rows_view, ncols, b_dst):
        for c0 in range(0, ncols, 1024):
            c1 = min(ncols, c0 + 1024)
            K.dma(G, dst[:, :, c0:c1], src_rows_view[:, :, c0:c1], w=[b_dst], acc=True)

    with ExitStack() as st:
        wglu = st.enter_context(nc.sbuf_tensor("wglu", [128, 4, 512], BF16))
        wbra = st.enter_context(nc.sbuf_tensor("wbra", [128, 8, 1024], BF16))
        wbrs = st.enter_context(nc.sbuf_tensor("wbrs", [128, 4, 1024], BF16))
        wout = st.enter_context(nc.sbuf_tensor("wout", [128, 8, 1024], BF16))
        wing = st.enter_context(nc.sbuf_tensor("wing", [128, 8, 2048], BF16))
        bglu = st.enter_context(nc.sbuf_tensor("bglu_sb", [128, 4], F32))
        b_wglu, b_winga, b_wbra, b_wings, b_wbrs, b_wout = (Buf() for _ in range(6))
        winv = win_d.rearrange("(k p) n -> p k n", p=128)
        K.dma(S, bglu[:], bglu_d, w=[b_wglu], acc=True)
        load_w(wglu, wglu_d.rearrange("(k p) n -> p k n", p=128), 512, b_wglu)
        K.dma(G, wing[:, :, 0:1024], winv[:, :, 2048:3072], w=[b_winga], acc=True)
        load_w(wbra, wbra_d.rearrange("(k p) n -> p k n", p=128), 1024, b_wbra)
        K.dma(G, wing[:, :, 1024:2048], winv[:, :, 3072:4096], w=[b_wings], acc=True)
        load_w(wbrs, wbrs_d.rearrange("(k p) n -> p k n", p=128), 1024, b_wbrs)
        load_w(wout, wout_d.rearrange("(k p) n -> p k n", p=128), 1024, b_wout)
        ygb = [st.enter_context(nc.sbuf_tensor("ygb%d" % i, [128, 4, 512], BF16)) for i in range(2)]
        otb = [st.enter_context(nc.sbuf_tensor("otb%d" % i, [128, 8, 512], BF16)) for i in range(2)]
        hb = [st.enter_context(nc.sbuf_tensor("hb%d" % i, [128, 8, 512], BF16)) for i in range(2)]
        y2T = st.enter_context(nc.sbuf_tensor("y2T", [128, 4, 512], BF16))
        mT = st.enter_context(nc.sbuf_tensor("mT", [128, 8, 512], BF16))
        m1s = st.enter_context(nc.sbuf_tensor("m1s", [128, 8, 512], F32))
        sg = [st.enter_context(nc.sbuf_tensor("sg%d" % i, [128, 512], F32)) for i in range(2)]
        m2 = [st.enter_context(nc.sbuf_tensor("m2%d" % i, [128, 512], F32)) for i in range(2)]
        xr_ = [st.enter_context(nc.sbuf_tensor("xr%d" % i, [128, D], F32)) for i in range(4)]
        xo = [st.enter_context(nc.sbuf_tensor("xo%d" % i, [128, D], F32)) for i in range(2)]
        accA = [st.enter_context(nc.psum_tensor("accA%d" % i, [128, 512], F32)) for i in range(2)]
        accB = [st.enter_context(nc.psum_tensor("accB%d" % i, [128, 512], F32)) for i in range(2)]
        px1 = [st.enter_context(nc.psum_tensor("px1%d" % i, [128, 512], F32)) for i in range(2)]
        b_ygb, b_otb, b_hb, b_sg, b_m2, b_xo, b_accA, b_accB, b_px1 = ([Buf(), Buf()] for _ in range(9))
        b_xr = [Buf() for _ in range(4)]
        b_y2T, b_mT, b_m1s = Buf(), Buf(), Buf()
        tix = 0
        stp = 0

        def load_blk(blk):
            q = blk % 2
            c0 = blk * 512
            K.dma(A, ygb[q][:], ygT_s.rearrange("(k p) n -> p k n", p=128)[:, :, c0:c0 + 512], r=[db("ygT_s")], w=[b_ygb[q]])
            K.dma(A, otb[q][:], OT_s.rearrange("(k p) n -> p k n", p=128)[:, :, c0:c0 + 512], r=[db("OT_s")], w=[b_otb[q]])
            K.dma(A, hb[q][:], hT_s.rearrange("(k p) n -> p k n", p=128)[:, :, c0:c0 + 512], r=[db("hT_s")], w=[b_hb[q]])

        def load_xr(blk):
            b = blk // 4
            c0 = blk * 512
            for tt_ in range(4):
                cc = (c0 % SEQ) + tt_ * 128
                i_, cb_ = cc // 256, (cc % 256) // 128
                K.dma(A, xr_[tt_][:], x_d[b].rearrange("(c i) d -> i c d", i=8)[i_, cb_ * 128:(cb_ + 1) * 128, :], w=[b_xr[tt_]])

        def acc_mm(dst, wt, nk, c0_, rhs_t):
            def f():
                inst = None
                for k in range(nk):
                    inst = nc.tensor.matmul(dst[:, :], lhsT=wt[:, k, c0_:c0_ + 128], rhs=rhs_t[:, k, :], start=(k == 0), stop=(k == nk - 1))
                return inst
            return f
        load_blk(0)
        for blk in range(NL // 512):
            b = blk // 4
            q = blk % 2
            c0 = blk * 512
            if blk + 1 < NL // 512:
                load_blk(blk + 1)
            load_xr(blk)
            for nt in range(4):
                w2 = stp % 2
                stp += 1
                K.op(P, acc_mm(accA[w2], wglu, 4, nt * 128, ygb[q]), r=[b_wglu, b_ygb[q]], w=[b_accA[w2]])
                K.op(A, lambda: nc.scalar.activation(out=sg[w2][:], in_=accA[w2][:, :], func=AF.Sigmoid, bias=bglu[:, nt:nt + 1]),
                     r=[b_accA[w2], b_wglu], w=[b_sg[w2]])
                K.op(V, lambda: nc.vector.tensor_tensor(out=y2T[:, nt, :], in0=ygb[q][:, nt, :], in1=sg[w2][:], op=ALU.mult),
                     r=[b_ygb[q], b_sg[w2]], w=[b_y2T], acc=(nt > 0))
            for nt in range(8):
                w2 = stp % 2
                stp += 1
                K.op(P, acc_mm(accA[w2], wing, 8, nt * 128, hb[q]), r=[b_winga, b_hb[q]], w=[b_accA[w2]])
                K.op(P, acc_mm(accB[w2], wbra, 8, nt * 128, otb[q]), r=[b_wbra, b_otb[q]], w=[b_accB[w2]])
                K.op(A, lambda: nc.scalar.activation(out=sg[w2][:], in_=accA[w2][:, :], func=AF.Sigmoid), r=[b_accA[w2]], w=[b_sg[w2]])
                K.op(V, lambda: nc.vector.tensor_tensor(out=m1s[:, nt, :], in0=sg[w2][:], in1=accB[w2][:, :], op=ALU.mult),
                     r=[b_sg[w2], b_accB[w2]], w=[b_m1s], acc=(nt > 0))
            for nt in range(8):
                w2 = stp % 2
                stp += 1
                K.op(P, acc_mm(accA[w2], wing, 8, 1024 + nt * 128, hb[q]), r=[b_wings, b_hb[q]], w=[b_accA[w2]])
                K.op(P, acc_mm(accB[w2], wbrs, 4, nt * 128, y2T), r=[b_wbrs, b_y2T], w=[b_accB[w2]])
                K.op(A, lambda: nc.scalar.activation(out=sg[w2][:], in_=accA[w2][:, :], func=AF.Sigmoid), r=[b_accA[w2]], w=[b_sg[w2]])
                K.op(V, lambda: nc.vector.tensor_tensor(out=m2[w2][:], in0=sg[w2][:], in1=accB[w2][:, :], op=ALU.mult),
                     r=[b_sg[w2], b_accB[w2]], w=[b_m2[w2]])
                K.op(V, lambda: nc.vector.tensor_tensor(out=mT[:, nt, :], in0=m1s[:, nt, :], in1=m2[w2][:], op=ALU.add),
                     r=[b_m1s, b_m2[w2]], w=[b_mT], acc=(nt > 0))
            for tt_ in range(4):
                cc = (c0 % SEQ) + tt_ * 128
                i_, cb_ = cc // 256, (cc % 256) // 128
                w2 = tix % 2
                tix += 1
                rows = lambda dten: dten[b].rearrange("(c i) d -> i c d", i=8)[i_, cb_ * 128:(cb_ + 1) * 128, :]
                for hf in range(2):
                    def mmo():
                        inst = None
                        for k in range(8):
                            inst = nc.tensor.matmul(px1[hf][:, :], lhsT=mT[:, k, tt_ * 128:(tt_ + 1) * 128],
                                                    rhs=wout[:, k, hf * 512:(hf + 1) * 512], start=(k == 0), stop=(k == 7))
                        return inst
                    K.op(P, mmo, r=[b_wout, b_mT], w=[b_px1[hf]])
                    hs_ = slice(hf * 512, (hf + 1) * 512)
                    K.op(V, lambda: nc.vector.tensor_tensor(out=xo[w2][:, hs_], in0=px1[hf][:, :], in1=Gb[:, 0, b, hs_], op=ALU.mult),
                         r=[b_px1[hf], b_Gb], w=[b_xo[w2]], acc=(hf > 0))
                K.op(V, lambda: nc.vector.tensor_tensor(out=xo[w2][:], in0=xo[w2][:], in1=xr_[tt_][:], op=ALU.add),
                     r=[b_xo[w2], b_xr[tt_]], w=[b_xo[w2]])
                K.dma(S, rows(x1_s), xo[w2][:], r=[b_xo[w2]], w=[db("x1_s")], acc=True)
        K.barrier()
    if debug == 6:
        return nc

    h2T_s = dscr("h2T_s", [D, NL], BF16)
    part_s = dscr("part_s", [NL, D], F32)
    with ExitStack() as st:
        w1h = [st.enter_context(nc.sbuf_tensor("w1h%d" % i, [128, 8, 2048], BF16)) for i in range(2)]
        w2h0 = st.enter_context(nc.sbuf_tensor("w2h0", [128, 16, 1024], BF16))
        w2h = [w2h0, w2h0]
        b_w1 = [[Buf() for _ in range(4)] for _ in range(2)]
        b_w2_0 = Buf()
        b_w2 = [b_w2_0, b_w2_0]
        b1c = st.enter_context(nc.sbuf_tensor("b1c_sb", [128, 32], F32))
        b2r = st.enter_context(nc.sbuf_tensor("b2r_sb", [1, D], F32))
        ones1 = st.enter_context(nc.sbuf_tensor("ones1", [1, 128], F32))
        b_cst = Buf()
        K.dma(S, b1c[:], b1c_d, w=[b_cst], acc=True)
        K.dma(S, b2r[:], b2r_d, w=[b_cst], acc=True)
        K.op(V, lambda: nc.vector.memset(ones1[:], 1.0), w=[b_cst], acc=True)
        h2b = [st.enter_context(nc.sbuf_tensor("h2b%d" % i, [128, 8, 512], BF16)) for i in range(2)]
        hid = st.enter_context(nc.sbuf_tensor("hid", [128, 16, 512], BF16))
        rl = [st.enter_context(nc.sbuf_tensor("rl%d" % i, [128, 512], F32)) for i in range(2)]
        ph = [st.enter_context(nc.psum_tensor("ph%d" % i, [128, 512], F32)) for i in range(2)]
        po = [st.enter_context(nc.psum_tensor("po%d" % i, [128, D], F32)) for i in range(2)]
        b_h2b, b_rl, b_ph, b_po = ([Buf(), Buf()] for _ in range(4))
        b_hid = Buf()
        env = None
        pa_ = None
        xa = [st.enter_context(nc.sbuf_tensor("xa%d" % i, [128, D], F32)) for i in range(2)] * 2
        pb_ = [st.enter_context(nc.sbuf_tensor("pb%d" % i, [128, D], F32)) for i in range(2)] * 2
        b_xa = [Buf(), Buf()] * 2
        b_pa = [Buf() for _ in range(4)]
        w1v = w1_d.rearrange("(k p) n -> p k n", p=128)
        jn = 0
        jn_box = [0]
        tix = 0

        def load_x1(n_):
            b_ = (n_ * 128) // SEQ
            cc = (n_ * 128) % SEQ
            i_, cb_ = cc // 256, (cc % 256) // 128
            K.dma(A, env["xt"][n_ % 3][:], x1_s[b_].rearrange("(c i) d -> i c d", i=8)[i_, cb_ * 128:(cb_ + 1) * 128, :],
                  r=[db("x1_s")], w=[env["b_xt"][n_ % 3]])

        def load_h2(blk_):
            K.dma(A, h2b[blk_ % 2][:], h2T_s.rearrange("(k p) n -> p k n", p=128)[:, :, blk_ * 512:(blk_ + 1) * 512],
                  r=[db("h2T_s")], w=[b_h2b[blk_ % 2]])
        def load_mlp_w1(hf):
            for cblk in range(4):
                K.dma(G, w1h[hf][:, :, cblk * 512:(cblk + 1) * 512], w1v[:, :, hf * 2048 + cblk * 512:hf * 2048 + (cblk + 1) * 512],
                      w=[b_w1[hf][cblk]], acc=True)

        def load_mlp_w2(hf):
            w2v = w2_d[hf * 2048:(hf + 1) * 2048, :].rearrange("(k p) n -> p k n", p=128)
            for kh in range(2):
                K.dma(G, w2h[hf][:, kh * 8:(kh + 1) * 8, :], w2v[:, kh * 8:(kh + 1) * 8, :], w=[b_w2[hf]], acc=(kh > 0))
        load_mlp_w1(0)
        load_mlp_w2(0)
        for hf in range(2):
            hst = ExitStack()
            if hf == 0:
                env = mk_norm_env(hst)
                env["xn_on_dve"] = True
                pa_ = [hst.enter_context(nc.sbuf_tensor("pa0", [128, D], F32))] * 4
                b_pa = [Buf()] * 4
                load_x1(0)
                load_x1(1)
                load_mlp_w1(1)
            else:
                load_mlp_w2(1)
                pa_ = pb_
                b_pa = [Buf(), Buf()] * 2
            for blk in range(NL // 512):
                b = blk // 4
                q = blk % 2
                c0 = blk * 512
                tiles_ = []
                for tt_ in range(4):
                    cc = (c0 % SEQ) + tt_ * 128
                    tiles_.append((cc // 256, (cc % 256) // 128))
                rows = lambda dten, i_, cb_: dten[b].rearrange("(c i) d -> i c d", i=8)[i_, cb_ * 128:(cb_ + 1) * 128, :]
                def norm_tile_1(blk_, tt2):
                    n_ = blk_ * 4 + tt2
                    if n_ + 2 < NL // 128:
                        load_x1(n_ + 2)
                    norm_T1(env, n_)

                def norm_tile_2(blk_, tt2):
                    n_ = blk_ * 4 + tt2
                    norm_T2a(env, n_)

                def norm_tile_3(blk_, tt2):
                    n_ = blk_ * 4 + tt2
                    b_ = blk_ // 4
                    q_ = blk_ % 2
                    pT, b_pT = env["pT"], env["b_pT"]
                    for k in range(8):
                        K.op(V, lambda k=k: nc.vector.tensor_scalar(
                            out=h2b[q_][:, k, tt2 * 128:(tt2 + 1) * 128], in0=pT[:, k, :], scalar1=A2[:, k, b_:b_ + 1],
                            scalar2=modcol[:, k, 2, b_:b_ + 1], op0=ALU.mult, op1=ALU.add),
                            r=[b_pT, b_A, b_modcol], w=[b_h2b[q_]], acc=(tt2 > 0 or k > 0))
                    if tt2 == 3:
                        K.dma(S, h2T_s.rearrange("(k p) n -> p k n", p=128)[:, :, blk_ * 512:(blk_ + 1) * 512], h2b[q_][:],
                              r=[b_h2b[q_]], w=[db("h2T_s")], acc=True)

                def norm_blk(blk_):
                    for tt2 in range(4):
                        norm_tile_1(blk_, tt2)
                        norm_tile_2(blk_, tt2)
                        norm_tile_3(blk_, tt2)
                if hf == 0:
                    if blk == 0:
                        norm_blk(0)
                else:
                    if blk == 0:
                        load_h2(0)
                    if blk + 1 < NL // 512:
                        load_h2(blk + 1)
                    def load_tail(tt2):
                        i2, cb2 = tiles_[tt2]
                        K.dma(A, pa_[tt2][:], part_s[c0 + tt2 * 128:c0 + (tt2 + 1) * 128, :], r=[db("part_s")], w=[b_pa[tt2]])
                        K.dma(A, xa[tt2][:], rows(x1_s, i2, cb2), r=[db("x1_s")], w=[b_xa[tt2]])
                    load_tail(0)
                    load_tail(1)
                for nt in range(16):
                    w2 = nt % 2

                    def mmh():
                        inst = None
                        for k in range(8):
                            inst = nc.tensor.matmul(ph[w2][:, :], lhsT=w1h[hf][:, k, nt * 128:(nt + 1) * 128], rhs=h2b[q][:, k, :],
                                                    start=(k == 0), stop=(k == 7))
                        return inst
                    K.op(P, mmh, r=[b_w1[hf][nt // 4], b_h2b[q]], w=[b_ph[w2]])
                    K.op(A, lambda: nc.scalar.activation(out=rl[w2][:], in_=ph[w2][:, :], func=AF.Relu,
                                                         bias=b1c[:, hf * 16 + nt:hf * 16 + nt + 1]), r=[b_ph[w2], b_cst], w=[b_rl[w2]])
                    K.op(V, lambda: nc.vector.tensor_tensor(out=hid[:, nt, :], in0=rl[w2][:], in1=rl[w2][:], op=ALU.mult),
                         r=[b_rl[w2]], w=[b_hid], acc=(nt > 0))
                nxt_norm = hf == 0 and blk + 1 < NL // 512
                for tt_, (i_, cb_) in enumerate(tiles_):
                    w2 = tix % 2
                    tix += 1
                    prow = part_s[c0 + tt_ * 128:c0 + (tt_ + 1) * 128, :]
                    if nxt_norm:
                        norm_tile_1(blk + 1, tt_)

                    def mmo2():
                        inst = None
                        for h2_ in range(2):
                            for k in range(16):
                                inst = nc.tensor.matmul(po[w2][:, h2_ * 512:(h2_ + 1) * 512], lhsT=hid[:, k, tt_ * 128:(tt_ + 1) * 128],
                                                        rhs=w2h[hf][:, k, h2_ * 512:(h2_ + 1) * 512], start=(k == 0), stop=(k == 15 and hf == 0))
                            if hf == 1:
                                inst = nc.tensor.matmul(po[w2][:, h2_ * 512:(h2_ + 1) * 512], lhsT=ones1[0:1, :],
                                                        rhs=b2r[0:1, h2_ * 512:(h2_ + 1) * 512], start=False, stop=True)
                        return inst
                    K.op(P, mmo2, r=[b_w2[hf], b_hid, b_cst], w=[b_po[w2]])
                    if nxt_norm:
                        norm_tile_2(blk + 1, tt_)
                    if hf == 0:
                        K.op(A, lambda: nc.scalar.copy(out=pa_[tt_][:], in_=po[w2][:, :]), r=[b_po[w2]], w=[b_pa[tt_]])
                        K.dma(S, prow, pa_[tt_][:], r=[b_pa[tt_]], w=[db("part_s")], acc=True)
                        if nxt_norm:
                            norm_tile_3(blk + 1, tt_)
                    else:
                        K.op(V, lambda: nc.vector.tensor_tensor(out=pa_[tt_][:], in0=po[w2][:, :], in1=pa_[tt_][:], op=ALU.add),
                             r=[b_po[w2], b_pa[tt_]], w=[b_pa[tt_]])
                        K.op(V, lambda: nc.vector.tensor_tensor(out=pa_[tt_][:], in0=pa_[tt_][:], in1=Gb[:, 1, b, :], op=ALU.mult),
                             r=[b_pa[tt_], b_Gb], w=[b_pa[tt_]])
                        K.op(V, lambda: nc.vector.tensor_tensor(out=xa[tt_][:], in0=xa[tt_][:], in1=pa_[tt_][:], op=ALU.add),
                             r=[b_pa[tt_], b_xa[tt_]], w=[b_xa[tt_]])
                        K.dma(S, rows(out_d, i_, cb_), xa[tt_][:], r=[b_xa[tt_]], w=[db("out")], acc=True)
                        if tt_ + 2 < 4:
                            load_tail(tt_ + 2)
            if hf == 1:
                K.barrier()
            hst.close()

    K.barrier()
    return nc


def _rope_tables():
    half = 16
    inv_freq = (np.float32(10000.0) ** (-np.arange(half, dtype=np.float32) / np.float32(half))).astype(np.float32)
    C = np.zeros((128, 16, 64), np.float32)
    Sg = np.zeros((128, 16, 64), np.float32)
    p = np.arange(128)
    for i in range(8):
        for cb in range(2):
            t = 8 * (128 * cb + p) + i
            row = (t // 64).astype(np.float32)
            col = (t % 64).astype(np.float32)
            ar = row[:, None] * inv_freq[None, :]
            ac = col[:, None] * inv_freq[None, :]
            cr, sr = np.cos(ar).astype(np.float32), np.sin(ar).astype(np.float32)
            cc, sc = np.cos(ac).astype(np.float32), np.sin(ac).astype(np.float32)
            j = i * 2 + cb
            C[:, j, 0:16] = cr
            C[:, j, 16:32] = cr
            C[:, j, 32:48] = cc
            C[:, j, 48:64] = cc
            Sg[:, j, 0:16] = -sr
            Sg[:, j, 16:32] = sr
            Sg[:, j, 32:48] = -sc
            Sg[:, j, 48:64] = sc
    return C, Sg


_PARTNER = np.concatenate([np.arange(16, 32), np.arange(0, 16), np.arange(48, 64), np.arange(32, 48)])


def make_in_maps(inp):
    f = lambda a: np.ascontiguousarray(np.asarray(a, dtype=np.float32))
    x = f(inp["x"]); c = f(inp["c"]); ctx = f(inp["ctx"]); c_ctx = f(inp["c_ctx"])
    ropeC, ropeS = _rope_tables()
    col = lambda v: np.ascontiguousarray(v.reshape(8, 128).T)
    qg = f(inp["q_norm_g"])[0]
    kg = f(inp["k_norm_g"])[0]
    qg2 = np.ascontiguousarray(np.broadcast_to(np.stack([qg, qg[_PARTNER]])[None], (128, 2, 64)))
    kg2 = np.ascontiguousarray(np.broadcast_to(np.stack([kg, kg[_PARTNER]])[None], (128, 2, 64)))
    dup = lambda a: np.ascontiguousarray(np.concatenate([a, a], 0))
    tl = lambda a: dup(f(a)[0].transpose(2, 1, 0).reshape(64, 64))
    lamr, lami = tl(inp["ssm_lambda_re"]), tl(inp["ssm_lambda_im"])
    ldt = np.ascontiguousarray(np.broadcast_to(f(inp["ssm_log_dt"])[0].T.reshape(1, 64), (128, 64)))
    bt = lambda a: dup(f(a)[0].transpose(2, 1, 0, 3).reshape(64, 64, 16))
    ct = lambda a: dup(f(a)[0].transpose(3, 1, 0, 2).reshape(64, 64, 16))
    dvec = f(inp["ssm_d"])[0].reshape(32, 16)
    dcol = np.ascontiguousarray(np.broadcast_to(dvec.T[None], (8, 16, 32)).reshape(128, 32))
    ii = np.arange(128) // 16
    maskf = (ii[None, :] >= ii[:, None]).astype(np.float32)
    maskb = (ii[None, :] <= ii[:, None]).astype(np.float32)
    ar = np.arange(288, dtype=np.float32)
    iota2 = np.ascontiguousarray(np.broadcast_to(np.stack([ar, 287 - ar])[None], (128, 2, 288)))
    shared = {
        "lamr": lamr, "lami": lami, "ldt": ldt, "bre": bt(inp["ssm_b_re"]), "bim": bt(inp["ssm_b_im"]),
        "cre": ct(inp["ssm_c_re"]), "cim": ct(inp["ssm_c_im"]), "dcol": dcol, "maskf": maskf, "maskb": maskb,
        "kv": np.ascontiguousarray(np.broadcast_to(np.arange(9, dtype=np.float32)[None], (128, 9))),
        "w_glu": f(inp["w_glu"])[0], "bglu": np.ascontiguousarray(f(inp["b_glu"])[0].reshape(4, 128).T),
        "w_br_attn": f(inp["w_br_attn"])[0], "w_br_ssm": f(inp["w_br_ssm"])[0], "w_out": f(inp["w_out"])[0],
        "w_mlp1": f(inp["w_mlp1"])[0], "b1c": np.ascontiguousarray(f(inp["b_mlp1"])[0].reshape(32, 128).T),
        "w_mlp2": f(inp["w_mlp2"])[0], "b2r": f(inp["b_mlp2"])[0][None, :],
        "iota2": iota2, "j128": np.ascontiguousarray(np.eye(128, dtype=np.float32)[::-1]),
        "w_mod": f(inp["w_mod"])[0], "b_mod": f(inp["b_mod"])[0][None, :],
        "n1g": col(f(inp["norm1_g"])[0]), "n2g": col(f(inp["norm2_g"])[0]),
        "w_in": f(inp["w_in"])[0], "qg": qg2, "kg": kg2,
        "ropeC": ropeC, "ropeS": ropeS, "ident": np.eye(128, dtype=np.float32),
        "esel": np.concatenate([np.zeros((64, 128), np.float32), np.ones((1, 128), np.float32), np.zeros((63, 128), np.float32)], 0),
        "sel": np.ascontiguousarray(np.broadcast_to(np.eye(3, dtype=np.float32)[:, :NB, None], (3, NB, 128))),
    }
    maps = []
    for core in range(NCORES):
        b0 = core * NB
        cv = np.stack([c[b0], c[b0 + 1], c_ctx])
        cT = np.ascontiguousarray(cv.reshape(3, 8, 128).transpose(2, 1, 0))
        m = dict(shared)
        m["x"] = x[b0:b0 + NB]
        m["ctx"] = ctx[b0:b0 + NB]
        m["cT"] = cT
        maps.append(m)
    return maps


def kernel(**inputs):
    nc = build_nc(0)
    maps = make_in_maps(inputs)
    res = run_bass_kernel_spmd(nc, maps, core_ids=list(range(NCORES)))
    return np.concatenate([r["out"] for r in res.results], axis=0)
```

```python
import math
import threading
from contextlib import ExitStack

import numpy as np
import concourse.bass as bass
import concourse.mybir as mybir
from concourse.bass_utils import run_bass_kernel_spmd

F32 = mybir.dt.float32
BF16 = mybir.dt.bfloat16
I32 = mybir.dt.int32
AF = mybir.ActivationFunctionType
ALU = mybir.AluOpType
AX = mybir.AxisListType

D = 1024
SEQ = 2048
CTXL = 256
NB = 2
NCORES = 8
EPS = 1e-6
NL = NB * SEQ
NCX = NB * CTXL
KEYS = SEQ + CTXL


class Buf:
    __slots__ = ("name", "writes", "reads", "wbar")

    def __init__(self, name=""):
        self.name = name
        self.writes = {}
        self.reads = {}
        self.wbar = {}


class Eng:
    def __init__(self, K, name, h, nring):
        self.K = K
        self.name = name
        self.h = h
        self.sem = K.new_sem("s_" + name)
        self.count = 0
        self.seen = {}
        self.ring = [[K.new_sem("d_%s%d" % (name, i)), 0] for i in range(nring)]
        self.rpos = 0

    def wait(self, ev):
        sem, val = ev
        k = id(sem)
        if self.seen.get(k, 0) >= val:
            return
        self.h.wait_ge(sem, val)
        self.seen[k] = val


class Interleaver:
    def __init__(self, fn):
        self.go = threading.Semaphore(0)
        self.back = threading.Semaphore(0)
        self.finished = False
        self.started = False
        self.err = None
        self.t = threading.Thread(target=self._run, args=(fn,), daemon=True)

    def _run(self, fn):
        self.go.acquire()
        try:
            fn()
        except BaseException as e:
            self.err = e
        self.finished = True
        self.back.release()

    def pause(self):
        self.back.release()
        self.go.acquire()

    def step(self, n=1):
        for _ in range(n):
            if self.finished:
                break
            if not self.started:
                self.started = True
                self.t.start()
            self.go.release()
            self.back.acquire()
        if self.err is not None:
            raise self.err

    def finish(self):
        while not self.finished:
            self.step(1)
        if self.err is not None:
            raise self.err


class Kern:
    def __init__(self, nc):
        self.nc = nc
        self.co = None
        self.es = ExitStack()
        self.nsem = 0
        self.pe = Eng(self, "pe", nc.tensor, 0)
        self.act = Eng(self, "act", nc.scalar, 6)
        self.dve = Eng(self, "dve", nc.vector, 0)
        self.pool = Eng(self, "pool", nc.gpsimd, 8)
        self.sp = Eng(self, "sp", nc.sync, 12)
        self.engs = [self.pe, self.act, self.dve, self.pool, self.sp]
        self.dma_events = []

    def new_sem(self, name):
        self.nsem += 1
        return self.es.enter_context(self.nc.semaphore(name))

    def sb(self, name, shape, dt):
        return self.es.enter_context(self.nc.sbuf_tensor(name, shape, dt))

    def _deps(self, r, w, acc):
        deps = {}
        for b in r:
            for k, ev in b.writes.items():
                if deps.get(k, (None, 0))[1] < ev[1]:
                    deps[k] = ev
        for b in w:
            for k, ev in b.reads.items():
                if deps.get(k, (None, 0))[1] < ev[1]:
                    deps[k] = ev
            if not acc:
                for k, ev in b.writes.items():
                    if deps.get(k, (None, 0))[1] < ev[1]:
                        deps[k] = ev
            else:
                for k, ev in b.wbar.items():
                    if deps.get(k, (None, 0))[1] < ev[1]:
                        deps[k] = ev
        return deps

    def _record(self, ev, r, w, acc):
        k = id(ev[0])
        for b in r:
            b.reads[k] = ev
        for b in w:
            if not acc:
                wb = dict(b.reads)
                for kk, e2 in b.writes.items():
                    if wb.get(kk, (None, 0))[1] < e2[1]:
                        wb[kk] = e2
                b.wbar = wb
                b.writes = {}
                b.reads = {}
            b.writes[k] = ev

    def op(self, eng, fn, r=(), w=(), acc=False):
        for ev in self._deps(r, w, acc).values():
            eng.wait(ev)
        inst = fn()
        eng.count += 1
        inst.then_inc(eng.sem, 1)
        ev = (eng.sem, eng.count)
        eng.seen[id(eng.sem)] = max(eng.seen.get(id(eng.sem), 0), 0)
        self._record(ev, r, w, acc)
        self._maybe_pause()
        return ev

    def _maybe_pause(self):
        co = self.co
        if co is not None and co.started and not co.finished and threading.current_thread() is co.t and getattr(co, "armed", False):
            co.pause()

    def dma(self, eng, out, in_, r=(), w=(), acc=False, **kw):
        for ev in self._deps(r, w, acc).values():
            eng.wait(ev)
        slot = eng.ring[eng.rpos]
        eng.rpos = (eng.rpos + 1) % len(eng.ring)
        if slot[1] > 0:
            eng.wait((slot[0], slot[1]))
        slot[1] += 16
        eng.h.dma_start(out=out, in_=in_, **kw).then_inc(slot[0], 16)
        ev = (slot[0], slot[1])
        self._record(ev, r, w, acc)
        self.dma_events.append(ev)
        self._maybe_pause()
        return ev

    def barrier(self):
        evs = [(e.sem, e.count) for e in self.engs if e.count > 0]
        for e in self.engs:
            for s in e.ring:
                if s[1] > 0:
                    evs.append((s[0], s[1]))
        for e in self.engs:
            for ev in evs:
                if ev[0] is e.sem:
                    continue
                e.wait(ev)


def tt(eng, out, in0, in1, op):
    return eng.h.tensor_tensor(out=out, in0=in0, in1=in1, op=op)


def build_nc(debug=0):
    nc = bass.Bass("TRN2", target_bir_lowering=False)
    K = Kern(nc)
    P = K.pe
    A = K.act
    V = K.dve
    G = K.pool
    S = K.sp

    def din(name, shape, dt=F32):
        return nc.dram_tensor(name, list(shape), dt, kind="ExternalInput").ap()

    def dscr(name, shape, dt):
        kind = "ExternalOutput" if debug else "Internal"
        return nc.dram_tensor(name, list(shape), dt, kind=kind).ap()

    x_d = din("x", [NB, SEQ, D])
    ctx_d = din("ctx", [NB, CTXL, D])
    cT_d = din("cT", [128, 8, 3])
    wmod_d = din("w_mod", [D, 6 * D])
    bmod_d = din("b_mod", [1, 6 * D])
    n1g_d = din("n1g", [128, 8])
    n2g_d = din("n2g", [128, 8])
    win_d = din("w_in", [D, 4096])
    qg_d = din("qg", [128, 2, 64])
    kg_d = din("kg", [128, 2, 64])
    ropeC_d = din("ropeC", [128, 16, 64])
    ropeS_d = din("ropeS", [128, 16, 64])
    ident_d = din("ident", [128, 128])
    sel_d = din("sel", [3, NB, 128])
    esel_d = din("esel", [128, 128])
    lamr_d, lami_d, ldt_d = din("lamr", [128, 64]), din("lami", [128, 64]), din("ldt", [128, 64])
    bre_d, bim_d = din("bre", [128, 64, 16]), din("bim", [128, 64, 16])
    cre_d, cim_d = din("cre", [128, 64, 16]), din("cim", [128, 64, 16])
    dcol_d = din("dcol", [128, 32])
    maskf_d, maskb_d = din("maskf", [128, 128]), din("maskb", [128, 128])
    kv_d = din("kv", [128, 9])
    iota_d = din("iota2", [128, 2, 288])
    j128_d = din("j128", [128, 128])
    wglu_d = din("w_glu", [512, 512])
    bglu_d = din("bglu", [128, 4])
    wbra_d = din("w_br_attn", [D, D])
    wbrs_d = din("w_br_ssm", [512, D])
    wout_d = din("w_out", [D, D])
    w1_d = din("w_mlp1", [D, 4096])
    b1c_d = din("b1c", [128, 32])
    w2_d = din("w_mlp2", [4096, D])
    b2r_d = din("b2r", [1, D])
    out_d = nc.dram_tensor("out", [NB, SEQ, D], F32, kind="ExternalOutput").ap()

    hT_s = dscr("hT_s", [D, NL], BF16)
    qT_s = dscr("qT_s", [8, 128, NL], BF16)
    kT_s = dscr("kT_s", [2, 128, NB * KEYS], BF16)
    v_s = dscr("v_s", [NB * KEYS, 256], BF16)
    ul_s = dscr("ul_s", [NB, 256, 8, 512], BF16)
    uc_s = dscr("uc_s", [NB, 32, 8, 512], BF16)
    mod_s = dscr("mod_s", [3, 6 * D], F32)
    dram_bufs = {}

    def db(name):
        if name not in dram_bufs:
            dram_bufs[name] = Buf(name)
        return dram_bufs[name]

    ident_f = K.sb("ident_f", [128, 128], F32)
    ident_b = K.sb("ident_b", [128, 128], BF16)
    b_identf = Buf("identf")
    b_identb = Buf("identb")
    K.dma(S, ident_f[:], ident_d, w=[b_identf])
    K.op(V, lambda: nc.vector.tensor_copy(out=ident_b[:], in_=ident_f[:]), r=[b_identf], w=[b_identb])

    modcol = K.sb("modcol", [128, 8, 4, 3], F32)
    Gb = K.sb("Gb", [128, 2, NB, D], F32)
    b_Gb = Buf("Gb")
    A1 = K.sb("A1", [128, 8, 3], F32)
    A2 = K.sb("A2", [128, 8, 3], F32)
    b_modrows = Buf("modrows")
    b_modcol = Buf("modcol")
    b_A = Buf("A12")
    win_st = ExitStack()
    win_sb = win_st.enter_context(nc.sbuf_tensor("win_sb", [128, 8, 2048], BF16))
    b_win = Buf()
    win_v = win_d.rearrange("(k p) n -> p k n", p=128)
    for hf in range(2):
        K.dma(G, win_sb[:, :, hf * 1024:(hf + 1) * 1024], win_v[:, :, hf * 1024:(hf + 1) * 1024],
              w=[b_win], acc=True)

    with ExitStack() as st:
        modrows = st.enter_context(nc.sbuf_tensor("modrows", [3, 6 * D], F32))
        sel = st.enter_context(nc.sbuf_tensor("sel_sb", [3, NB, 128], F32))
        b_sel = Buf()
        K.dma(S, sel[:], sel_d, w=[b_sel])
        cT = st.enter_context(nc.sbuf_tensor("cT_sb", [128, 8, 3], F32))
        scT = st.enter_context(nc.sbuf_tensor("scT", [128, 8, 3], F32))
        ones13 = st.enter_context(nc.sbuf_tensor("ones13", [1, 4], F32))
        bmod = st.enter_context(nc.sbuf_tensor("bmod", [1, 6 * D], F32))
        n1g = st.enter_context(nc.sbuf_tensor("n1g_sb", [128, 8], F32))
        n2g = st.enter_context(nc.sbuf_tensor("n2g_sb", [128, 8], F32))
        wm = [st.enter_context(nc.sbuf_tensor("wm%d" % i, [128, 8, 512], F32)) for i in range(2)]
        pm = [st.enter_context(nc.psum_tensor("pm%d" % i, [128, 512], F32)) for i in range(2)]
        ptc = st.enter_context(nc.psum_tensor("ptc", [128, 32, 4], F32))
        b_cT, b_scT, b_ones, b_bmod, b_ng = Buf(), Buf(), Buf(), Buf(), Buf()
        b_wm = [Buf(), Buf()]
        b_pm = [Buf(), Buf()]
        b_ptc = Buf()
        K.dma(S, cT[:], cT_d, w=[b_cT])
        K.dma(S, bmod[:], bmod_d, w=[b_bmod])
        K.dma(S, n1g[:], n1g_d, w=[b_ng], acc=True)
        K.dma(S, n2g[:], n2g_d, w=[b_ng], acc=True)
        K.op(A, lambda: nc.scalar.activation(out=scT[:], in_=cT[:], func=AF.Silu), r=[b_cT], w=[b_scT])
        K.op(V, lambda: nc.vector.memset(ones13[:], 1.0), w=[b_ones])
        wmod_v = wmod_d.rearrange("(k p) n -> p k n", p=128)
        for blk in range(12):
            wb = wm[blk % 2]
            K.dma(S, wb[:], wmod_v[:, :, blk * 512:(blk + 1) * 512], w=[b_wm[blk % 2]])

            def mm(blk=blk, wb=wb):
                for k in range(8):
                    nc.tensor.matmul(pm[blk % 2][0:3, :], lhsT=scT[:, k, :], rhs=wb[:, k, :],
                                     start=(k == 0), stop=False)
                return nc.tensor.matmul(pm[blk % 2][0:3, :], lhsT=ones13[0:1, 0:3],
                                        rhs=bmod[0:1, blk * 512:(blk + 1) * 512],
                                        start=False, stop=True)
            K.op(P, mm, r=[b_scT, b_wm[blk % 2], b_ones, b_bmod], w=[b_pm[blk % 2]])
            K.op(V, lambda blk=blk: nc.vector.tensor_copy(out=modrows[:, blk * 512:(blk + 1) * 512],
                                                          in_=pm[blk % 2][0:3, :]),
                 r=[b_pm[blk % 2]], w=[b_modrows], acc=True)
        def tr():
            inst = None
            for j, ch in enumerate((0, 1, 3, 4)):
                for k in range(8):
                    c0 = ch * D + k * 128
                    inst = nc.tensor.transpose(out=ptc[:, j * 8 + k, 0:3], in_=modrows[0:3, c0:c0 + 128],
                                               identity=ident_f[0:3, 0:3])
            return inst
        K.op(P, tr, r=[b_modrows, b_identf], w=[b_ptc])
        K.op(V, lambda: nc.vector.tensor_copy(
            out=modcol[:].rearrange("p k j r -> p j k r"),
            in_=ptc[:, :, 0:3].rearrange("p (j k) r -> p j k r", j=4)), r=[b_ptc], w=[b_modcol])
        for (Ax, ng, j) in ((A1, n1g, 1), (A2, n2g, 3)):
            K.op(V, lambda Ax=Ax, j=j: nc.vector.tensor_scalar(
                out=Ax[:], in0=modcol[:, :, j, :], scalar1=1.0, scalar2=None, op0=ALU.add),
                r=[b_modcol], w=[b_A], acc=True)
            K.op(V, lambda Ax=Ax, ng=ng: nc.vector.tensor_tensor(
                out=Ax[:], in0=Ax[:], in1=ng[:].rearrange("p (k o) -> p k o", o=1).to_broadcast([128, 8, 3]),
                op=ALU.mult), r=[b_A, b_ng], w=[b_A])
        gi = 0
        for gj, ch in enumerate((2, 5)):
            for b in range(NB):
                for hf in range(2):
                    c0 = ch * D + hf * 512
                    pp = pm[gi % 2]
                    K.op(P, lambda pp=pp, b=b, c0=c0: nc.tensor.matmul(
                        pp[:, :], lhsT=sel[0:3, b, :], rhs=modrows[0:3, c0:c0 + 512], start=True, stop=True),
                        r=[b_sel, b_modrows], w=[b_pm[gi % 2]])
                    K.op(V, lambda pp=pp, gj=gj, b=b, hf=hf: nc.vector.tensor_copy(
                        out=Gb[:, gj, b, hf * 512:(hf + 1) * 512], in_=pp[:, :]),
                        r=[b_pm[gi % 2]], w=[b_Gb], acc=True)
                    gi += 1
        if debug:
            K.dma(S, mod_s, modrows[:], r=[b_modrows], w=[db("mod_s")])
        K.barrier()

    if debug == 1:
        fin = K.sb("fin", [128, 8, 6], F32)
        b_fin = Buf()
        K.op(V, lambda: nc.vector.tensor_copy(out=fin[:, :, 0:3], in_=A1[:]), r=[b_A], w=[b_fin])
        K.op(V, lambda: nc.vector.tensor_copy(out=fin[:, :, 3:6], in_=A2[:]), r=[b_A, b_fin], w=[b_fin])
        dbg = nc.dram_tensor("dbg", [128, 8, 6], F32, kind="ExternalOutput").ap()
        K.dma(S, dbg, fin[:], r=[b_fin], w=[db("dbg")])
        K.barrier()
        return nc


    def norm_T1(env, j):
        xt, b_xt = env["xt"][j % 3], env["b_xt"][j % 3]
        xn, b_xn = env["xn"][j % 3], env["b_xn"][j % 3]
        st_, b_st = env["stat"][j % 3], env["b_stat"][j % 3]
        junk, b_junk = env["junk"], env["b_junk"]
        K.op(A, lambda: nc.scalar.activation(out=junk[:], in_=xt[:], func=AF.Square, accum_out=st_[:, 0:1]),
             r=[b_xt], w=[b_junk, b_st])
        K.op(A, lambda: nc.scalar.activation(out=st_[:, 1:2], in_=st_[:, 0:1], func=AF.Sqrt,
                                             scale=1.0 / D, bias=env["eps"][:, 0:1]), r=[b_st, env["b_eps"]], w=[b_st])
        K.op(V, lambda: nc.vector.reciprocal(out=st_[:, 2:3], in_=st_[:, 1:2]), r=[b_st], w=[b_st])
        if env.get("xn_on_dve"):
            K.op(V, lambda: nc.vector.tensor_scalar(out=xn[:], in0=xt[:], scalar1=st_[:, 2:3], scalar2=None, op0=ALU.mult),
                 r=[b_xt, b_st], w=[b_xn])
        else:
            K.op(A, lambda: nc.scalar.activation(out=xn[:], in_=xt[:], func=AF.Copy, scale=st_[:, 2:3]),
                 r=[b_xt, b_st], w=[b_xn])

    def norm_T2a(env, j):
        xn, b_xn = env["xn"][j % 3], env["b_xn"][j % 3]
        pT, b_pT = env["pT"], env["b_pT"]

        def tr():
            inst = None
            for k in range(8):
                inst = nc.tensor.transpose(out=pT[:, k, :], in_=xn[:, k * 128:(k + 1) * 128], identity=ident_b[:])
            return inst
        K.op(P, tr, r=[b_xn, b_identb], w=[b_pT])

    def norm_T2b(env, j, r, Acol, Bcol_j):
        hTt, b_hTt = env["hTt"][j % 3], env["b_hTt"][j % 3]
        pT, b_pT = env["pT"], env["b_pT"]
        for k in range(8):
            K.op(A, lambda k=k: nc.scalar.activation(
                out=hTt[:, k, :], in_=pT[:, k, :], func=AF.Identity, scale=Acol[:, k, r:r + 1],
                bias=modcol[:, k, Bcol_j, r:r + 1]),
                r=[b_pT, b_A, b_modcol], w=[b_hTt], acc=(k > 0))
        return hTt, b_hTt

    def norm_T(env, j, r, Acol, Bcol_j):
        norm_T1(env, j)
        norm_T2a(env, j)
        return norm_T2b(env, j, r, Acol, Bcol_j)

    envc = [0]

    def mk_norm_env(st):
        env = {}
        envc[0] += 1
        pf = "e%d_" % envc[0]
        env["xt"] = [st.enter_context(nc.sbuf_tensor(pf + "xt%d" % i, [128, D], F32)) for i in range(3)]
        env["xn"] = [st.enter_context(nc.sbuf_tensor(pf + "xn%d" % i, [128, D], BF16)) for i in range(3)]
        env["hTt"] = [st.enter_context(nc.sbuf_tensor(pf + "hTt%d" % i, [128, 8, 128], BF16)) for i in range(3)]
        env["stat"] = [st.enter_context(nc.sbuf_tensor(pf + "stat%d" % i, [128, 4], F32)) for i in range(3)]
        env["junk"] = st.enter_context(nc.sbuf_tensor(pf + "junk", [128, D], BF16))
        env["eps"] = st.enter_context(nc.sbuf_tensor(pf + "eps_t", [128, 1], F32))
        env["pT"] = st.enter_context(nc.psum_tensor(pf + "pT", [128, 8, 128], BF16))
        for nm in ("xt", "xn", "hTt", "stat"):
            env["b_" + nm] = [Buf(), Buf(), Buf()]
        env["b_junk"], env["b_pT"], env["b_eps"] = Buf(), Buf(), Buf()
        K.op(V, lambda: nc.vector.memset(env["eps"][:], EPS), w=[env["b_eps"]])
        return env

    def head_norm(env, src, b_src, H, dst, b_dst, CAt, SBt, j, tagbufs, part=0):
        sq, qn, T1, T2, hs = tagbufs["sq"], tagbufs["qn"], tagbufs["T1"], tagbufs["T2"], tagbufs["hs"]
        b_sq, b_qn, b_T1, b_T2, b_hs = tagbufs["b_sq"], tagbufs["b_qn"], tagbufs["b_T1"], tagbufs["b_T2"], tagbufs["b_hs"]
        W = H * 64
        if part in (0, 1):
            K.op(A, lambda: nc.scalar.activation(out=sq[:, 0:W], in_=src, func=AF.Square), r=[b_src], w=[b_sq])
        if part in (0, 2):
            K.op(V, lambda: nc.vector.tensor_reduce(out=hs[:, 0, 0:H], in_=sq[:, 0:W].rearrange("p (h e) -> p h e", e=64),
                                                    axis=AX.X, op=ALU.add), r=[b_sq], w=[b_hs])
        if part in (0, 3):
            K.op(A, lambda: nc.scalar.activation(out=hs[:, 1, 0:H], in_=hs[:, 0, 0:H], func=AF.Sqrt,
                                                 scale=1.0 / 64, bias=env["eps"][:, 0:1]), r=[b_hs, env["b_eps"]], w=[b_hs])
        if part not in (0, 4):
            return
        K.op(V, lambda: nc.vector.reciprocal(out=hs[:, 2, 0:H], in_=hs[:, 1, 0:H]), r=[b_hs], w=[b_hs])
        K.op(V, lambda: nc.vector.tensor_tensor(
            out=qn[:, 0:W].rearrange("p (h e) -> p h e", e=64), in0=src.rearrange("p (h e) -> p h e", e=64),
            in1=hs[:, 2, 0:H].rearrange("p (h o) -> p h o", o=1).to_broadcast([128, H, 64]), op=ALU.mult),
            r=[b_src, b_hs], w=[b_qn])
        cab = CAt.rearrange("p (o e) -> p o e", o=1).to_broadcast([128, H, 64])
        if SBt is None:
            K.op(V, lambda: nc.vector.tensor_tensor(out=dst.rearrange("p (h e) -> p h e", e=64),
                                                    in0=qn[:, 0:W].rearrange("p (h e) -> p h e", e=64),
                                                    in1=cab, op=ALU.mult), r=[b_qn, env["b_tab"]], w=[b_dst])
            return
        K.op(V, lambda: nc.vector.tensor_tensor(out=T1[:, 0:W].rearrange("p (h e) -> p h e", e=64),
                                                in0=qn[:, 0:W].rearrange("p (h e) -> p h e", e=64),
                                                in1=cab, op=ALU.mult), r=[b_qn, env["b_tab"]], w=[b_T1])
        for hf in range(2):
            sbv = SBt.rearrange("p (o a f r) -> p o a f r", o=1, a=2, f=2)[:, :, :, hf, :].to_broadcast([128, H, 2, 16])
            qv = qn[:, 0:W].rearrange("p (h a f r) -> p h a f r", a=2, f=2, r=16)[:, :, :, 1 - hf, :]
            ov = T2[:, 0:W].rearrange("p (h a f r) -> p h a f r", a=2, f=2, r=16)[:, :, :, hf, :]
            K.op(V, lambda sbv=sbv, qv=qv, ov=ov: nc.vector.tensor_tensor(out=ov, in0=qv, in1=sbv, op=ALU.mult),
                 r=[b_qn, env["b_tab"]], w=[b_T2], acc=(hf > 0))
        K.op(V, lambda: nc.vector.tensor_tensor(out=dst, in0=T1[:, 0:W], in1=T2[:, 0:W], op=ALU.add),
             r=[b_T1, b_T2], w=[b_dst])

    with ExitStack() as st:
        env = mk_norm_env(st)
        tabs = {}
        env["b_tab"] = Buf()
        gq = st.enter_context(nc.sbuf_tensor("gq_sb", [128, 2, 64], F32))
        gk = st.enter_context(nc.sbuf_tensor("gk_sb", [128, 2, 64], F32))
        b_g = Buf()
        K.dma(S, gq[:], qg_d, w=[b_g], acc=True)
        K.dma(S, gk[:], kg_d, w=[b_g], acc=True)
        for nm, src_d, gsel in (("CAq", ropeC_d, (gq, 0)), ("SBq", ropeS_d, (gq, 1)),
                                ("CAk", ropeC_d, (gk, 0)), ("SBk", ropeS_d, (gk, 1))):
            t = st.enter_context(nc.sbuf_tensor(nm, [128, 16, 64], F32))
            tabs[nm] = t
            bt = Buf()
            K.dma(S, t[:], src_d, w=[bt])
            gt, gi_ = gsel
            K.op(V, lambda t=t, gt=gt, gi_=gi_: nc.vector.tensor_tensor(
                out=t[:], in0=t[:], in1=gt[:, gi_:gi_ + 1, :].to_broadcast([128, 16, 64]), op=ALU.mult),
                r=[bt, b_g], w=[bt])
            K.op(V, lambda t=t: nc.vector.tensor_copy(out=t[:, 0:1, 0:1], in_=t[:, 0:1, 0:1]),
                 r=[bt], w=[env["b_tab"]], acc=True)
        tb = {}
        for nm, shp, dt in (("sq", [128, D], F32), ("qn", [128, D], F32), ("T1", [128, D], F32),
                            ("T2", [128, D], F32), ("hs", [128, 3, 16], F32)):
            tb[nm] = st.enter_context(nc.sbuf_tensor("s1_" + nm, shp, dt))
            tb["b_" + nm] = Buf()
        qr = [st.enter_context(nc.sbuf_tensor("qr%d" % i, [128, D], BF16)) for i in range(2)]
        kr = [st.enter_context(nc.sbuf_tensor("kr%d" % i, [128, 256], BF16)) for i in range(2)]
        vb = [st.enter_context(nc.sbuf_tensor("vb%d" % i, [128, 256], BF16)) for i in range(2)]
        ub = [st.enter_context(nc.sbuf_tensor("ub%d" % i, [128, 512], BF16)) for i in range(2)]
        qTt = [st.enter_context(nc.sbuf_tensor("qTt%d" % i, [128, 8, 128], BF16)) for i in range(2)]
        kTt = [st.enter_context(nc.sbuf_tensor("kTt%d" % i, [128, 2, 128], BF16)) for i in range(2)]
        b_qr, b_kr, b_vb, b_ub, b_qTt, b_kTt = ([Buf(), Buf()] for _ in range(6))
        pq = st.enter_context(nc.psum_tensor("pq", [128, D], F32))
        pkv = st.enter_context(nc.psum_tensor("pkv", [128, 512], F32))
        pu = st.enter_context(nc.psum_tensor("pu", [128, 512], F32))
        pT2 = st.enter_context(nc.psum_tensor("pT2", [128, 8, 128], BF16))
        pTk = st.enter_context(nc.psum_tensor("pTk", [128, 2, 128], BF16))
        b_pq, b_pkv, b_pu, b_pT2, b_pTk = Buf(), Buf(), Buf(), Buf(), Buf()

        tiles = []
        for b in range(NB):
            for ih in range(2):
                tiles.append(("c", b, ih, 0))
            for i in range(8):
                for cb in range(2):
                    tiles.append(("l", b, i, cb))
        hts = {}

        def load_x(j):
            kind, b, i, cb = tiles[j]
            xt, b_xt = env["xt"][j % 3], env["b_xt"][j % 3]
            if kind == "l":
                src = x_d[b].rearrange("(c i) d -> i c d", i=8)[i, cb * 128:(cb + 1) * 128, :]
                K.dma(A, xt[:], src, w=[b_xt])
            else:
                cv = ctx_d[b].rearrange("(c i) d -> i c d", i=8)
                for isub in range(4):
                    K.dma(A, xt[isub * 32:(isub + 1) * 32, :], cv[4 * i + isub, :, :], w=[b_xt], acc=(isub > 0))

        tbk = {}
        for nm, shp, dt in (("sq", [128, 256], F32), ("qn", [128, 256], F32), ("T1", [128, 256], F32),
                            ("T2", [128, 256], F32), ("hs", [128, 3, 16], F32)):
            tbk[nm] = st.enter_context(nc.sbuf_tensor("s1k_" + nm, shp, dt))
            tbk["b_" + nm] = Buf()

        def phaseA1(j):
            if j + 2 < len(tiles):
                load_x(j + 2)
            norm_T1(env, j)

        def phaseA2(j):
            kind, b, i, cb = tiles[j]
            norm_T2a(env, j)

        def phaseA3(j):
            kind, b, i, cb = tiles[j]
            hTt, b_hTt = norm_T2b(env, j, (b if kind == "l" else 2), A1, 0)
            hts[j] = (hTt, b_hTt)
            if kind == "l":
                col0 = b * SEQ + i * 256 + cb * 128
                K.dma(S, hT_s.rearrange("(k p) n -> p k n", p=128)[:, :, col0:col0 + 128], hTt[:],
                      r=[b_hTt], w=[db("hT_s")], acc=True)

        def phaseBmm(j):
            kind, b, i, cb = tiles[j]
            hTt, b_hTt = hts[j]

            def mm(dst, c0, n):
                def f():
                    inst = None
                    for blk in range(n):
                        for k in range(8):
                            inst = nc.tensor.matmul(dst[:, blk * 512:(blk + 1) * 512], lhsT=hTt[:, k, :],
                                                    rhs=win_sb[:, k, c0 + blk * 512:c0 + (blk + 1) * 512],
                                                    start=(k == 0), stop=(k == 7))
                    return inst
                return f
            if kind == "l":
                K.op(P, mm(pq, 0, 2), r=[b_hTt, b_win], w=[b_pq])
            K.op(P, mm(pkv, 1024, 1), r=[b_hTt, b_win], w=[b_pkv])
            K.op(P, mm(pu, 1536, 1), r=[b_hTt, b_win], w=[b_pu])

        def phaseBch(j, part):
            kind, b, i, cb = tiles[j]
            jj = j % 2
            kcol0 = b * KEYS + (i * 256 + cb * 128 if kind == "l" else SEQ + i * 128)
            if kind == "l":
                jt = i * 2 + cb
                head_norm(env, pq[:, :], b_pq, 16, qr[jj][:, :], b_qr[jj], tabs["CAq"][:, jt, :], tabs["SBq"][:, jt, :], j, tb, part)
                head_norm(env, pkv[:, 0:256], b_pkv, 4, kr[jj][:, :], b_kr[jj], tabs["CAk"][:, jt, :], tabs["SBk"][:, jt, :], j, tbk, part)
            else:
                head_norm(env, pkv[:, 0:256], b_pkv, 4, kr[jj][:, :], b_kr[jj], gk[:, 0, :], None, j, tbk, part)
            if part != 4:
                return
            K.op(A, lambda: nc.scalar.copy(out=vb[jj][:], in_=pkv[:, 256:512]), r=[b_pkv], w=[b_vb[jj]])
            K.dma(S, v_s[kcol0:kcol0 + 128, :], vb[jj][:], r=[b_vb[jj]], w=[db("v_s")], acc=True)
            K.op(A, lambda: nc.scalar.copy(out=ub[jj][:], in_=pu[:, :]), r=[b_pu], w=[b_ub[jj]])
            if kind == "l":
                K.dma(S, ul_s[b, cb * 128:(cb + 1) * 128, i, :], ub[jj][:], r=[b_ub[jj]], w=[db("ul_s")], acc=True)
            else:
                for isub in range(4):
                    K.dma(S, uc_s[b, :, 4 * i + isub, :], ub[jj][isub * 32:(isub + 1) * 32, :],
                          r=[b_ub[jj]], w=[db("uc_s")], acc=True)

        def phaseC(j):
            kind, b, i, cb = tiles[j]
            jj = j % 2
            kcol0 = b * KEYS + (i * 256 + cb * 128 if kind == "l" else SEQ + i * 128)
            if kind == "l":
                col0 = b * SEQ + i * 256 + cb * 128

                def trq():
                    inst = None
                    for k in range(8):
                        inst = nc.tensor.transpose(out=pT2[:, k, :], in_=qr[jj][:, k * 128:(k + 1) * 128], identity=ident_b[:])
                    return inst
                K.op(P, trq, r=[b_qr[jj], b_identb], w=[b_pT2])
                K.op(A, lambda: nc.scalar.copy(out=qTt[jj][:], in_=pT2[:]), r=[b_pT2], w=[b_qTt[jj]])
                K.dma(S, qT_s.rearrange("h p n -> p h n")[:, :, col0:col0 + 128], qTt[jj][:],
                      r=[b_qTt[jj]], w=[db("qT_s")], acc=True)

            def trk():
                inst = None
                for k in range(2):
                    inst = nc.tensor.transpose(out=pTk[:, k, :], in_=kr[jj][:, k * 128:(k + 1) * 128], identity=ident_b[:])
                return inst
            K.op(P, trk, r=[b_kr[jj], b_identb], w=[b_pTk])
            K.op(A, lambda: nc.scalar.copy(out=kTt[jj][:], in_=pTk[:]), r=[b_pTk], w=[b_kTt[jj]])
            K.dma(S, kT_s.rearrange("h p n -> p h n")[:, :, kcol0:kcol0 + 128], kTt[jj][:],
                  r=[b_kTt[jj]], w=[db("kT_s")], acc=True)

        nt_ = len(tiles)
        load_x(0)
        load_x(1)
        for j0 in range(2):
            phaseA1(j0)
            phaseA2(j0)
            phaseA3(j0)
        for j in range(nt_):
            if j + 2 < nt_:
                phaseA1(j + 2)
            phaseBmm(j)
            if j + 2 < nt_:
                phaseA2(j + 2)
            phaseBch(j, 1)
            phaseBch(j, 2)
            phaseBch(j, 3)
            phaseBch(j, 4)
            if j + 2 < nt_:
                phaseA3(j + 2)
            if j >= 1:
                phaseC(j - 1)
        phaseC(nt_ - 1)
        K.barrier()
    win_st.close()
    if debug == 2:
        return nc

    TWO_PI = 2.0 * math.pi
    CW1 = 6.28125
    CW2 = float(np.float32(TWO_PI - CW1))
    wd_s = dscr("wd_s", [128, 64, 128], BF16)
    wd2_s = dscr("wd2_s", [128, 64, 128], BF16)
    cwa_s = dscr("cwa_s", [128, 64, 128], BF16)
    cwb_s = dscr("cwb_s", [128, 64, 128], BF16)
    toep_s = dscr("toep_s", [128, 32, 128], BF16)
    rc_s = dscr("rc_s", [128, 2, 64], F32)
    prep_st = ExitStack()
    b_prep = Buf("prep")
    pw_shared = []

    def prep_body():
        sc_y = prep_st.enter_context(nc.sbuf_tensor("psc_y", [128, 576], F32))
        sc_ki = prep_st.enter_context(nc.sbuf_tensor("psc_ki", [128, 576], I32))
        sc_kf = prep_st.enter_context(nc.sbuf_tensor("psc_kf", [128, 576], F32))
        sc_r = prep_st.enter_context(nc.sbuf_tensor("psc_r", [128, 576], F32))

        def sincos(ang, n, sin_out, cos_out, bufr, bufw):
            y, ki, kf, r_ = sc_y[:, 0:n], sc_ki[:, 0:n], sc_kf[:, 0:n], sc_r[:, 0:n]
            rr, ww = [b_prep] + bufr, [b_prep] + bufw
            o = lambda fn: K.op(V, fn, r=rr, w=ww)
            o(lambda: nc.vector.tensor_scalar(out=y, in0=ang, scalar1=1.0 / TWO_PI, scalar2=None, op0=ALU.mult))
            o(lambda: nc.vector.tensor_copy(out=ki, in_=y))
            o(lambda: nc.vector.tensor_copy(out=kf, in_=ki))
            o(lambda: nc.vector.scalar_tensor_tensor(out=r_, in0=kf, scalar=-CW1, in1=ang, op0=ALU.mult, op1=ALU.add))
            o(lambda: nc.vector.scalar_tensor_tensor(out=r_, in0=kf, scalar=-CW2, in1=r_, op0=ALU.mult, op1=ALU.add))
            o(lambda: nc.vector.tensor_scalar(out=kf, in0=r_, scalar1=math.pi / 2, scalar2=-TWO_PI, op0=ALU.is_gt, op1=ALU.mult))
            o(lambda: nc.vector.scalar_tensor_tensor(out=y, in0=r_, scalar=math.pi / 2, in1=kf, op0=ALU.add, op1=ALU.add))
            o(lambda: nc.vector.tensor_scalar(out=y, in0=y, scalar1=-math.pi, scalar2=math.pi, op0=ALU.max, op1=ALU.min))
            o(lambda: nc.vector.tensor_scalar(out=r_, in0=r_, scalar1=-math.pi, scalar2=math.pi, op0=ALU.max, op1=ALU.min))
            K.op(A, lambda: nc.scalar.activation(out=cos_out, in_=y, func=AF.Sin), r=rr, w=ww)
            K.op(A, lambda: nc.scalar.activation(out=sin_out, in_=r_, func=AF.Sin), r=rr, w=ww)
        def pt(name, shape, dt=F32):
            return prep_st.enter_context(nc.sbuf_tensor("pp_" + name, shape, dt))
        lamr, lami, ldt = pt("lamr", [128, 64]), pt("lami", [128, 64]), pt("ldt", [128, 64])
        bre, bim = pt("bre", [128, 64, 16]), pt("bim", [128, 64, 16])
        cre, cim = pt("cre", [128, 64, 16]), pt("cim", [128, 64, 16])
        dcol = pt("dcol", [128, 32])
        mkf, mkb = pt("mkf", [128, 128]), pt("mkb", [128, 128])
        kv = pt("kv", [128, 9])
        for t_, d_ in ((lamr, lamr_d), (lami, lami_d), (ldt, ldt_d), (bre, bre_d), (bim, bim_d), (cre, cre_d),
                       (cim, cim_d), (dcol, dcol_d), (mkf, maskf_d), (mkb, maskb_d), (kv, kv_d)):
            K.dma(S, t_[:], d_, w=[b_prep], acc=True)
        dt_ = pt("dt", [128, 64]); xr = pt("xr", [128, 64]); th = pt("th", [128, 64])
        MX = pt("MX", [128, 9, 64]); ANG = pt("ANG", [128, 9, 64])
        EPOS = pt("EPOS", [128, 9, 64]); ENEG = pt("ENEG", [128, 9, 64])
        COSM = pt("COSM", [128, 9, 64]); SINM = pt("SINM", [128, 9, 64])
        PR = pt("PR", [128, 9, 64]); PI = pt("PI", [128, 9, 64]); NR = pt("NR", [128, 9, 64]); NI = pt("NI", [128, 9, 64])

        def vop(fn):
            K.op(V, fn, r=[b_prep], w=[b_prep])

        def aop(fn):
            K.op(A, fn, r=[b_prep], w=[b_prep])

        aop(lambda: nc.scalar.activation(out=dt_[:], in_=ldt[:], func=AF.Exp))
        vop(lambda: nc.vector.tensor_tensor(out=xr[:], in0=lamr[:], in1=dt_[:], op=ALU.mult))
        vop(lambda: nc.vector.tensor_tensor(out=th[:], in0=lami[:], in1=dt_[:], op=ALU.mult))
        kvb = kv[:].rearrange("p (m o) -> p m o", o=1).to_broadcast([128, 9, 64])
        vop(lambda: nc.vector.tensor_tensor(out=MX[:], in0=xr[:].rearrange("p (o t) -> p o t", o=1).to_broadcast([128, 9, 64]), in1=kvb, op=ALU.mult))
        vop(lambda: nc.vector.tensor_tensor(out=ANG[:], in0=th[:].rearrange("p (o t) -> p o t", o=1).to_broadcast([128, 9, 64]), in1=kvb, op=ALU.mult))
        aop(lambda: nc.scalar.activation(out=EPOS[:], in_=MX[:], func=AF.Exp))
        aop(lambda: nc.scalar.activation(out=ENEG[:], in_=MX[:], func=AF.Exp, scale=-1.0))
        fl = lambda t_: t_[:].rearrange("p m t -> p (m t)")
        sincos(fl(ANG), 576, fl(SINM), fl(COSM), [], [])
        vop(lambda: nc.vector.tensor_tensor(out=PR[:], in0=EPOS[:], in1=COSM[:], op=ALU.mult))
        vop(lambda: nc.vector.tensor_tensor(out=PI[:], in0=EPOS[:], in1=SINM[:], op=ALU.mult))
        vop(lambda: nc.vector.tensor_tensor(out=NR[:], in0=ENEG[:], in1=COSM[:], op=ALU.mult))
        vop(lambda: nc.vector.scalar_tensor_tensor(out=NI[:], in0=ENEG[:], scalar=-1.0, in1=SINM[:], op0=ALU.mult, op1=ALU.mult))
        K.dma(S, rc_s[:, 0, :], EPOS[:, 8, :], r=[b_prep], w=[db("rc_s")], acc=True)
        K.dma(S, rc_s[:, 1, :], ANG[:, 8, :], r=[b_prep], w=[db("rc_s")], acc=True)
        ar1 = pt("ar1", [128, 64]); den = pt("den", [128, 64]); tq = pt("tq", [128, 64]); tq2 = pt("tq2", [128, 64])
        cr = pt("cr", [128, 64]); ci = pt("ci", [128, 64])
        vop(lambda: nc.vector.tensor_scalar(out=ar1[:], in0=PR[:, 1, :], scalar1=-1.0, scalar2=None, op0=ALU.add))
        vop(lambda: nc.vector.tensor_tensor(out=den[:], in0=lamr[:], in1=lamr[:], op=ALU.mult))
        vop(lambda: nc.vector.tensor_tensor(out=tq[:], in0=lami[:], in1=lami[:], op=ALU.mult))
        vop(lambda: nc.vector.tensor_tensor(out=den[:], in0=den[:], in1=tq[:], op=ALU.add))
        vop(lambda: nc.vector.reciprocal(out=den[:], in_=den[:]))
        vop(lambda: nc.vector.tensor_tensor(out=tq[:], in0=ar1[:], in1=lamr[:], op=ALU.mult))
        vop(lambda: nc.vector.tensor_tensor(out=tq2[:], in0=PI[:, 1, :], in1=lami[:], op=ALU.mult))
        vop(lambda: nc.vector.tensor_tensor(out=tq[:], in0=tq[:], in1=tq2[:], op=ALU.add))
        vop(lambda: nc.vector.tensor_tensor(out=cr[:], in0=tq[:], in1=den[:], op=ALU.mult))
        vop(lambda: nc.vector.tensor_tensor(out=tq[:], in0=PI[:, 1, :], in1=lamr[:], op=ALU.mult))
        vop(lambda: nc.vector.tensor_tensor(out=tq2[:], in0=ar1[:], in1=lami[:], op=ALU.mult))
        vop(lambda: nc.vector.tensor_tensor(out=tq[:], in0=tq[:], in1=tq2[:], op=ALU.subtract))
        vop(lambda: nc.vector.tensor_tensor(out=ci[:], in0=tq[:], in1=den[:], op=ALU.mult))
        Bbr = pt("Bbr", [128, 64, 16]); Bbi = pt("Bbi", [128, 64, 16]); tb1 = pt("tb1", [128, 64, 16])
        crb = cr[:].rearrange("p (t o) -> p t o", o=1).to_broadcast([128, 64, 16])
        cib = ci[:].rearrange("p (t o) -> p t o", o=1).to_broadcast([128, 64, 16])
        vop(lambda: nc.vector.tensor_tensor(out=Bbr[:], in0=bre[:], in1=crb, op=ALU.mult))
        vop(lambda: nc.vector.tensor_tensor(out=tb1[:], in0=bim[:], in1=cib, op=ALU.mult))
        vop(lambda: nc.vector.tensor_tensor(out=Bbr[:], in0=Bbr[:], in1=tb1[:], op=ALU.subtract))
        vop(lambda: nc.vector.tensor_tensor(out=Bbi[:], in0=bim[:], in1=crb, op=ALU.mult))
        vop(lambda: nc.vector.tensor_tensor(out=tb1[:], in0=bre[:], in1=cib, op=ALU.mult))
        vop(lambda: nc.vector.tensor_tensor(out=Bbi[:], in0=Bbi[:], in1=tb1[:], op=ALU.add))
        EPr, EPi, FPr, FPi = (pt(n_, [128, 32, 2, 8]) for n_ in ("EPr", "EPi", "FPr", "FPi"))
        for dst_, src_ in ((EPr, PR), (EPi, PI), (FPr, NR), (FPi, NI)):
            sv = src_[:].rearrange("p m (g d) -> p m g d", d=2)
            vop(lambda dst_=dst_, sv=sv: nc.vector.tensor_copy(
                out=dst_[:, :, 0, :], in_=sv[:, 7::-1, :, 0].rearrange("p m g -> p g m")))
            vop(lambda dst_=dst_, sv=sv: nc.vector.tensor_copy(
                out=dst_[:, :, 1, :], in_=sv[:, 0:8, :, 1].rearrange("p m g -> p g m")))
        TN = 4
        Er, Ei, Fr, Fi, t1, CWr, CWi = (pt(n_, [128, TN, 8, 16]) for n_ in ("Er", "Ei", "Fr", "Fi", "t1", "CWr", "CWi"))
        ET, ET2, FT2 = (pt(n_, [128, TN, 128]) for n_ in ("ET", "ET2", "FT2"))
        ttmp = pt("ttmp", [128, 128]); ttmp2 = pt("ttmp2", [128, 128])
        pw0, b_pw0 = pw_shared[0], pw_shared[1]
        pw = [pw0, pw0]
        b_pw = [b_pw0, b_pw0]
        stg = {}
        for nm_ in ("WD", "WD2", "CWa", "CWb"):
            stg[nm_] = [pt("st_" + nm_ + str(i_), [128, TN, 128], BF16) for i_ in range(2)]
            stg["b_" + nm_] = [Buf(), Buf()]
        stg["Toep"] = [pt("st_Toep" + str(i_), [128, TN // 2, 128], BF16) for i_ in range(2)]
        stg["b_Toep"] = [Buf(), Buf()]
        pass
        v4 = lambda t_: t_[:].rearrange("p t (i h) -> p t i h", h=16)
        for ch in range(64 // TN):
            t0 = ch * TN
            cq = ch % 2
            ep = lambda t_: t_[:].rearrange("p g d i -> p (g d) i")[:, t0:t0 + TN, :].rearrange("p t (i o) -> p t i o", o=1).to_broadcast([128, TN, 8, 16])
            bb = lambda t_: t_[:, t0:t0 + TN, :].rearrange("p t (o h) -> p t o h", o=1).to_broadcast([128, TN, 8, 16])
            a8 = lambda t_: t_[:, 8, t0:t0 + TN].rearrange("p (t o q) -> p t o q", o=1, q=1).to_broadcast([128, TN, 8, 16])

            def cmul(outr, outi, ar_, ai_, br_, bi_):
                vop(lambda: nc.vector.tensor_tensor(out=outr[:], in0=ar_, in1=br_, op=ALU.mult))
                vop(lambda: nc.vector.tensor_tensor(out=t1[:], in0=ai_, in1=bi_, op=ALU.mult))
                vop(lambda: nc.vector.tensor_tensor(out=outr[:], in0=outr[:], in1=t1[:], op=ALU.subtract))
                vop(lambda: nc.vector.tensor_tensor(out=outi[:], in0=ar_, in1=bi_, op=ALU.mult))
                vop(lambda: nc.vector.tensor_tensor(out=t1[:], in0=ai_, in1=br_, op=ALU.mult))
                vop(lambda: nc.vector.tensor_tensor(out=outi[:], in0=outi[:], in1=t1[:], op=ALU.add))
            cmul(Er, Ei, ep(EPr), ep(EPi), bb(Bbr), bb(Bbi))
            cmul(Fr, Fi, ep(FPr), ep(FPi), bb(cre), bb(cim))
            cmul(CWr, CWi, a8(PR), a8(PI), Fr[:], Fi[:])
            lo, hi = slice(0, 64), slice(64, 128)

            def asm(dst, dsl, src, neg):
                if neg:
                    vop(lambda: nc.vector.tensor_scalar(out=dst, in0=src, scalar1=-1.0, scalar2=None, op0=ALU.mult))
                else:
                    vop(lambda: nc.vector.tensor_copy(out=dst, in_=src))
            asm(v4(ET)[lo], None, Er[lo], False); asm(v4(ET)[hi], None, Ei[hi], False)
            asm(v4(ET2)[lo], None, Ei[lo], False); asm(v4(ET2)[hi], None, Er[hi], True)
            asm(v4(FT2)[lo], None, Fr[lo], False); asm(v4(FT2)[hi], None, Fi[hi], True)
            K.op(V, lambda: nc.vector.tensor_copy(out=stg["CWa"][cq][lo].rearrange("p t (i h) -> p t i h", h=16), in_=CWr[lo]),
                 r=[b_prep], w=[stg["b_CWa"][cq]])
            K.op(V, lambda: nc.vector.tensor_scalar(out=stg["CWa"][cq][hi].rearrange("p t (i h) -> p t i h", h=16), in0=CWi[hi],
                                                    scalar1=-1.0, scalar2=None, op0=ALU.mult), r=[b_prep], w=[stg["b_CWa"][cq]], acc=True)
            K.op(V, lambda: nc.vector.tensor_scalar(out=stg["CWb"][cq][lo].rearrange("p t (i h) -> p t i h", h=16), in0=CWi[lo],
                                                    scalar1=-1.0, scalar2=None, op0=ALU.mult), r=[b_prep], w=[stg["b_CWb"][cq]])
            K.op(V, lambda: nc.vector.tensor_scalar(out=stg["CWb"][cq][hi].rearrange("p t (i h) -> p t i h", h=16), in0=CWr[hi],
                                                    scalar1=-1.0, scalar2=None, op0=ALU.mult), r=[b_prep], w=[stg["b_CWb"][cq]], acc=True)
            K.dma(S, cwa_s[:, t0:t0 + TN, :], stg["CWa"][cq][:], r=[stg["b_CWa"][cq]], w=[db("cwa_s")], acc=True)
            K.dma(S, cwb_s[:, t0:t0 + TN, :], stg["CWb"][cq][:], r=[stg["b_CWb"][cq]], w=[db("cwb_s")], acc=True)
            for src_, dstw in ((ET, "WD"), (ET2, "WD2")):
                for q in range(TN // 4):
                    pp, b_pp = pw[q % 2], b_pw[q % 2]

                    def trw(src_=src_, q=q, pp=pp):
                        inst = None
                        for u in range(4):
                            inst = nc.tensor.transpose(out=pp[:, u, :], in_=src_[:, q * 4 + u, :], identity=ident_f[:])
                        return inst
                    K.co.armed = False
                    K.op(P, trw, r=[b_prep, b_identf], w=[b_pp])
                    K.op(V, lambda dstw=dstw, q=q, pp=pp: nc.vector.tensor_copy(out=stg[dstw][cq][:, q * 4:q * 4 + 4, :], in_=pp[:]),
                         r=[b_pp], w=[stg["b_" + dstw][cq]], acc=(q > 0))
                    K.co.armed = True
                K.dma(S, {"WD": wd_s, "WD2": wd2_s}[dstw][:, t0:t0 + TN, :], stg[dstw][cq][:], r=[stg["b_" + dstw][cq]], w=[db(dstw + "_s")], acc=True)
            for gl in range(TN // 2):
                g = t0 // 2 + gl
                pp, b_pp = pw[gl % 2], b_pw[gl % 2]

                def tz(gl=gl, pp=pp):
                    inst = None
                    for d_ in range(2):
                        inst = nc.tensor.matmul(pp[:, d_, :], lhsT=ET[:, gl * 2 + d_, :], rhs=FT2[:, gl * 2 + d_, :], start=True, stop=True)
                    return inst
                K.co.armed = False
                K.op(P, tz, r=[b_prep], w=[b_pp])
                K.op(V, lambda pp=pp: nc.vector.tensor_tensor(out=ttmp[:], in0=pp[:, 0, :], in1=mkf[:], op=ALU.mult), r=[b_pp, b_prep], w=[b_prep])
                K.op(V, lambda pp=pp: nc.vector.tensor_tensor(out=ttmp2[:], in0=pp[:, 1, :], in1=mkb[:], op=ALU.mult), r=[b_pp, b_prep], w=[b_prep])
                K.co.armed = True
                vop(lambda: nc.vector.tensor_tensor(out=ttmp[:], in0=ttmp[:], in1=ttmp2[:], op=ALU.add))
                K.op(V, lambda g=g, gl=gl: nc.vector.scalar_tensor_tensor(out=stg["Toep"][cq][:, gl, :], in0=ident_f[:], scalar=dcol[:, g:g + 1], in1=ttmp[:],
                                                                  op0=ALU.mult, op1=ALU.add), r=[b_prep, b_identf], w=[stg["b_Toep"][cq]], acc=(gl > 0))
            K.dma(S, toep_s[:, t0 // 2:t0 // 2 + TN // 2, :], stg["Toep"][cq][:], r=[stg["b_Toep"][cq]], w=[db("toep_s")], acc=True)

    K.co = Interleaver(prep_body)
    K.co.armed = True

    OT_s = dscr("OT_s", [D, NL], BF16)
    with ExitStack() as st:
        esel_f = st.enter_context(nc.sbuf_tensor("esel_sb", [128, 128], F32))
        esel = st.enter_context(nc.sbuf_tensor("esel_bf", [128, 128], BF16))
        b_esel = Buf()
        K.dma(S, esel_f[:], esel_d, w=[b_esel])
        K.op(V, lambda: nc.vector.tensor_copy(out=esel[:], in_=esel_f[:]), r=[b_esel], w=[b_esel])
        Ohi = [st.enter_context(nc.sbuf_tensor("Ohi%d" % i, [128, 512], BF16)) for i in range(2)]
        Olo = [st.enter_context(nc.sbuf_tensor("Olo%d" % i, [128, 512], BF16)) for i in range(2)]
        b_Ohl = [Buf(), Buf()]
        for i_ in range(2):
            K.op(V, lambda i_=i_: nc.vector.memset(Ohi[i_][:], 0.0), w=[b_Ohl[i_]])
            K.op(V, lambda i_=i_: nc.vector.memset(Olo[i_][:], 0.0), w=[b_Ohl[i_]], acc=True)
        kT2 = st.enter_context(nc.sbuf_tensor("kT2", [128, 2, KEYS], BF16))
        Vb = st.enter_context(nc.sbuf_tensor("Vb", [128, 18, 4 * 65 + 64], BF16))
        qTp = [st.enter_context(nc.sbuf_tensor("qTp%d" % i, [128, SEQ], BF16)) for i in range(4)]
        pTs = [st.enter_context(nc.sbuf_tensor("pTs%d" % i, [128, 2, 512], BF16)) for i in range(4)]
        Osb = [st.enter_context(nc.sbuf_tensor("Osb%d" % i, [128, 512], F32)) for i in range(2)]
        On = [st.enter_context(nc.sbuf_tensor("On%d" % i, [64, 512], BF16)) for i in range(2)]
        LA = 2
        pS = [st.enter_context(nc.psum_tensor("pS%d" % i, [128, 2, 512], F32)) for i in range(3)]
        pO0 = st.enter_context(nc.psum_tensor("pO0", [128, 512], F32))
        pO = [pO0, pO0]
        pB4 = st.enter_context(nc.psum_tensor("pB", [128, 4, 128], F32))
        pB = pB4[:].rearrange("p u c -> p (u c)")
        b_kT2, b_Vb, b_pB = Buf(), Buf(), Buf()
        pw_shared.extend([pB4, b_pB])
        b_Osb, b_On = ([Buf(), Buf()] for _ in range(2))
        b_pO0 = Buf()
        b_pO = [b_pO0, b_pO0]
        b_qTp = [Buf() for _ in range(4)]
        b_pTs = [Buf() for _ in range(4)]
        b_pS = [Buf() for _ in range(3)]
        K.op(V, lambda: nc.vector.memset(Vb[:], 0.0), w=[b_Vb])
        K.op(V, lambda: nc.vector.memset(Vb[:, :, 0:260].rearrange("p t (g e) -> p t g e", e=65)[:, :, :, 64:65], 1.0), r=[b_Vb], w=[b_Vb])
        for i_ in range(4):
            K.op(V, lambda i_=i_: nc.vector.memset(qTp[i_][:], 0.0), w=[b_qTp[i_]])
        for i_ in range(2):
            K.op(V, lambda i_=i_: nc.vector.memset(Osb[i_][:], 0.0), w=[b_Osb[i_]])

        def load_batch(b):
            for gp_ in range(2):
                K.dma(S, kT2[:, gp_, :], kT_s[gp_, :, b * KEYS:(b + 1) * KEYS], r=[db("kT_s")], w=[b_kT2], acc=(gp_ > 0))
            for g in range(4):
                K.dma(S, Vb[:, :, g * 65:g * 65 + 64],
                      v_s[b * KEYS:(b + 1) * KEYS, g * 64:(g + 1) * 64].rearrange("(t p) e -> p t e", p=128),
                      r=[db("v_s")], w=[b_Vb], acc=(g > 0))

        def load_q(b, hp):
            for h2 in range(2):
                h = hp * 2 + h2
                half = (h // 4) % 2
                qi = (hp % 2) * 2 + h2
                oh = 1 - half
                K.op(V, lambda qi=qi, oh=oh: nc.vector.memset(qTp[qi][oh * 64:(oh + 1) * 64, :], 0.0), w=[b_qTp[qi]])
                K.dma(S, qTp[qi][half * 64:(half + 1) * 64, :], qT_s[hp, h2 * 64:(h2 + 1) * 64, b * SEQ:(b + 1) * SEQ],
                      r=[db("qT_s")], w=[b_qTp[qi]], acc=True)

        jobs = []
        for b in range(NB):
            for hp in range(8):
                for h2 in range(2):
                    for qb in range(4):
                        for ktp in range(9):
                            jobs.append((b, hp, h2, qb, ktp))
        nj = len(jobs)
        pend = []

        def emit_front(idx):
            b, hp, h2, qb, ktp = jobs[idx]
            if hp == 0 and h2 == 0 and qb == 0 and ktp == 0:
                load_batch(b)
                load_q(b, 0)
            if h2 == 0 and qb == 0 and ktp == 0:
                if hp + 1 < 8:
                    load_q(b, hp + 1)
            h = hp * 2 + h2
            g = h // 4
            qi = (hp % 2) * 2 + h2
            qt, b_qt = qTp[qi], b_qTp[qi]
            ps, b_ps = pS[idx % 3], b_pS[idx % 3]
            pt, b_pt = pTs[idx % 4], b_pTs[idx % 4]

            def qk():
                inst = None
                for u in range(2):
                    kt = ktp * 2 + u
                    inst = nc.tensor.matmul(ps[:, u, :], lhsT=kT2[:, g // 2, kt * 128:(kt + 1) * 128],
                                            rhs=qt[:, qb * 512:(qb + 1) * 512], start=True, stop=True)
                return inst
            K.op(P, qk, r=[b_kT2, b_qt], w=[b_ps])
            K.op(A, lambda: nc.scalar.activation(out=pt[:], in_=ps[:], func=AF.Exp, scale=0.125), r=[b_ps], w=[b_pt])

        def emit_back(idx):
            b, hp, h2, qb, ktp = jobs[idx]
            h = hp * 2 + h2
            g = h // 4
            it = idx // 9
            pt, b_pt = pTs[idx % 4], b_pTs[idx % 4]
            po, b_po = pO[it % 2], b_pO[it % 2]

            def pv():
                inst = None
                for u in range(2):
                    kt = ktp * 2 + u
                    inst = nc.tensor.matmul(po[:, :], lhsT=Vb[:, kt, g * 65:g * 65 + 128], rhs=pt[:, u, :], start=(kt == 0), stop=(kt == 17))
                return inst
            K.op(P, pv, r=[b_Vb, b_pt], w=[b_po], acc=(ktp > 0))
            if ktp == 8:
                osb, b_osb = Osb[it % 2], b_Osb[it % 2]
                on, b_on = On[it % 2], b_On[it % 2]
                fin_dve = K.co.finished
                nrow = 65 if fin_dve else 64
                K.op(V, lambda: nc.vector.tensor_copy(out=osb[0:nrow, :], in_=po[0:nrow, :]), r=[b_po], w=[b_osb])
                if fin_dve:
                    K.op(V, lambda: nc.vector.reciprocal(out=osb[64:65, :], in_=osb[64:65, :]), r=[b_osb], w=[b_osb])
                else:
                    K.op(A, lambda: nc.scalar.activation(out=osb[64:65, :], in_=po[64:65, :], func=AF.Ln), r=[b_po], w=[b_osb], acc=True)
                    K.op(A, lambda: nc.scalar.activation(out=osb[64:65, :], in_=osb[64:65, :], func=AF.Exp, scale=-1.0), r=[b_osb], w=[b_osb], acc=True)

                ohi, olo, b_ohl = Ohi[it % 2], Olo[it % 2], b_Ohl[it % 2]
                K.op(V, lambda: nc.vector.tensor_copy(out=ohi[64:65, :], in_=osb[64:65, :]), r=[b_osb], w=[b_ohl])
                K.op(V, lambda: nc.vector.tensor_tensor(out=olo[64:65, :], in0=osb[64:65, :], in1=ohi[64:65, :], op=ALU.subtract),
                     r=[b_osb, b_ohl], w=[b_ohl], acc=True)

                def fin2():
                    def em():
                        nc.tensor.matmul(pB, lhsT=esel[:, :], rhs=ohi[:, :], start=True, stop=False)
                        return nc.tensor.matmul(pB, lhsT=esel[:, :], rhs=olo[:, :], start=False, stop=True)
                    K.op(P, em, r=[b_esel, b_ohl], w=[b_pB])
                    K.op(V, lambda: nc.vector.tensor_tensor(out=on[:], in0=osb[0:64, :], in1=pB4[0:64, :, :].rearrange("p u c -> p (u c)"), op=ALU.mult),
                         r=[b_osb, b_pB], w=[b_on])
                    K.dma(S, OT_s[h * 64:(h + 1) * 64, b * SEQ + qb * 512:b * SEQ + (qb + 1) * 512], on[:],
                          r=[b_on], w=[db("OT_s")], acc=True)
                pend.append((idx + 7, fin2))

        nback = 0
        for idx in range(nj + LA):
            newb = idx < nj and idx >= 1 and jobs[idx][1:] == (0, 0, 0, 0)
            if newb:
                while nback < idx:
                    emit_back(nback)
                    nback += 1
            if idx < nj:
                emit_front(idx)
            if idx >= LA and nback <= idx - LA:
                emit_back(nback)
                nback += 1
            while pend and pend[0][0] <= idx:
                pend.pop(0)[1]()
            K.co.step(1)
        while pend:
            pend.pop(0)[1]()
        K.co.finish()
        K.barrier()
        prep_st.close()
    if debug == 3:
        return nc

    ygT_s = dscr("ygT_s", [512, NL], BF16)
    with ExitStack() as st:
        WD = st.enter_context(nc.sbuf_tensor("WD", [128, 64, 128], BF16))
        WD2 = st.enter_context(nc.sbuf_tensor("WD2", [128, 64, 128], BF16))
        CWa = st.enter_context(nc.sbuf_tensor("CWa", [128, 64, 128], BF16))
        CWb = st.enter_context(nc.sbuf_tensor("CWb", [128, 64, 128], BF16))
        Toep = st.enter_context(nc.sbuf_tensor("Toep", [128, 32, 128], BF16))
        RT = st.enter_context(nc.sbuf_tensor("RT", [128, 2, 64], F32))
        Rcol = RT[:, 0, :]
        TH8 = RT[:, 1, :]
        iot = st.enter_context(nc.sbuf_tensor("iot", [128, 2, 288], F32))
        b_w = Buf("ssm_w")
        sc_y = st.enter_context(nc.sbuf_tensor("sc_y", [128, 576], F32))
        sc_ki = st.enter_context(nc.sbuf_tensor("sc_ki", [128, 576], I32))
        sc_kf = st.enter_context(nc.sbuf_tensor("sc_kf", [128, 576], F32))
        sc_r = st.enter_context(nc.sbuf_tensor("sc_r", [128, 576], F32))
        K.dma(S, iot[:], iota_d, w=[b_w], acc=True)
        for t_, d_, nm_ in ((WD, wd_s, "WD_s"), (WD2, wd2_s, "WD2_s"), (CWa, cwa_s, "cwa_s"), (CWb, cwb_s, "cwb_s")):
            for hh in range(2):
                K.dma(S if hh == 0 else A, t_[:, hh * 32:(hh + 1) * 32, :], d_[:, hh * 32:(hh + 1) * 32, :], r=[db(nm_)], w=[b_w], acc=True)
        K.dma(S, Toep[:], toep_s, r=[db("toep_s")], w=[b_w], acc=True)
        K.dma(S, RT[:], rc_s, r=[db("rc_s")], w=[b_w], acc=True)
        if debug == 4:
            for nm, t_ in (("WD", WD), ("WD2", WD2), ("CWa", CWa), ("CWb", CWb)):
                dd = nc.dram_tensor("dbg_" + nm, [128, 64, 128], BF16, kind="ExternalOutput").ap()
                K.dma(S, dd, t_[:], r=[b_w], w=[db("dbg_" + nm)])
            dd = nc.dram_tensor("dbg_Toep", [128, 32, 128], BF16, kind="ExternalOutput").ap()
            K.dma(S, dd, Toep[:], r=[b_w], w=[db("dbg_Toep")])
            dd = nc.dram_tensor("dbg_R", [128, 2, 64], F32, kind="ExternalOutput").ap()
            K.dma(S, dd, RT[:], r=[b_w], w=[db("dbg_R")], acc=True)
            K.barrier()
            return nc

        Ug = st.enter_context(nc.sbuf_tensor("Ug", [128, NB, 2, 4096], BF16))
        Ucg = st.enter_context(nc.sbuf_tensor("Ucg", [128, NB, 4096], BF16))
        b_U = Buf("U")
        K.op(V, lambda: nc.vector.memset(Ucg[:], 0.0), w=[b_U])
        with ExitStack() as su:
            Utmp = [su.enter_context(nc.sbuf_tensor("Utmp%d" % i, [128, 4096], BF16)) for i in range(2)]
            b_Utmp = [Buf(), Buf()]
            ui = 0
            for b in range(NB):
                for cbk in range(3):
                    tmp, b_tmp = Utmp[ui % 2], b_Utmp[ui % 2]
                    ui += 1
                    if cbk < 2:
                        K.dma(S, tmp[:], ul_s[b, cbk * 128:(cbk + 1) * 128].rearrange("p i n -> p (i n)"), r=[db("ul_s")], w=[b_tmp])
                        K.op(V, lambda: nc.vector.tensor_copy(out=Ug[:, b, cbk, :].rearrange("p (g i h) -> p g i h", g=32, i=8),
                                                              in_=tmp[:].rearrange("p (i g h) -> p g i h", i=8, g=32)),
                             r=[b_tmp], w=[b_U], acc=True)
                    else:
                        K.dma(S, tmp[0:32, :], uc_s[b].rearrange("c i n -> c (i n)"), r=[db("uc_s")], w=[b_tmp])
                        K.op(A, lambda: nc.scalar.copy(out=Ucg[0:32, b, :].rearrange("p (g i h) -> p g i h", g=32, i=8),
                                                       in_=tmp[0:32, :].rearrange("p (i g h) -> p g i h", i=8, g=32)),
                             r=[b_tmp], w=[b_U], acc=True)
            K.barrier()
        UText = [st.enter_context(nc.sbuf_tensor("UText%d" % i, [128, NB, 320], BF16)) for i in range(2)]
        tabCS = [st.enter_context(nc.sbuf_tensor("tabCS%d" % i, [128, 2, 2, 288], F32)) for i in range(2)]
        tabC = [t_[:, :, 0, :] for t_ in tabCS]
        tabS = [t_[:, :, 1, :] for t_ in tabCS]
        TH8s = st.enter_context(nc.sbuf_tensor("TH8s", [128, 64], F32))
        Wt = [st.enter_context(nc.sbuf_tensor("Wt%d" % i, [128, 288], F32)) for i in range(2)]
        Wt2 = [st.enter_context(nc.sbuf_tensor("Wt2%d" % i, [128, 288], F32)) for i in range(2)]
        Zt = [st.enter_context(nc.sbuf_tensor("Zt%d" % i, [128, 288], F32)) for i in range(2)]
        ZCS = [[[st.enter_context(nc.sbuf_tensor("ZCS%d%d%d" % (i, b, d_), [128, 2, 288], BF16)) for d_ in range(2)] for b in range(NB)] for i in range(2)]
        ZC = [[[ZCS[i][b][d_][:, 0, :] for d_ in range(2)] for b in range(NB)] for i in range(2)]
        ZS = [[[ZCS[i][b][d_][:, 1, :] for d_ in range(2)] for b in range(NB)] for i in range(2)]
        y_sb = st.enter_context(nc.sbuf_tensor("y_sb", [128, 4, 8, 128], F32))
        gx = [st.enter_context(nc.sbuf_tensor("gx%d" % i, [128, 512], F32)) for i in range(2)]
        gt1 = [st.enter_context(nc.sbuf_tensor("gt1%d" % i, [128, 512], F32)) for i in range(2)]
        gsg = [st.enter_context(nc.sbuf_tensor("gsg%d" % i, [128, 512], F32)) for i in range(2)]
        ygt = [st.enter_context(nc.sbuf_tensor("ygt%d" % i, [128, 4, 128], BF16)) for i in range(2)]
        pU = st.enter_context(nc.psum_tensor("pU", [128, 320], F32))
        pD = [st.enter_context(nc.psum_tensor("pD%d" % i, [128, 288], F32)) for i in range(2)]
        pD2 = [st.enter_context(nc.psum_tensor("pD2%d" % i, [128, 288], F32)) for i in range(2)]
        py = [st.enter_context(nc.psum_tensor("py%d" % i, [128, 4, 128], F32)) for i in range(2)]
        pTy = st.enter_context(nc.psum_tensor("pTy", [128, 4, 128], F32))
        b_UText, b_tab, b_Wt, b_Wt2, b_Zt, b_pD, b_pD2, b_py, b_gx, b_gt1, b_gsg, b_ygt = ([Buf(), Buf()] for _ in range(12))
        b_ZC = [[[Buf(), Buf()] for _ in range(NB)] for _ in range(2)]
        b_ZS = [[[Buf(), Buf()] for _ in range(NB)] for _ in range(2)]
        b_pU, b_ysb, b_pTy, b_ths = Buf(), Buf(), Buf(), Buf()
        GELU_S = 2.0 * math.sqrt(2.0 / math.pi)
        K.op(V, lambda: nc.vector.tensor_scalar(out=TH8s[:], in0=TH8, scalar1=1.0 / TWO_PI, scalar2=None, op0=ALU.mult),
             r=[b_w], w=[b_ths])
        SIN_SC = TWO_PI * (1.0 - 2e-7)
        hpi = st.enter_context(nc.sbuf_tensor("hpi", [128, 1], F32))
        b_hpi = Buf()
        K.op(V, lambda: nc.vector.memset(hpi[:], math.pi / 2 * (1.0 - 2e-7)), w=[b_hpi])
        one_c = st.enter_context(nc.sbuf_tensor("one_c", [128, 1], F32))
        K.op(V, lambda: nc.vector.memset(one_c[:], 1.0), w=[b_hpi], acc=True)

        def tables(g):
            gp = g % 2
            y, ki, kf, r_ = sc_y[:, 0:576], sc_ki[:, 0:576], sc_kf[:, 0:576], sc_r[:, 0:576]
            bs = [b_prep]
            for d_ in range(2):
                T = g * 2 + d_
                K.op(A, lambda d_=d_, T=T: nc.scalar.activation(out=sc_y[:, d_ * 288:(d_ + 1) * 288], in_=iot[:, d_, :],
                                                                 func=AF.Copy, scale=TH8s[:, T:T + 1]),
                     r=[b_w, b_ths, b_prep], w=[b_prep], acc=(d_ > 0))
            K.op(A, lambda: nc.scalar.copy(out=ki, in_=y), r=bs, w=bs)
            K.op(A, lambda: nc.scalar.copy(out=kf, in_=ki), r=bs, w=bs)
            K.op(V, lambda: nc.vector.tensor_tensor(out=r_, in0=y, in1=kf, op=ALU.subtract), r=bs, w=bs)
            K.op(A, lambda: nc.scalar.activation(out=tabS[gp], in_=r_.rearrange("p (d n) -> p d n", d=2), func=AF.Sin, scale=SIN_SC),
                 r=bs, w=[b_tab[gp]])
            K.op(A, lambda: nc.scalar.activation(out=kf, in_=r_, func=AF.Abs), r=bs, w=bs)
            K.op(A, lambda: nc.scalar.activation(out=tabC[gp], in_=kf.rearrange("p (d n) -> p d n", d=2), func=AF.Sin, scale=-SIN_SC,
                                                 bias=hpi[:, 0:1]), r=bs + [b_hpi], w=[b_tab[gp]], acc=True)

        lvc = [0]

        def chain(g, b, d_, ut, b_ut):
            gp = g % 2
            T = g * 2 + d_
            off = 0 if d_ == 0 else 32
            l2 = lvc[0] % 2
            lvc[0] += 1
            K.op(P, lambda: nc.tensor.matmul(pD[l2][:, :], lhsT=WD[:, T, :], rhs=ut[:, b, off:off + 288], start=True, stop=True),
                 r=[b_w, b_ut], w=[b_pD[l2]])
            K.op(P, lambda: nc.tensor.matmul(pD2[l2][:, :], lhsT=WD2[:, T, :], rhs=ut[:, b, off:off + 288], start=True, stop=True),
                 r=[b_w, b_ut], w=[b_pD2[l2]])
            K.op(V, lambda: nc.vector.tensor_tensor(out=Wt[l2][:], in0=pD[l2][:, :], in1=tabC[gp][:, d_, :], op=ALU.mult),
                 r=[b_pD[l2], b_tab[gp]], w=[b_Wt[l2]])
            K.op(V, lambda: nc.vector.tensor_tensor(out=Wt2[l2][:], in0=pD2[l2][:, :], in1=tabS[gp][:, d_, :], op=ALU.mult),
                 r=[b_pD2[l2], b_tab[gp]], w=[b_Wt2[l2]])
            K.op(V, lambda: nc.vector.tensor_tensor(out=Wt[l2][:], in0=Wt[l2][:], in1=Wt2[l2][:], op=ALU.add),
                 r=[b_Wt[l2], b_Wt2[l2]], w=[b_Wt[l2]])
            rv = (lambda a_: a_) if d_ == 0 else (lambda a_: a_[:, ::-1])
            K.op(V, lambda: nc.vector.tensor_tensor_scan(out=rv(Zt[l2][:]), data0=Rcol[:, T:T + 1].to_broadcast([128, 288]),
                                                         data1=rv(Wt[l2][:]), initial=0.0, op0=ALU.mult, op1=ALU.add),
                 r=[b_Wt[l2], b_w], w=[b_Zt[l2]])
            K.op(V, lambda: nc.vector.tensor_tensor(
                out=ZCS[gp][b][d_][:], in0=Zt[l2][:].rearrange("p (o n) -> p o n", o=1).to_broadcast([128, 2, 288]),
                in1=tabCS[gp][:, d_, :, :], op=ALU.mult),
                r=[b_Zt[l2], b_tab[gp]], w=[b_ZC[gp][b][d_], b_ZS[gp][b][d_]])

        def ymm_group(g):
            gp = g % 2
            ut, b_ut = UText[gp], b_UText[gp]
            pyt, b_pyt = py[gp], b_py[gp]
            for b in range(NB):
                def ymm():
                    inst = None
                    for cbk in range(2):
                        o_ = pyt[:, b * 2 + cbk, :]
                        nc.tensor.matmul(o_, lhsT=ut[:, b, 32 + cbk * 128:32 + (cbk + 1) * 128], rhs=Toep[:, g, :], start=True, stop=False)
                        for d_ in range(2):
                            T = g * 2 + d_
                            o2 = (31 if d_ == 0 else 1) + cbk * 128
                            nc.tensor.matmul(o_, lhsT=ZC[gp][b][d_][:, o2:o2 + 128], rhs=CWa[:, T, :], start=False, stop=False)
                            inst = nc.tensor.matmul(o_, lhsT=ZS[gp][b][d_][:, o2:o2 + 128], rhs=CWb[:, T, :], start=False, stop=(d_ == 1))
                    return inst
                K.op(P, ymm, r=[b_ut, b_w, b_ZC[gp][b][0], b_ZC[gp][b][1], b_ZS[gp][b][0], b_ZS[gp][b][1]], w=[b_pyt], acc=(b > 0))
            K.op(A, lambda: nc.scalar.copy(
                out=y_sb[:].rearrange("p q i (l h) -> p q i l h", h=16)[:, :, :, g % 8, :],
                in_=pyt[:].rearrange("p q (i h) -> p q i h", h=16)), r=[b_pyt], w=[b_ysb], acc=(g % 8 > 0))

        gcnt = [0]

        def gelu_out(gt_):
            for b in range(NB):
                for cbk in range(2):
                    for ih in range(2):
                        q = gcnt[0] % 2
                        gcnt[0] += 1

                        def try_():
                            inst = None
                            for u in range(4):
                                i_ = ih * 4 + u
                                inst = nc.tensor.transpose(out=pTy[:, u, :], in_=y_sb[:, b * 2 + cbk, i_, :], identity=ident_f[:])
                            return inst
                        K.op(P, try_, r=[b_ysb, b_identf], w=[b_pTy])
                        pf = pTy[:].rearrange("p u c -> p (u c)")
                        K.op(A, lambda: nc.scalar.copy(out=gx[q][:], in_=pf), r=[b_pTy], w=[b_gx[q]])
                        K.op(A, lambda: nc.scalar.activation(out=gt1[q][:], in_=pf, func=AF.Square), r=[b_pTy], w=[b_gt1[q]])
                        K.op(A, lambda: nc.scalar.activation(out=gt1[q][:], in_=gt1[q][:], func=AF.Identity, scale=0.044715, bias=one_c[:, 0:1]),
                             r=[b_gt1[q], b_hpi], w=[b_gt1[q]])
                        K.op(V, lambda: nc.vector.tensor_tensor(out=gt1[q][:], in0=gt1[q][:], in1=gx[q][:], op=ALU.mult),
                             r=[b_gt1[q], b_gx[q]], w=[b_gt1[q]])
                        K.op(A, lambda: nc.scalar.activation(out=gsg[q][:], in_=gt1[q][:], func=AF.Sigmoid, scale=GELU_S),
                             r=[b_gt1[q]], w=[b_gsg[q]])
                        K.op(V, lambda: nc.vector.tensor_tensor(out=ygt[q][:].rearrange("p u c -> p (u c)"), in0=gx[q][:], in1=gsg[q][:],
                                                                op=ALU.mult), r=[b_gx[q], b_gsg[q]], w=[b_ygt[q]])
                        c0 = b * SEQ + ih * 4 * 256
                        dst = ygT_s[gt_ * 128:(gt_ + 1) * 128, c0:c0 + 4 * 256].rearrange(
                            "p (u c) -> p u c", c=256)[:, :, cbk * 128:(cbk + 1) * 128]
                        K.dma(S, dst, ygt[q][:], r=[b_ygt[q]], w=[db("ygT_s")], acc=True)

        tables(0)
        for g in range(32):
            gp = g % 2
            ut, b_ut = UText[gp], b_UText[gp]
            for b in range(NB):
                def mkU():
                    nc.tensor.matmul(pU[:, 0:32], lhsT=Ucg[:, b, g * 128:(g + 1) * 128], rhs=ident_b[:, 0:32], start=True, stop=True)
                    inst = None
                    for cbk in range(2):
                        inst = nc.tensor.matmul(pU[:, 32 + cbk * 128:32 + (cbk + 1) * 128], lhsT=Ug[:, b, cbk, g * 128:(g + 1) * 128],
                                                rhs=ident_b[:, :], start=True, stop=True)
                    return inst
                K.op(P, mkU, r=[b_U, b_identb], w=[b_pU])
                K.op(A, lambda: nc.scalar.copy(out=ut[:, b, 0:288], in_=pU[:, 0:288]), r=[b_pU], w=[b_ut], acc=(b > 0))
                K.op(A, lambda: nc.scalar.copy(out=ut[:, b, 288:320], in_=pU[:, 0:32]), r=[b_pU], w=[b_ut], acc=True)
            chain(g, 0, 0, ut, b_ut)
            chain(g, 0, 1, ut, b_ut)
            if g + 1 < 32:
                tables(g + 1)
            chain(g, 1, 0, ut, b_ut)
            chain(g, 1, 1, ut, b_ut)
            if g >= 1:
                ymm_group(g - 1)
                if (g - 1) % 8 == 7:
                    gelu_out((g - 1) // 8)
        ymm_group(31)
        gelu_out(3)
        K.barrier()
    if debug == 5:
        return nc

    x1_s = dscr("x1_s", [NB, SEQ, D], F32)

    def load_w(dst, src_rows_view, ncols, b_dst):
        for c0 in range(0, ncols, 1024):
            c1 = min(ncols, c0 + 1024)
            K.dma(G, dst[:, :, c0:c1], src_rows_view[:, :, c0:c1], w=[b_dst], acc=True)

    with ExitStack() as st:
        wglu = st.enter_context(nc.sbuf_tensor("wglu", [128, 4, 512], BF16))
        wbra = st.enter_context(nc.sbuf_tensor("wbra", [128, 8, 1024], BF16))
        wbrs = st.enter_context(nc.sbuf_tensor("wbrs", [128, 4, 1024], BF16))
        wout = st.enter_context(nc.sbuf_tensor("wout", [128, 8, 1024], BF16))
        wing = st.enter_context(nc.sbuf_tensor("wing", [128, 8, 2048], BF16))
        bglu = st.enter_context(nc.sbuf_tensor("bglu_sb", [128, 4], F32))
        b_wglu, b_winga, b_wbra, b_wings, b_wbrs, b_wout = (Buf() for _ in range(6))
        winv = win_d.rearrange("(k p) n -> p k n", p=128)
        K.dma(S, bglu[:], bglu_d, w=[b_wglu], acc=True)
        load_w(wglu, wglu_d.rearrange("(k p) n -> p k n", p=128), 512, b_wglu)
        K.dma(G, wing[:, :, 0:1024], winv[:, :, 2048:3072], w=[b_winga], acc=True)
        load_w(wbra, wbra_d.rearrange("(k p) n -> p k n", p=128), 1024, b_wbra)
        K.dma(G, wing[:, :, 1024:2048], winv[:, :, 3072:4096], w=[b_wings], acc=True)
        load_w(wbrs, wbrs_d.rearrange("(k p) n -> p k n", p=128), 1024, b_wbrs)
        load_w(wout, wout_d.rearrange("(k p) n -> p k n", p=128), 1024, b_wout)
        ygb = [st.enter_context(nc.sbuf_tensor("ygb%d" % i, [128, 4, 512], BF16)) for i in range(2)]
        otb = [st.enter_context(nc.sbuf_tensor("otb%d" % i, [128, 8, 512], BF16)) for i in range(2)]
        hb = [st.enter_context(nc.sbuf_tensor("hb%d" % i, [128, 8, 512], BF16)) for i in range(2)]
        y2T = st.enter_context(nc.sbuf_tensor("y2T", [128, 4, 512], BF16))
        mT = st.enter_context(nc.sbuf_tensor("mT", [128, 8, 512], BF16))
        m1s = st.enter_context(nc.sbuf_tensor("m1s", [128, 8, 512], F32))
        sg = [st.enter_context(nc.sbuf_tensor("sg%d" % i, [128, 512], F32)) for i in range(2)]
        m2 = [st.enter_context(nc.sbuf_tensor("m2%d" % i, [128, 512], F32)) for i in range(2)]
        xr_ = [st.enter_context(nc.sbuf_tensor("xr%d" % i, [128, D], F32)) for i in range(4)]
        xo = [st.enter_context(nc.sbuf_tensor("xo%d" % i, [128, D], F32)) for i in range(2)]
        accA = [st.enter_context(nc.psum_tensor("accA%d" % i, [128, 512], F32)) for i in range(2)]
        accB = [st.enter_context(nc.psum_tensor("accB%d" % i, [128, 512], F32)) for i in range(2)]
        px1 = [st.enter_context(nc.psum_tensor("px1%d" % i, [128, 512], F32)) for i in range(2)]
        b_ygb, b_otb, b_hb, b_sg, b_m2, b_xo, b_accA, b_accB, b_px1 = ([Buf(), Buf()] for _ in range(9))
        b_xr = [Buf() for _ in range(4)]
        b_y2T, b_mT, b_m1s = Buf(), Buf(), Buf()
        tix = 0
        stp = 0

        def load_blk(blk):
            q = blk % 2
            c0 = blk * 512
            K.dma(A, ygb[q][:], ygT_s.rearrange("(k p) n -> p k n", p=128)[:, :, c0:c0 + 512], r=[db("ygT_s")], w=[b_ygb[q]])
            K.dma(A, otb[q][:], OT_s.rearrange("(k p) n -> p k n", p=128)[:, :, c0:c0 + 512], r=[db("OT_s")], w=[b_otb[q]])
            K.dma(A, hb[q][:], hT_s.rearrange("(k p) n -> p k n", p=128)[:, :, c0:c0 + 512], r=[db("hT_s")], w=[b_hb[q]])

        def load_xr(blk):
            b = blk // 4
            c0 = blk * 512
            for tt_ in range(4):
                cc = (c0 % SEQ) + tt_ * 128
                i_, cb_ = cc // 256, (cc % 256) // 128
                K.dma(A, xr_[tt_][:], x_d[b].rearrange("(c i) d -> i c d", i=8)[i_, cb_ * 128:(cb_ + 1) * 128, :], w=[b_xr[tt_]])

        def acc_mm(dst, wt, nk, c0_, rhs_t):
            def f():
                inst = None
                for k in range(nk):
                    inst = nc.tensor.matmul(dst[:, :], lhsT=wt[:, k, c0_:c0_ + 128], rhs=rhs_t[:, k, :], start=(k == 0), stop=(k == nk - 1))
                return inst
            return f
        load_blk(0)
        for blk in range(NL // 512):
            b = blk // 4
            q = blk % 2
            c0 = blk * 512
            if blk + 1 < NL // 512:
                load_blk(blk + 1)
            load_xr(blk)
            for nt in range(4):
                w2 = stp % 2
                stp += 1
                K.op(P, acc_mm(accA[w2], wglu, 4, nt * 128, ygb[q]), r=[b_wglu, b_ygb[q]], w=[b_accA[w2]])
                K.op(A, lambda: nc.scalar.activation(out=sg[w2][:], in_=accA[w2][:, :], func=AF.Sigmoid, bias=bglu[:, nt:nt + 1]),
                     r=[b_accA[w2], b_wglu], w=[b_sg[w2]])
                K.op(V, lambda: nc.vector.tensor_tensor(out=y2T[:, nt, :], in0=ygb[q][:, nt, :], in1=sg[w2][:], op=ALU.mult),
                     r=[b_ygb[q], b_sg[w2]], w=[b_y2T], acc=(nt > 0))
            for nt in range(8):
                w2 = stp % 2
                stp += 1
                K.op(P, acc_mm(accA[w2], wing, 8, nt * 128, hb[q]), r=[b_winga, b_hb[q]], w=[b_accA[w2]])
                K.op(P, acc_mm(accB[w2], wbra, 8, nt * 128, otb[q]), r=[b_wbra, b_otb[q]], w=[b_accB[w2]])
                K.op(A, lambda: nc.scalar.activation(out=sg[w2][:], in_=accA[w2][:, :], func=AF.Sigmoid), r=[b_accA[w2]], w=[b_sg[w2]])
                K.op(V, lambda: nc.vector.tensor_tensor(out=m1s[:, nt, :], in0=sg[w2][:], in1=accB[w2][:, :], op=ALU.mult),
                     r=[b_sg[w2], b_accB[w2]], w=[b_m1s], acc=(nt > 0))
            for nt in range(8):
                w2 = stp % 2
                stp += 1
                K.op(P, acc_mm(accA[w2], wing, 8, 1024 + nt * 128, hb[q]), r=[b_wings, b_hb[q]], w=[b_accA[w2]])
                K.op(P, acc_mm(accB[w2], wbrs, 4, nt * 128, y2T), r=[b_wbrs, b_y2T], w=[b_accB[w2]])
                K.op(A, lambda: nc.scalar.activation(out=sg[w2][:], in_=accA[w2][:, :], func=AF.Sigmoid), r=[b_accA[w2]], w=[b_sg[w2]])
                K.op(V, lambda: nc.vector.tensor_tensor(out=m2[w2][:], in0=sg[w2][:], in1=accB[w2][:, :], op=ALU.mult),
                     r=[b_sg[w2], b_accB[w2]], w=[b_m2[w2]])
                K.op(V, lambda: nc.vector.tensor_tensor(out=mT[:, nt, :], in0=m1s[:, nt, :], in1=m2[w2][:], op=ALU.add),
                     r=[b_m1s, b_m2[w2]], w=[b_mT], acc=(nt > 0))
            for tt_ in range(4):
                cc = (c0 % SEQ) + tt_ * 128
                i_, cb_ = cc // 256, (cc % 256) // 128
                w2 = tix % 2
                tix += 1
                rows = lambda dten: dten[b].rearrange("(c i) d -> i c d", i=8)[i_, cb_ * 128:(cb_ + 1) * 128, :]
                for hf in range(2):
                    def mmo():
                        inst = None
                        for k in range(8):
                            inst = nc.tensor.matmul(px1[hf][:, :], lhsT=mT[:, k, tt_ * 128:(tt_ + 1) * 128],
                                                    rhs=wout[:, k, hf * 512:(hf + 1) * 512], start=(k == 0), stop=(k == 7))
                        return inst
                    K.op(P, mmo, r=[b_wout, b_mT], w=[b_px1[hf]])
                    hs_ = slice(hf * 512, (hf + 1) * 512)
                    K.op(V, lambda: nc.vector.tensor_tensor(out=xo[w2][:, hs_], in0=px1[hf][:, :], in1=Gb[:, 0, b, hs_], op=ALU.mult),
                         r=[b_px1[hf], b_Gb], w=[b_xo[w2]], acc=(hf > 0))
                K.op(V, lambda: nc.vector.tensor_tensor(out=xo[w2][:], in0=xo[w2][:], in1=xr_[tt_][:], op=ALU.add),
                     r=[b_xo[w2], b_xr[tt_]], w=[b_xo[w2]])
                K.dma(S, rows(x1_s), xo[w2][:], r=[b_xo[w2]], w=[db("x1_s")], acc=True)
        K.barrier()
    if debug == 6:
        return nc

    h2T_s = dscr("h2T_s", [D, NL], BF16)
    part_s = dscr("part_s", [NL, D], F32)
    with ExitStack() as st:
        w1h = [st.enter_context(nc.sbuf_tensor("w1h%d" % i, [128, 8, 2048], BF16)) for i in range(2)]
        w2h0 = st.enter_context(nc.sbuf_tensor("w2h0", [128, 16, 1024], BF16))
        w2h = [w2h0, w2h0]
        b_w1 = [[Buf() for _ in range(4)] for _ in range(2)]
        b_w2_0 = Buf()
        b_w2 = [b_w2_0, b_w2_0]
        b1c = st.enter_context(nc.sbuf_tensor("b1c_sb", [128, 32], F32))
        b2r = st.enter_context(nc.sbuf_tensor("b2r_sb", [1, D], F32))
        ones1 = st.enter_context(nc.sbuf_tensor("ones1", [1, 128], F32))
        b_cst = Buf()
        K.dma(S, b1c[:], b1c_d, w=[b_cst], acc=True)
        K.dma(S, b2r[:], b2r_d, w=[b_cst], acc=True)
        K.op(V, lambda: nc.vector.memset(ones1[:], 1.0), w=[b_cst], acc=True)
        h2b = [st.enter_context(nc.sbuf_tensor("h2b%d" % i, [128, 8, 512], BF16)) for i in range(2)]
        hid = st.enter_context(nc.sbuf_tensor("hid", [128, 16, 512], BF16))
        rl = [st.enter_context(nc.sbuf_tensor("rl%d" % i, [128, 512], F32)) for i in range(2)]
        ph = [st.enter_context(nc.psum_tensor("ph%d" % i, [128, 512], F32)) for i in range(2)]
        po = [st.enter_context(nc.psum_tensor("po%d" % i, [128, D], F32)) for i in range(2)]
        b_h2b, b_rl, b_ph, b_po = ([Buf(), Buf()] for _ in range(4))
        b_hid = Buf()
        env = None
        pa_ = None
        xa = [st.enter_context(nc.sbuf_tensor("xa%d" % i, [128, D], F32)) for i in range(2)] * 2
        pb_ = [st.enter_context(nc.sbuf_tensor("pb%d" % i, [128, D], F32)) for i in range(2)] * 2
        b_xa = [Buf(), Buf()] * 2
        b_pa = [Buf() for _ in range(4)]
        w1v = w1_d.rearrange("(k p) n -> p k n", p=128)
        jn = 0
        jn_box = [0]
        tix = 0

        def load_x1(n_):
            b_ = (n_ * 128) // SEQ
            cc = (n_ * 128) % SEQ
            i_, cb_ = cc // 256, (cc % 256) // 128
            K.dma(A, env["xt"][n_ % 3][:], x1_s[b_].rearrange("(c i) d -> i c d", i=8)[i_, cb_ * 128:(cb_ + 1) * 128, :],
                  r=[db("x1_s")], w=[env["b_xt"][n_ % 3]])

        def load_h2(blk_):
            K.dma(A, h2b[blk_ % 2][:], h2T_s.rearrange("(k p) n -> p k n", p=128)[:, :, blk_ * 512:(blk_ + 1) * 512],
                  r=[db("h2T_s")], w=[b_h2b[blk_ % 2]])
        def load_mlp_w1(hf):
            for cblk in range(4):
                K.dma(G, w1h[hf][:, :, cblk * 512:(cblk + 1) * 512], w1v[:, :, hf * 2048 + cblk * 512:hf * 2048 + (cblk + 1) * 512],
                      w=[b_w1[hf][cblk]], acc=True)

        def load_mlp_w2(hf):
            w2v = w2_d[hf * 2048:(hf + 1) * 2048, :].rearrange("(k p) n -> p k n", p=128)
            for kh in range(2):
                K.dma(G, w2h[hf][:, kh * 8:(kh + 1) * 8, :], w2v[:, kh * 8:(kh + 1) * 8, :], w=[b_w2[hf]], acc=(kh > 0))
        load_mlp_w1(0)
        load_mlp_w2(0)
        for hf in range(2):
            hst = ExitStack()
            if hf == 0:
                env = mk_norm_env(hst)
                env["xn_on_dve"] = True
                pa_ = [hst.enter_context(nc.sbuf_tensor("pa0", [128, D], F32))] * 4
                b_pa = [Buf()] * 4
                load_x1(0)
                load_x1(1)
                load_mlp_w1(1)
            else:
                load_mlp_w2(1)
                pa_ = pb_
                b_pa = [Buf(), Buf()] * 2
            for blk in range(NL // 512):
                b = blk // 4
                q = blk % 2
                c0 = blk * 512
                tiles_ = []
                for tt_ in range(4):
                    cc = (c0 % SEQ) + tt_ * 128
                    tiles_.append((cc // 256, (cc % 256) // 128))
                rows = lambda dten, i_, cb_: dten[b].rearrange("(c i) d -> i c d", i=8)[i_, cb_ * 128:(cb_ + 1) * 128, :]
                def norm_tile_1(blk_, tt2):
                    n_ = blk_ * 4 + tt2
                    if n_ + 2 < NL // 128:
                        load_x1(n_ + 2)
                    norm_T1(env, n_)

                def norm_tile_2(blk_, tt2):
                    n_ = blk_ * 4 + tt2
                    norm_T2a(env, n_)

                def norm_tile_3(blk_, tt2):
                    n_ = blk_ * 4 + tt2
                    b_ = blk_ // 4
                    q_ = blk_ % 2
                    pT, b_pT = env["pT"], env["b_pT"]
                    for k in range(8):
                        K.op(V, lambda k=k: nc.vector.tensor_scalar(
                            out=h2b[q_][:, k, tt2 * 128:(tt2 + 1) * 128], in0=pT[:, k, :], scalar1=A2[:, k, b_:b_ + 1],
                            scalar2=modcol[:, k, 2, b_:b_ + 1], op0=ALU.mult, op1=ALU.add),
                            r=[b_pT, b_A, b_modcol], w=[b_h2b[q_]], acc=(tt2 > 0 or k > 0))
                    if tt2 == 3:
                        K.dma(S, h2T_s.rearrange("(k p) n -> p k n", p=128)[:, :, blk_ * 512:(blk_ + 1) * 512], h2b[q_][:],
                              r=[b_h2b[q_]], w=[db("h2T_s")], acc=True)

                def norm_blk(blk_):
                    for tt2 in range(4):
                        norm_tile_1(blk_, tt2)
                        norm_tile_2(blk_, tt2)
                        norm_tile_3(blk_, tt2)
                if hf == 0:
                    if blk == 0:
                        norm_blk(0)
                else:
                    if blk == 0:
                        load_h2(0)
                    if blk + 1 < NL // 512:
                        load_h2(blk + 1)
                    def load_tail(tt2):
                        i2, cb2 = tiles_[tt2]
                        K.dma(A, pa_[tt2][:], part_s[c0 + tt2 * 128:c0 + (tt2 + 1) * 128, :], r=[db("part_s")], w=[b_pa[tt2]])
                        K.dma(A, xa[tt2][:], rows(x1_s, i2, cb2), r=[db("x1_s")], w=[b_xa[tt2]])
                    load_tail(0)
                    load_tail(1)
                for nt in range(16):
                    w2 = nt % 2

                    def mmh():
                        inst = None
                        for k in range(8):
                            inst = nc.tensor.matmul(ph[w2][:, :], lhsT=w1h[hf][:, k, nt * 128:(nt + 1) * 128], rhs=h2b[q][:, k, :],
                                                    start=(k == 0), stop=(k == 7))
                        return inst
                    K.op(P, mmh, r=[b_w1[hf][nt // 4], b_h2b[q]], w=[b_ph[w2]])
                    K.op(A, lambda: nc.scalar.activation(out=rl[w2][:], in_=ph[w2][:, :], func=AF.Relu,
                                                         bias=b1c[:, hf * 16 + nt:hf * 16 + nt + 1]), r=[b_ph[w2], b_cst], w=[b_rl[w2]])
                    K.op(V, lambda: nc.vector.tensor_tensor(out=hid[:, nt, :], in0=rl[w2][:], in1=rl[w2][:], op=ALU.mult),
                         r=[b_rl[w2]], w=[b_hid], acc=(nt > 0))
                nxt_norm = hf == 0 and blk + 1 < NL // 512
                for tt_, (i_, cb_) in enumerate(tiles_):
                    w2 = tix % 2
                    tix += 1
                    prow = part_s[c0 + tt_ * 128:c0 + (tt_ + 1) * 128, :]
                    if nxt_norm:
                        norm_tile_1(blk + 1, tt_)

                    def mmo2():
                        inst = None
                        for h2_ in range(2):
                            for k in range(16):
                                inst = nc.tensor.matmul(po[w2][:, h2_ * 512:(h2_ + 1) * 512], lhsT=hid[:, k, tt_ * 128:(tt_ + 1) * 128],
                                                        rhs=w2h[hf][:, k, h2_ * 512:(h2_ + 1) * 512], start=(k == 0), stop=(k == 15 and hf == 0))
                            if hf == 1:
                                inst = nc.tensor.matmul(po[w2][:, h2_ * 512:(h2_ + 1) * 512], lhsT=ones1[0:1, :],
                                                        rhs=b2r[0:1, h2_ * 512:(h2_ + 1) * 512], start=False, stop=True)
                        return inst
                    K.op(P, mmo2, r=[b_w2[hf], b_hid, b_cst], w=[b_po[w2]])
                    if nxt_norm:
                        norm_tile_2(blk + 1, tt_)
                    if hf == 0:
                        K.op(A, lambda: nc.scalar.copy(out=pa_[tt_][:], in_=po[w2][:, :]), r=[b_po[w2]], w=[b_pa[tt_]])
                        K.dma(S, prow, pa_[tt_][:], r=[b_pa[tt_]], w=[db("part_s")], acc=True)
                        if nxt_norm:
                            norm_tile_3(blk + 1, tt_)
                    else:
                        K.op(V, lambda: nc.vector.tensor_tensor(out=pa_[tt_][:], in0=po[w2][:, :], in1=pa_[tt_][:], op=ALU.add),
                             r=[b_po[w2], b_pa[tt_]], w=[b_pa[tt_]])
                        K.op(V, lambda: nc.vector.tensor_tensor(out=pa_[tt_][:], in0=pa_[tt_][:], in1=Gb[:, 1, b, :], op=ALU.mult),
                             r=[b_pa[tt_], b_Gb], w=[b_pa[tt_]])
                        K.op(V, lambda: nc.vector.tensor_tensor(out=xa[tt_][:], in0=xa[tt_][:], in1=pa_[tt_][:], op=ALU.add),
                             r=[b_pa[tt_], b_xa[tt_]], w=[b_xa[tt_]])
                        K.dma(S, rows(out_d, i_, cb_), xa[tt_][:], r=[b_xa[tt_]], w=[db("out")], acc=True)
                        if tt_ + 2 < 4:
                            load_tail(tt_ + 2)
            if hf == 1:
                K.barrier()
            hst.close()

    K.barrier()
    return nc


def _rope_tables():
    half = 16
    inv_freq = (np.float32(10000.0) ** (-np.arange(half, dtype=np.float32) / np.float32(half))).astype(np.float32)
    C = np.zeros((128, 16, 64), np.float32)
    Sg = np.zeros((128, 16, 64), np.float32)
    p = np.arange(128)
    for i in range(8):
        for cb in range(2):
            t = 8 * (128 * cb + p) + i
            row = (t // 64).astype(np.float32)
            col = (t % 64).astype(np.float32)
            ar = row[:, None] * inv_freq[None, :]
            ac = col[:, None] * inv_freq[None, :]
            cr, sr = np.cos(ar).astype(np.float32), np.sin(ar).astype(np.float32)
            cc, sc = np.cos(ac).astype(np.float32), np.sin(ac).astype(np.float32)
            j = i * 2 + cb
            C[:, j, 0:16] = cr
            C[:, j, 16:32] = cr
            C[:, j, 32:48] = cc
            C[:, j, 48:64] = cc
            Sg[:, j, 0:16] = -sr
            Sg[:, j, 16:32] = sr
            Sg[:, j, 32:48] = -sc
            Sg[:, j, 48:64] = sc
    return C, Sg


_PARTNER = np.concatenate([np.arange(16, 32), np.arange(0, 16), np.arange(48, 64), np.arange(32, 48)])


def make_in_maps(inp):
    f = lambda a: np.ascontiguousarray(np.asarray(a, dtype=np.float32))
    x = f(inp["x"]); c = f(inp["c"]); ctx = f(inp["ctx"]); c_ctx = f(inp["c_ctx"])
    ropeC, ropeS = _rope_tables()
    col = lambda v: np.ascontiguousarray(v.reshape(8, 128).T)
    qg = f(inp["q_norm_g"])[0]
    kg = f(inp["k_norm_g"])[0]
    qg2 = np.ascontiguousarray(np.broadcast_to(np.stack([qg, qg[_PARTNER]])[None], (128, 2, 64)))
    kg2 = np.ascontiguousarray(np.broadcast_to(np.stack([kg, kg[_PARTNER]])[None], (128, 2, 64)))
    dup = lambda a: np.ascontiguousarray(np.concatenate([a, a], 0))
    tl = lambda a: dup(f(a)[0].transpose(2, 1, 0).reshape(64, 64))
    lamr, lami = tl(inp["ssm_lambda_re"]), tl(inp["ssm_lambda_im"])
    ldt = np.ascontiguousarray(np.broadcast_to(f(inp["ssm_log_dt"])[0].T.reshape(1, 64), (128, 64)))
    bt = lambda a: dup(f(a)[0].transpose(2, 1, 0, 3).reshape(64, 64, 16))
    ct = lambda a: dup(f(a)[0].transpose(3, 1, 0, 2).reshape(64, 64, 16))
    dvec = f(inp["ssm_d"])[0].reshape(32, 16)
    dcol = np.ascontiguousarray(np.broadcast_to(dvec.T[None], (8, 16, 32)).reshape(128, 32))
    ii = np.arange(128) // 16
    maskf = (ii[None, :] >= ii[:, None]).astype(np.float32)
    maskb = (ii[None, :] <= ii[:, None]).astype(np.float32)
    ar = np.arange(288, dtype=np.float32)
    iota2 = np.ascontiguousarray(np.broadcast_to(np.stack([ar, 287 - ar])[None], (128, 2, 288)))
    shared = {
        "lamr": lamr, "lami": lami, "ldt": ldt, "bre": bt(inp["ssm_b_re"]), "bim": bt(inp["ssm_b_im"]),
        "cre": ct(inp["ssm_c_re"]), "cim": ct(inp["ssm_c_im"]), "dcol": dcol, "maskf": maskf, "maskb": maskb,
        "kv": np.ascontiguousarray(np.broadcast_to(np.arange(9, dtype=np.float32)[None], (128, 9))),
        "w_glu": f(inp["w_glu"])[0], "bglu": np.ascontiguousarray(f(inp["b_glu"])[0].reshape(4, 128).T),
        "w_br_attn": f(inp["w_br_attn"])[0], "w_br_ssm": f(inp["w_br_ssm"])[0], "w_out": f(inp["w_out"])[0],
        "w_mlp1": f(inp["w_mlp1"])[0], "b1c": np.ascontiguousarray(f(inp["b_mlp1"])[0].reshape(32, 128).T),
        "w_mlp2": f(inp["w_mlp2"])[0], "b2r": f(inp["b_mlp2"])[0][None, :],
        "iota2": iota2, "j128": np.ascontiguousarray(np.eye(128, dtype=np.float32)[::-1]),
        "w_mod": f(inp["w_mod"])[0], "b_mod": f(inp["b_mod"])[0][None, :],
        "n1g": col(f(inp["norm1_g"])[0]), "n2g": col(f(inp["norm2_g"])[0]),
        "w_in": f(inp["w_in"])[0], "qg": qg2, "kg": kg2,
        "ropeC": ropeC, "ropeS": ropeS, "ident": np.eye(128, dtype=np.float32),
        "esel": np.concatenate([np.zeros((64, 128), np.float32), np.ones((1, 128), np.float32), np.zeros((63, 128), np.float32)], 0),
        "sel": np.ascontiguousarray(np.broadcast_to(np.eye(3, dtype=np.float32)[:, :NB, None], (3, NB, 128))),
    }
    maps = []
    for core in range(NCORES):
        b0 = core * NB
        cv = np.stack([c[b0], c[b0 + 1], c_ctx])
        cT = np.ascontiguousarray(cv.reshape(3, 8, 128).transpose(2, 1, 0))
        m = dict(shared)
        m["x"] = x[b0:b0 + NB]
        m["ctx"] = ctx[b0:b0 + NB]
        m["cT"] = cT
        maps.append(m)
    return maps


def kernel(**inputs):
    nc = build_nc(0)
    maps = make_in_maps(inputs)
    res = run_bass_kernel_spmd(nc, maps, core_ids=list(range(NCORES)))
    return np.concatenate([r["out"] for r in res.results], axis=0)
```
